# Optimizing a Trainium2 kernel written in Bass

```python
import jax, jax.numpy as jnp
from jax import lax
import numpy as np

D_MODEL = 1024
BATCH = 4
SEQ = 8192
DEPTH = 2

HEAD_DIM = 64
D_MIX = D_MODEL
N_HEADS_A = 4
N_HEADS_B = 4
N_HEADS_C = D_MIX // HEAD_DIM - N_HEADS_A - N_HEADS_B
ROPE_DIM = HEAD_DIM // 4
ROPE_THETA = 500000.0
MOBA_BLOCK = 256
MOBA_TOPK = 3
DSA_TOPK = 256
IDX_HEADS = 8
IDX_DIM = 64
Q_BLOCK = 128
D_FF = 2816
N_MOD = 9
RMS_EPS = 1e-6
NEG = -1e30

kernel_name = "hymba_style_moba_dsa_stickbreaking_macaron"

SPLIT_SIZES = (N_HEADS_A * HEAD_DIM, N_HEADS_A * HEAD_DIM, N_HEADS_A * HEAD_DIM,
               N_HEADS_B * HEAD_DIM, N_HEADS_B * HEAD_DIM, N_HEADS_B * HEAD_DIM,
               IDX_HEADS * IDX_DIM, IDX_DIM, IDX_HEADS,
               N_HEADS_C * HEAD_DIM, N_HEADS_C * HEAD_DIM, N_HEADS_C * HEAD_DIM)
D_IN = sum(SPLIT_SIZES)


def _rms(x):
    xf = x.astype(jnp.float32)
    return (xf * lax.rsqrt(jnp.mean(xf * xf, axis=-1, keepdims=True) + RMS_EPS))


def rmsnorm(x, g):
    return (_rms(x) * g.astype(jnp.float32)).astype(x.dtype)


def modulate(h, shift, scale):
    return h * (1 + scale) + shift


def swiglu(h, w1, w3, w2):
    return (jax.nn.silu(h @ w1) * (h @ w3)) @ w2


def rope_tables(seq):
    pos = jnp.arange(seq, dtype=jnp.float32)
    inv = ROPE_THETA ** (-jnp.arange(0, ROPE_DIM, 2, dtype=jnp.float32) / ROPE_DIM)
    ang = pos[:, None] * inv[None, :]
    return jnp.cos(ang), jnp.sin(ang)


def apply_partial_rope(x, cos, sin):
    half = ROPE_DIM // 2
    xf = x.astype(jnp.float32)
    x1, x2 = xf[..., :half], xf[..., half:ROPE_DIM]
    c, s = cos[None, :, None, :], sin[None, :, None, :]
    out = jnp.concatenate([x1 * c - x2 * s, x2 * c + x1 * s, xf[..., ROPE_DIM:]], axis=-1)
    return out.astype(x.dtype)


def _chunk(a, n):
    return jnp.moveaxis(a.reshape(a.shape[0], n, Q_BLOCK, *a.shape[2:]), 1, 0)


def _unchunk(a):
    a = jnp.moveaxis(a, 0, 1)
    return a.reshape(a.shape[0], a.shape[1] * a.shape[2], *a.shape[3:])


def moba_attention(q, k, v):
    B, S, H, d = q.shape
    nb = -(-S // MOBA_BLOCK)
    pad = nb * MOBA_BLOCK - S
    kp = jnp.pad(k, ((0, 0), (0, pad), (0, 0), (0, 0)))
    vp = jnp.pad(v, ((0, 0), (0, pad), (0, 0), (0, 0)))
    kb = kp.reshape(B, nb, MOBA_BLOCK, H, d)
    vb = vp.reshape(B, nb, MOBA_BLOCK, H, d)
    kmean = jnp.mean(kb.astype(jnp.float32), axis=2)
    kbt = jnp.transpose(kb, (0, 3, 1, 2, 4))
    vbt = jnp.transpose(vb, (0, 3, 1, 2, 4))
    n_sel = min(MOBA_TOPK, max(nb - 1, 1))
    scale = HEAD_DIM ** -0.5
    nq = S // Q_BLOCK
    gather_blocks = jax.vmap(jax.vmap(lambda blk, ii: blk[ii]))

    def one_block(args):
        qi, ci = args
        t = ci * Q_BLOCK + jnp.arange(Q_BLOCK)
        cur = (ci * Q_BLOCK) // MOBA_BLOCK
        gate = jnp.einsum('bqhd,bnhd->bqhn', qi.astype(jnp.float32), kmean)
        gate = jnp.where(jnp.arange(nb) < cur, gate, NEG)
        _, idx = lax.top_k(gate, n_sel)
        valid = jnp.transpose(idx < cur, (0, 2, 1, 3))
        idx_t = jnp.transpose(idx, (0, 2, 1, 3))
        ksel = gather_blocks(kbt, idx_t)
        vsel = gather_blocks(vbt, idx_t)
        qh = jnp.transpose(qi, (0, 2, 1, 3))
        s_past = jnp.einsum('bhqd,bhqnkd->bhqnk', qh, ksel).astype(jnp.float32) * scale
        s_past = jnp.where(valid[..., None], s_past, NEG)
        kown = lax.dynamic_slice_in_dim(kp, cur * MOBA_BLOCK, MOBA_BLOCK, axis=1)
        vown = lax.dynamic_slice_in_dim(vp, cur * MOBA_BLOCK, MOBA_BLOCK, axis=1)
        s_own = jnp.einsum('bhqd,bkhd->bhqk', qh, kown).astype(jnp.float32) * scale
        kpos = cur * MOBA_BLOCK + jnp.arange(MOBA_BLOCK)
        s_own = jnp.where(kpos[None, :] <= t[:, None], s_own, NEG)
        n_past = n_sel * MOBA_BLOCK
        s = jnp.concatenate([s_past.reshape(B, H, Q_BLOCK, n_past), s_own], axis=-1)
        p = jax.nn.softmax(s, axis=-1).astype(v.dtype)
        p_past = p[..., :n_past].reshape(B, H, Q_BLOCK, n_sel, MOBA_BLOCK)
        p_own = p[..., n_past:]
        return (jnp.einsum('bhqnk,bhqnkd->bqhd', p_past, vsel)
                + jnp.einsum('bhqk,bkhd->bqhd', p_own, vown))

    out = lax.map(one_block, (_chunk(q, nq), jnp.arange(nq)))
    return _unchunk(out)


def dsa_attention(q, k, v, q_idx, k_idx, w_idx):
    B, S, H, d = q.shape
    topk = min(DSA_TOPK, S // 4)
    scale = HEAD_DIM ** -0.5
    nq = S // Q_BLOCK
    kf = k.reshape(B, S, H * d)
    vf = v.reshape(B, S, H * d)
    kix = k_idx.astype(jnp.float32)
    gather_rows = jax.vmap(lambda rows, ii: rows[ii])

    def one_block(args):
        qb, qib, wib, ci = args
        t = ci * Q_BLOCK + jnp.arange(Q_BLOCK)
        logits = jnp.einsum('bqjd,bsd->bqjs', qib.astype(jnp.float32), kix)
        score = jnp.einsum('bqj,bqjs->bqs', wib.astype(jnp.float32), jax.nn.relu(logits))
        score = jnp.where(jnp.arange(S)[None, :] <= t[:, None], score, NEG)
        _, sel = lax.top_k(score, topk)
        valid = sel <= t[None, :, None]
        ksel = gather_rows(kf, sel).reshape(B, Q_BLOCK, topk, H, d)
        vsel = gather_rows(vf, sel).reshape(B, Q_BLOCK, topk, H, d)
        s = jnp.einsum('bqhd,bqkhd->bhqk', qb, ksel).astype(jnp.float32) * scale
        s = jnp.where(valid[:, None, :, :], s, NEG)
        p = jax.nn.softmax(s, axis=-1).astype(v.dtype)
        return jnp.einsum('bhqk,bqkhd->bqhd', p, vsel)

    out = lax.map(one_block, (_chunk(q, nq), _chunk(q_idx, nq), _chunk(w_idx, nq), jnp.arange(nq)))
    return _unchunk(out)


def stick_breaking_attention(q, k, v):
    B, S, H, d = q.shape
    scale = HEAD_DIM ** -0.5
    nq = S // Q_BLOCK
    kpos = jnp.arange(S)

    def one_block(args):
        qb, ci = args
        t = ci * Q_BLOCK + jnp.arange(Q_BLOCK)
        z = jnp.einsum('bqhd,bshd->bhqs', qb, k).astype(jnp.float32) * scale
        strict = kpos[None, :] < t[:, None]
        log_beta = jax.nn.log_sigmoid(z)
        log_1m = jnp.where(strict, jax.nn.log_sigmoid(-z), 0.0)
        after = lax.cumsum(log_1m, axis=3, reverse=True) - log_1m
        a = jnp.where(strict, jnp.exp(log_beta + after), 0.0).astype(v.dtype)
        return jnp.einsum('bhqs,bshd->bqhd', a, v)

    out = lax.map(one_block, (_chunk(q, nq), jnp.arange(nq)))
    return _unchunk(out)


def setup_inputs(seed: int = 0) -> dict:
    key = jax.random.key(seed)
    ks = jax.random.split(key, 12)
    f32 = jnp.float32
    x = jax.random.normal(ks[0], (BATCH, SEQ, D_MODEL), f32)
    c = jax.random.normal(ks[1], (BATCH, D_MODEL), f32)
    w_ada = jax.random.normal(ks[2], (DEPTH, D_MODEL, N_MOD * D_MODEL), f32) * D_MODEL ** -0.5
    b_ada = 0.01 * jax.random.normal(ks[3], (DEPTH, N_MOD * D_MODEL), f32)
    norm_g = 1.0 + 0.02 * jax.random.normal(ks[4], (DEPTH, 3, D_MODEL), f32)
    w_in = jax.random.normal(ks[5], (DEPTH, D_MODEL, D_IN), f32) * D_MODEL ** -0.5
    qk_g = 1.0 + 0.02 * jax.random.normal(ks[6], (DEPTH, 4, HEAD_DIM), f32)
    out_g = 1.0 + 0.02 * jax.random.normal(ks[7], (DEPTH, D_MIX), f32)
    w_out = jax.random.normal(ks[8], (DEPTH, D_MIX, D_MODEL), f32) * D_MIX ** -0.5
    ffn_w1 = jax.random.normal(ks[9], (DEPTH, 2, D_MODEL, D_FF), f32) * D_MODEL ** -0.5
    ffn_w3 = jax.random.normal(ks[10], (DEPTH, 2, D_MODEL, D_FF), f32) * D_MODEL ** -0.5
    ffn_w2 = jax.random.normal(ks[11], (DEPTH, 2, D_FF, D_MODEL), f32) * D_FF ** -0.5
    return {"x": x, "c": c, "w_ada": w_ada, "b_ada": b_ada, "norm_g": norm_g, "w_in": w_in,
            "qk_g": qk_g, "out_g": out_g, "w_out": w_out, "ffn_w1": ffn_w1,
            "ffn_w3": ffn_w3, "ffn_w2": ffn_w2}


def reference(x, c, w_ada, b_ada, norm_g, w_in, qk_g, out_g, w_out, ffn_w1, ffn_w3, ffn_w2):
    B, S, _ = x.shape
    cos, sin = rope_tables(S)
    split_points = [int(p) for p in np.cumsum(SPLIT_SIZES)[:-1]]
    silu_c = jax.nn.silu(c)
    for layer in range(DEPTH):
        mod = silu_c @ w_ada[layer] + b_ada[layer]
        sh1, sc1, g1, sh2, sc2, g2, sh3, sc3, g3 = [m[:, None, :] for m in jnp.split(mod, N_MOD, axis=-1)]

        h = modulate(rmsnorm(x, norm_g[layer, 0]), sh1, sc1)
        x = x + 0.5 * g1 * swiglu(h, ffn_w1[layer, 0], ffn_w3[layer, 0], ffn_w2[layer, 0])

        h = modulate(rmsnorm(x, norm_g[layer, 1]), sh2, sc2)
        proj = h @ w_in[layer]
        (qa, ka, va, qb, kb, vb, qi, ki, wi, qc, kc, vc) = jnp.split(proj, split_points, axis=-1)
        heads = lambda t, n: t.reshape(B, S, n, HEAD_DIM)
        qa = apply_partial_rope(rmsnorm(heads(qa, N_HEADS_A), qk_g[layer, 0]), cos, sin)
        ka = apply_partial_rope(rmsnorm(heads(ka, N_HEADS_A), qk_g[layer, 1]), cos, sin)
        qb = apply_partial_rope(rmsnorm(heads(qb, N_HEADS_B), qk_g[layer, 2]), cos, sin)
        kb = apply_partial_rope(rmsnorm(heads(kb, N_HEADS_B), qk_g[layer, 3]), cos, sin)
        qi = apply_partial_rope(qi.reshape(B, S, IDX_HEADS, IDX_DIM), cos, sin)
        ki = apply_partial_rope(ki[:, :, None, :], cos, sin)[:, :, 0, :]

        oa = moba_attention(qa, ka, heads(va, N_HEADS_A))
        ob = dsa_attention(qb, kb, heads(vb, N_HEADS_B), qi, ki, wi)
        oc = stick_breaking_attention(heads(qc, N_HEADS_C), heads(kc, N_HEADS_C), heads(vc, N_HEADS_C))

        o = jnp.concatenate([oa, ob, oc], axis=2)
        o = (_rms(o).reshape(B, S, D_MIX) * out_g[layer].astype(jnp.float32)).astype(x.dtype)
        x = x + g2 * (o @ w_out[layer])

        h = modulate(rmsnorm(x, norm_g[layer, 2]), sh3, sc3)
        x = x + 0.5 * g3 * swiglu(h, ffn_w1[layer, 1], ffn_w3[layer, 1], ffn_w2[layer, 1])
    return x
```

```python
import numpy as np
import ml_dtypes
from contextlib import ExitStack
import concourse.bass as bass
import concourse.mybir as mybir
from concourse.bass_utils import run_bass_kernel_spmd


F32 = mybir.dt.float32
BF16 = mybir.dt.bfloat16
AF = mybir.ActivationFunctionType
ALU = mybir.AluOpType
AX = mybir.AxisListType


def C(name, *a, **kw):
    return lambda e: getattr(e, name)(*a, **kw)


class Sched:
    COMPUTE = ("pe", "act", "dve", "pool")

    def __init__(self, nc, es, n_dma_sems=24):
        self.nc = nc
        self.es = es
        self.engs = {"pe": nc.tensor, "act": nc.scalar, "dve": nc.vector, "pool": nc.gpsimd, "sp": nc.sync}
        self.sem = {e: es.enter_context(nc.semaphore("s_" + e)) for e in self.COMPUTE}
        self.cnt = {e: 0 for e in self.COMPUTE}
        self.known = {e: {} for e in self.engs}
        self.dsems = {}
        self.regW = {}
        self.regR = {}
        self.ops = {e: [] for e in self.engs}
        self.nops = 0

    def _dsem(self, key):
        if key not in self.dsems:
            self.dsems[key] = [self.es.enter_context(self.nc.semaphore("d_%d" % len(self.dsems))), 0]
        return self.dsems[key]

    def _need(self, eng, deps, same_engine_ok):
        waits = []
        kn = self.known[eng]
        for sid, (sem, val) in deps.items():
            if same_engine_ok and sid == eng:
                continue
            if kn.get(sid, 0) >= val:
                continue
            kn[sid] = val
            waits.append((sem, val))
        return waits

    def op(self, eng, fn, reads=(), writes=(), dsem=None, nosame=None):
        if nosame is None:
            nosame = (eng == "pe")
        deps = {}
        for r in reads:
            for sid, sv in self.regW.get(r, {}).items():
                if deps.get(sid, (None, 0))[1] < sv[1]:
                    deps[sid] = sv
        for w in writes:
            for d in (self.regW.get(w, {}), self.regR.get(w, {})):
                for sid, sv in d.items():
                    if deps.get(sid, (None, 0))[1] < sv[1]:
                        deps[sid] = sv
        waits = self._need(eng, deps, nosame)
        if dsem is None:
            self.cnt[eng] += 1
            sid, sem, val, inc = eng, self.sem[eng], self.cnt[eng], 1
        else:
            ds = self._dsem(dsem)
            ds[1] += 16
            sid, sem, val, inc = ("d", dsem), ds[0], ds[1], 16
        for r in reads:
            self.regR.setdefault(r, {})[sid] = (sem, val)
        for w in writes:
            self.regW.setdefault(w, {})[sid] = (sem, val)
        self.ops[eng].append((waits, fn, sem, inc))
        self.nops += 1
        return (sid, sem, val)

    def barrier(self, roll=30000):
        deps = {}
        for e in self.COMPUTE:
            if self.cnt[e] > 0:
                deps[e] = (self.sem[e], self.cnt[e])
        for k, (sem, c) in self.dsems.items():
            if c > 0:
                deps[("d", k)] = (sem, c)
        for e in self.engs:
            waits = self._need(e, dict(deps), False)
            self.ops[e].append((waits, None, None, 0))
        self.regW.clear(); self.regR.clear()
        for e in self.COMPUTE:
            if self.cnt[e] > roll:
                self.nroll = getattr(self, "nroll", 0) + 1
                self.sem[e] = self.es.enter_context(self.nc.semaphore("s_%s_%d" % (e, self.nroll)))
                self.cnt[e] = 0
                for e2 in self.engs:
                    self.known[e2].pop(e, None)
        for k in list(self.dsems.keys()):
            if self.dsems[k][1] > roll:
                self.nroll = getattr(self, "nroll", 0) + 1
                self.dsems[k] = [self.es.enter_context(self.nc.semaphore("d_r%d" % self.nroll)), 0]
                for e2 in self.engs:
                    self.known[e2].pop(("d", k), None)

    def finish_wait(self, eng, regions):
        deps = {}
        for r in regions:
            for sid, sv in self.regW.get(r, {}).items():
                if deps.get(sid, (None, 0))[1] < sv[1]:
                    deps[sid] = sv
        waits = self._need(eng, deps, False)
        self.ops[eng].append((waits, None, None, 0))

    def emit(self):
        nc = self.nc
        with nc.Block() as block:
            def mk(ename):
                lst = self.ops[ename]

                def body(e):
                    for waits, fn, sem, inc in lst:
                        for (ws, wv) in waits:
                            e.wait_ge(ws, wv)
                        if fn is not None:
                            ins = fn(e)
                            ins.then_inc(sem, inc)
                return body
            block.tensor(mk("pe"))
            block.scalar(mk("act"))
            block.vector(mk("dve"))
            block.gpsimd(mk("pool"))
            block.sync(mk("sp"))


D = 1024
DFF = 2816
NFC = DFF // 128
DIN = 3656
NPB = 3648
EPS = 1e-6
NIT = 20
BIGA = 240000.0
TOPK = 256
NFB = 21


def _barrier(S):
    S.barrier()


def build_fused(NCH, DEPTH=2, R=1):
    NT = NCH * 512
    SK = NCH * R * 512
    NKT = SK // 128
    NB = SK // 256
    nc = bass.Bass("TRN2", target_bir_lowering=False)
    din = lambda name, shape, dt=F32: nc.dram_tensor(name, list(shape), dt, kind="ExternalInput").ap()
    dout = lambda name, shape, dt=F32: nc.dram_tensor(name, list(shape), dt, kind="ExternalOutput").ap()
    dscr = lambda name, shape, dt=F32: nc.dram_tensor(name, list(shape), dt, kind="Internal").ap()
    uid = [0]

    def nm(name):
        uid[0] += 1
        return "s%d_%s" % (uid[0], name)

    x_in = din("x", [NT, D]); cT = din("cT", [128, 8]); identd = din("ident", [128, 128]); uincd = din("uinc", [128, 128])
    cs = din("cs", [NT, 16])
    L = []
    for l in range(DEPTH):
        p = {}
        p["wada"] = din("wada%d" % l, [D, 9 * D]); p["badar"] = din("badar%d" % l, [1, 9 * D]); p["badac"] = din("badac%d" % l, [128, 72])
        p["ng"] = din("ng%d" % l, [128, 24]); p["w_in"] = din("w_in%d" % l, [D, DIN]); p["qkg"] = din("qkg%d" % l, [1, 1024])
        p["w_out"] = din("w_out%d" % l, [D, D]); p["outg"] = din("outg%d" % l, [1, D])
        for i in range(2):
            p["w1_%d" % i] = din("w1_%d_%d" % (l, i), [D, DFF]); p["w3_%d" % i] = din("w3_%d_%d" % (l, i), [D, DFF]); p["w2_%d" % i] = din("w2_%d_%d" % (l, i), [DFF, D])
        L.append(p)
    mLE = din("mLE", [128, 4 * R, 512], BF16); mLT = din("mLT", [128, 4 * R, 512], BF16)
    cbd = din("cb", [128, 4, R * 512]); ohd = din("oh", [32, SK], BF16)
    pbd = din("pastbias", [128, NCH, 4, 32]); pvd = din("pastvalid", [128, NCH, 4, 32]); fud = din("future", [128, NCH, 4, 32])
    pow2d = din("pow2", [128, NIT])
    y_out = dout("y", [NT, D])
    xA = dscr("xA", [NT, D]); xB = dscr("xB", [NT, D]); xC = dscr("xC", [NT, D])
    FT = dscr("FT", [NFB * 128, SK], BF16); Vscr = dscr("Vscr", [NKT, 128, 16 * 65], BF16); wiS = dscr("wiS", [NT, 8])
    OT = dscr("OT", [16, 65, NT]); nsT = dscr("nsT", [NCH, NKT, 128, 512], BF16)

    with ExitStack() as es:
        S = Sched(nc, es)
        ps = es.enter_context(nc.psum_tensor("ps", [128, 8, 512], F32))
        sbg = lambda name, shape, dt=F32: es.enter_context(nc.sbuf_tensor(nm(name), list(shape), dt))
        ident32 = sbg("ident32", [128, 128]); ident = sbg("identb", [128, 128], BF16); id32b = sbg("id32b", [128, 128])
        nbigI = sbg("nbigI", [128, 128], BF16); uinc = sbg("uinc", [128, 128], BF16); ones = sbg("ones", [128, 128], BF16)
        sc = sbg("sc", [128, 8]); scb = sbg("scb", [128, 8, 128]); epsb = sbg("epsb", [128, 1])
        S.op("sp", C("dma_start", out=ident32[:], in_=identd[:]), writes=["ident32"], dsem="c_id")
        S.op("sp", C("dma_start", out=id32b[:], in_=uincd[:]), writes=["id32b"], dsem="c_id2")
        S.op("sp", C("dma_start", out=sc[:], in_=cT[:]), writes=["sc"], dsem="c_sc")
        S.op("dve", C("tensor_copy", out=ident[:], in_=ident32[:]), reads=["ident32"], writes=["ident"])
        S.op("dve", C("tensor_scalar", out=nbigI[:], in0=ident32[:], scalar1=-BIGA, scalar2=None, op0=ALU.mult), reads=["ident32"], writes=["nbigI"])
        S.op("dve", C("tensor_copy", out=uinc[:], in_=id32b[:]), reads=["id32b"], writes=["uinc"])
        S.op("dve", C("memset", ones[:], 1.0), writes=["ones"])
        S.op("dve", C("memset", epsb[:], EPS), writes=["epsb"])
        S.op("act", C("activation", out=sc[:], in_=sc[:], func=AF.Silu), reads=["sc"], writes=["sc"])
        S.op("dve", C("tensor_copy", out=scb[:], in_=sc[:].unsqueeze(2).to_broadcast([128, 8, 128])), reads=["sc"], writes=["scb"])

        def mod_phase(p, cols, rows, scope):
            out = {}
            sbs = lambda name, shape, dt=F32: scope.enter_context(nc.sbuf_tensor(nm(name), list(shape), dt))
            for (m, name) in cols:
                out[name] = sbs(name, [128, 8])
            for (m, name, scl) in rows:
                out[name] = sbs(name, [128, 1024])
            ngt = sbs("ng", [128, 24])
            with ExitStack() as ph:
                wa = ph.enter_context(nc.sbuf_tensor(nm("wa"), [128, 8, 1024], F32))
                bc = ph.enter_context(nc.sbuf_tensor(nm("bc"), [128, 72], F32))
                br = ph.enter_context(nc.sbuf_tensor(nm("br"), [128, 1024], F32))
                S.op("sp", C("dma_start", out=bc[:], in_=p["badac"][:]), writes=["bc"], dsem="bc")
                for (m, name) in cols:
                    t = out[name]
                    S.op("sp", C("dma_start", out=wa[:], in_=p["wada"][:, m * 1024:(m + 1) * 1024].rearrange("(k p) n -> p k n", p=128)), writes=["wa"], dsem="wa")
                    for kk in range(8):
                        for k in range(8):
                            S.op("pe", C("matmul", ps[:, 0, kk:kk + 1], lhsT=wa[:, k, kk * 128:(kk + 1) * 128], rhs=sc[:, k:k + 1], start=(k == 0), stop=(k == 7)),
                                 reads=["wa", "sc"], writes=["b0"])
                    S.op("dve", C("tensor_tensor", out=t[:], in0=ps[:, 0, 0:8], in1=bc[:, m * 8:(m + 1) * 8], op=ALU.add), reads=["b0", "bc"], writes=["modv"])
                for (m, name, scl) in rows:
                    t = out[name]
                    S.op("sp", C("dma_start", out=wa[:], in_=p["wada"][:, m * 1024:(m + 1) * 1024].rearrange("(k p) n -> p k n", p=128)), writes=["wa"], dsem="wa")
                    S.op("sp", C("dma_start", out=br[:], in_=p["badar"][:, m * 1024:(m + 1) * 1024].partition_broadcast(128)), writes=["br"], dsem="br")
                    for hf in range(2):
                        for k in range(8):
                            S.op("pe", C("matmul", ps[:, 1 + hf, :], lhsT=scb[:, k, :], rhs=wa[:, k, hf * 512:(hf + 1) * 512], start=(k == 0), stop=(k == 7)),
                                 reads=["wa", "scb"], writes=["b%d" % (1 + hf)])
                    S.op("dve", C("tensor_tensor", out=t[:], in0=ps[:, 1:3, :].rearrange("p a b -> p (a b)"), in1=br[:], op=ALU.add), reads=["b1", "b2", "br"], writes=["modv"])
                    if scl != 1.0:
                        S.op("dve", C("tensor_scalar", out=t[:], in0=t[:], scalar1=scl, scalar2=None, op0=ALU.mult), reads=["modv"], writes=["modv"])
                S.op("sp", C("dma_start", out=ngt[:], in_=p["ng"][:]), writes=["modv"], dsem="c_ng")
                out["ng"] = ngt
                _barrier(S)
            return out

        def make_A(scope, sct, ngt, which):
            A = scope.enter_context(nc.sbuf_tensor(nm("A"), [128, 8], F32))
            S.op("dve", C("scalar_tensor_tensor", out=A[:], in0=sct[:], scalar=1.0, in1=ngt[:, which * 8:(which + 1) * 8], op0=ALU.add, op1=ALU.mult), reads=["modv"], writes=["modv"])
            return A

        def norm_T(tiles, xs, A, B, hT):
            ss, rs, xn = tiles
            for t in range(2):
                S.op("act", C("activation", out=xn[:, t, :], in_=xs[:, t, :], func=AF.Square, accum_out=ss[:, t:t + 1]), reads=["xs"], writes=["xn", "ss"])
            S.op("act", C("activation", out=rs[:], in_=ss[:], func=AF.Ln, scale=1.0 / D, bias=epsb[:]), reads=["ss", "epsb"], writes=["rs"])
            S.op("act", C("activation", out=rs[:], in_=rs[:], func=AF.Exp, scale=-0.5), reads=["rs"], writes=["rs"])
            for t in range(2):
                S.op("dve", C("tensor_scalar", out=xn[:, t, :], in0=xs[:, t, :], scalar1=rs[:, t:t + 1], scalar2=None, op0=ALU.mult), reads=["xs", "rs"], writes=["xn"])
            psT = ps[:, 6:8, :].rearrange("p a b -> p (a b)").bitcast(BF16)
            for k in range(8):
                for t in range(2):
                    S.op("pe", C("transpose", out=psT[:, k * 256 + t * 128:k * 256 + (t + 1) * 128], in_=xn[:, t, k * 128:(k + 1) * 128], identity=ident[:]),
                         reads=["xn", "ident"], writes=["b6", "b7"])
            for k in range(8):
                if k % 2 == 0:
                    S.op("dve", C("tensor_scalar", out=hT[:, k, :], in0=psT[:, k * 256:(k + 1) * 256], scalar1=A[:, k:k + 1], scalar2=B[:, k:k + 1], op0=ALU.mult, op1=ALU.add),
                         reads=["b6", "b7", "modv"], writes=["hT"])
                else:
                    S.op("act", C("activation", out=hT[:, k, :], in_=psT[:, k * 256:(k + 1) * 256], func=AF.Identity, scale=A[:, k:k + 1], bias=B[:, k:k + 1]),
                         reads=["b6", "b7", "modv"], writes=["hT"])

        def ffn_phase(src, dst, A, B, G, w1, w3, w2):
            with ExitStack() as ph:
                sbp = lambda name, shape, dt=F32: ph.enter_context(nc.sbuf_tensor(nm(name), list(shape), dt))
                w1s = sbp("w1s", [128, 8, DFF], BF16); w3s = sbp("w3s", [128, 8, DFF], BF16); w2s = sbp("w2s", [128, NFC, D], BF16)
                xs = sbp("xs", [128, 2, D]); xn = sbp("xn", [128, 2, D], BF16)
                ss = sbp("ss", [128, 2]); rs = sbp("rs", [128, 2])
                hT = sbp("hT", [128, 8, 256], BF16)
                s1 = [sbp("s1_%d" % i, [128, 256]) for i in range(2)]
                g = [sbp("g_%d" % i, [128, 256], BF16) for i in range(2)]
                tmp = sbp("tmp", [128, D])
                hw = DFF // 2
                for k in range(8):
                    for hf in range(2):
                        S.op("pool", C("dma_start", out=w1s[:, k, hf * hw:(hf + 1) * hw], in_=w1[k * 128:(k + 1) * 128, hf * hw:(hf + 1) * hw]), writes=["w1s"], dsem="w1s")
                        S.op("pool", C("dma_start", out=w3s[:, k, hf * hw:(hf + 1) * hw], in_=w3[k * 128:(k + 1) * 128, hf * hw:(hf + 1) * hw]), writes=["w3s"], dsem="w3s")
                for fc in range(NFC):
                    S.op("pool", C("dma_start", out=w2s[:, fc, :], in_=w2[fc * 128:(fc + 1) * 128, :]), writes=["w2s"], dsem="w2s")
                for hs in range(NT // 256):
                    r0 = hs * 256
                    S.op("sp", C("dma_start", out=xs[:], in_=src[r0:r0 + 256, :].rearrange("(t p) d -> p t d", p=128)), writes=["xs"], dsem="xs")
                    norm_T((ss, rs, xn), xs, A, B, hT)

                    def ymm(fc):
                        gb = g[fc % 2]
                        for t in range(2):
                            for hf in range(2):
                                S.op("pe", C("matmul", ps[:, t * 2 + hf, :], lhsT=gb[:, t * 128:(t + 1) * 128], rhs=w2s[:, fc, hf * 512:(hf + 1) * 512],
                                             start=(fc == 0), stop=(fc == NFC - 1)), reads=["g%d" % (fc % 2), "w2s"], writes=["b%d" % (t * 2 + hf)])
                    for fc in range(NFC):
                        ub = 4 + 2 * (fc % 2)
                        for (wsb, wn, bank) in ((w1s, "w1s", ub), (w3s, "w3s", ub + 1)):
                            for k in range(8):
                                S.op("pe", C("matmul", ps[:, bank, 0:256], lhsT=wsb[:, k, fc * 128:(fc + 1) * 128], rhs=hT[:, k, :], start=(k == 0), stop=(k == 7)),
                                     reads=[wn, "hT"], writes=["b%d" % bank])
                        if fc > 0:
                            ymm(fc - 1)
                        S.op("act", C("activation", out=s1[fc % 2][:], in_=ps[:, ub, 0:256], func=AF.Silu), reads=["b%d" % ub], writes=["s1_%d" % (fc % 2)])
                        S.op("dve", C("tensor_tensor", out=g[fc % 2][:], in0=ps[:, ub + 1, 0:256], in1=s1[fc % 2][:], op=ALU.mult),
                             reads=["b%d" % (ub + 1), "s1_%d" % (fc % 2)], writes=["g%d" % (fc % 2)])
                    ymm(NFC - 1)
                    for t in range(2):
                        S.op("dve", C("tensor_tensor", out=tmp[:], in0=ps[:, 2 * t:2 * t + 2, :].rearrange("p a b -> p (a b)"), in1=G[:], op=ALU.mult),
                             reads=["b%d" % (2 * t), "b%d" % (2 * t + 1), "modv"], writes=["tmp"])
                        S.op("pool", C("tensor_tensor", out=xs[:, t, :], in0=tmp[:], in1=xs[:, t, :], op=ALU.add), reads=["tmp", "xs"], writes=["xs"])
                    S.op("sp", C("dma_start", out=dst[r0:r0 + 256, :].rearrange("(t p) d -> p t d", p=128), in_=xs[:]), reads=["xs"], writes=["xdst"], dsem="xo")
                _barrier(S)

        def lt_mix(l, src, dst):
            p = L[l]
            with ExitStack() as mixscope:
                mm = mod_phase(p, [(6, "sh3"), (7, "sc3")], [(5, "g2", 1.0), (8, "g3", 0.5)], mixscope)
                A3 = make_A(mixscope, mm["sc3"], mm["ng"], 2)
                with ExitStack() as ph:
                    sbp = lambda name, shape, dt=F32: ph.enter_context(nc.sbuf_tensor(nm(name), list(shape), dt))
                    wos = sbp("wos", [128, 8, D], BF16)
                    og = sbp("og", [128, D])
                    ot65 = sbp("ot65", [65, 16, 128])
                    oa = sbp("oa", [128, 16, 65]); sq = sbp("osq", [128, 16, 64]); oss = sbp("oss", [128, 16]); z2 = sbp("z2", [128, 16])
                    on = sbp("on", [128, D], BF16); on32 = sbp("on32", [128, D]); oT = sbp("oT", [128, 8, 128], BF16)
                    xs = sbp("mxs", [128, D]); tmp = sbp("mtmp", [128, D]); xo = sbp("mxo", [128, D])
                    for k in range(8):
                        S.op("pool", C("dma_start", out=wos[:, k, :], in_=p["w_out"][k * 128:(k + 1) * 128, :]), writes=["wos"], dsem="wos")
                    S.op("sp", C("dma_start", out=og[:], in_=p["outg"][:, :].partition_broadcast(128)), writes=["og"], dsem="og")
                    for tt in range(NT // 128):
                        r0 = tt * 128
                        S.op("sp", C("dma_start", out=ot65[:], in_=OT[:, :, r0:r0 + 128].rearrange("h r n -> r h n")), writes=["ot65"], dsem="ot65")
                        S.op("sp", C("dma_start", out=xs[:], in_=src[r0:r0 + 128, :]), writes=["mxs"], dsem="mxs")
                        for h in range(16):
                            bank = 2 + h // 7; off = (h % 7) * 65
                            S.op("pe", C("transpose", out=ps[:, bank, off:off + 65], in_=ot65[:, h, :], identity=ident32[0:65, 0:65]), reads=["ot65", "ident32"], writes=["b%d" % bank])
                        for (bank, h0, nh_) in ((2, 0, 7), (3, 7, 7), (4, 14, 2)):
                            S.op("act", C("activation", out=oa[:, h0:h0 + nh_, :].rearrange("p h d -> p (h d)"), in_=ps[:, bank, 0:nh_ * 65], func=AF.Copy),
                                 reads=["b%d" % bank], writes=["oa"])
                        S.op("pool", C("tensor_tensor", out=sq[:], in0=oa[:, :, 0:64], in1=oa[:, :, 0:64], op=ALU.mult), reads=["oa"], writes=["osq"])
                        S.op("dve", C("tensor_reduce", out=oss[:], in_=sq[:, :, :], axis=AX.X, op=ALU.add, opt_input=False, opt_output=False), reads=["osq"], writes=["oss"])
                        S.op("dve", C("scalar_tensor_tensor", out=z2[:], in0=oa[:, :, 64], scalar=EPS, in1=oa[:, :, 64], op0=ALU.mult, op1=ALU.mult), reads=["oa"], writes=["z2"])
                        S.op("dve", C("scalar_tensor_tensor", out=oss[:], in0=oss[:], scalar=1.0 / 64, in1=z2[:], op0=ALU.mult, op1=ALU.add), reads=["oss", "z2"], writes=["oss"])
                        S.op("act", C("activation", out=oss[:], in_=oss[:], func=AF.Ln), reads=["oss"], writes=["oss"])
                        S.op("act", C("activation", out=oss[:], in_=oss[:], func=AF.Exp, scale=-0.5), reads=["oss"], writes=["oss"])
                        S.op("dve", C("tensor_tensor", out=on32[:].rearrange("p (h d) -> p h d", d=64), in0=oa[:, :, 0:64],
                                      in1=oss[:].unsqueeze(2).to_broadcast([128, 16, 64]), op=ALU.mult), reads=["oa", "oss"], writes=["on32"])
                        S.op("pool", C("tensor_tensor", out=on[:], in0=on32[:], in1=og[:], op=ALU.mult), reads=["on32", "og"], writes=["on"])
                        psT = ps[:, 6, :].bitcast(BF16)
                        for k in range(8):
                            S.op("pe", C("transpose", out=psT[:, k * 128:(k + 1) * 128], in_=on[:, k * 128:(k + 1) * 128], identity=ident[:]), reads=["on", "ident"], writes=["b6"])
                        S.op("act", C("activation", out=oT[:].rearrange("p k t -> p (k t)"), in_=psT[:, 0:1024], func=AF.Copy), reads=["b6"], writes=["oT"])
                        for hf in range(2):
                            for k in range(8):
                                S.op("pe", C("matmul", ps[:, hf, :], lhsT=oT[:, k, :], rhs=wos[:, k, hf * 512:(hf + 1) * 512], start=(k == 0), stop=(k == 7)),
                                     reads=["oT", "wos"], writes=["b%d" % hf])
                        S.op("dve", C("tensor_tensor", out=tmp[:], in0=ps[:, 0:2, :].rearrange("p a b -> p (a b)"), in1=mm["g2"][:], op=ALU.mult),
                             reads=["b0", "b1", "modv"], writes=["mtmp"])
                        S.op("pool", C("tensor_tensor", out=xo[:], in0=tmp[:], in1=xs[:], op=ALU.add), reads=["mtmp", "mxs"], writes=["mxo"])
                        S.op("sp", C("dma_start", out=xC[r0:r0 + 128, :], in_=xo[:]), reads=["mxo"], writes=["xdst"], dsem="mxo")
                    _barrier(S)
                ffn_phase(xC, dst, A3, mm["sh3"], mm["g3"], p["w1_1"], p["w3_1"], p["w2_1"])

        def lt_pre(l, src, dst):
            p = L[l]
            with ExitStack() as prescope:
                pm = mod_phase(p, [(0, "sh1"), (1, "sc1"), (3, "sh2"), (4, "sc2")], [(2, "g1", 0.5)], prescope)
                A1 = make_A(prescope, pm["sc1"], pm["ng"], 0)
                A2 = make_A(prescope, pm["sc2"], pm["ng"], 1)
                ffn_phase(src, dst, A1, pm["sh1"], pm["g1"], p["w1_0"], p["w3_0"], p["w2_0"])
                with ExitStack() as ph:
                    sbp = lambda name, shape, dt=F32: ph.enter_context(nc.sbuf_tensor(nm(name), list(shape), dt))
                    wis = sbp("wis", [128, 8, DIN], BF16)
                    qg = sbp("qg", [128, 1024])
                    xs = sbp("pxs", [128, 2, D]); xn = sbp("pxn", [128, 2, D], BF16)
                    ss = sbp("pss", [128, 2]); rs = sbp("prs", [128, 2])
                    hT = sbp("phT", [128, 8, 256], BF16)
                    pr = [sbp("pr%d" % i, [128, DIN]) for i in range(2)]
                    prb = [sbp("prb%d" % i, [128, NPB], BF16) for i in range(2)]
                    vaug = [sbp("vaug%d" % i, [128, 16, 65], BF16) for i in range(2)]
                    fT = sbp("fT", [128, NFB, 256], BF16)
                    sq = sbp("psq", [128, 1024]); hs_ = sbp("phs", [128, 16])
                    cst = sbp("cst", [128, 2, 16])
                    r1 = sbp("r1", [128, 25, 8]); r2 = sbp("r2", [128, 25, 8]); r3 = sbp("r3", [128, 25, 8]); r4 = sbp("r4", [128, 25, 8])
                    cw = 1828
                    for k in range(8):
                        for hf in range(2):
                            S.op("pool", C("dma_start", out=wis[:, k, hf * cw:(hf + 1) * cw], in_=p["w_in"][k * 128:(k + 1) * 128, hf * cw:(hf + 1) * cw]), writes=["wis"], dsem="wis")
                    S.op("sp", C("dma_start", out=qg[:], in_=p["qkg"][:, :].partition_broadcast(128)), writes=["qg"], dsem="qg")
                    for t in range(2):
                        S.op("pool", C("memset", vaug[t][:], 1.0), writes=["vaug%d" % t])
                    S.op("pool", C("memset", fT[:], 0.0), writes=["fT"])
                    for hs in range(NT // 256):
                        r0 = hs * 256
                        S.op("sp", C("dma_start", out=xs[:], in_=dst[r0:r0 + 256, :].rearrange("(t p) d -> p t d", p=128)), writes=["xs"], dsem="pxs")
                        S.op("sp", C("dma_start", out=cst[:], in_=cs[r0:r0 + 256, :].rearrange("(t p) d -> p t d", p=128)), writes=["cst"], dsem="cst")
                        norm_T((ss, rs, xn), xs, A2, pm["sh2"], hT)
                        for t in range(2):
                            prt = pr[t]; prbt = prb[t]; pn = "pr%d" % t; pbn = "prb%d" % t
                            ngrp = (DIN + 511) // 512
                            for gi in range(ngrp):
                                c0 = gi * 512; c1 = min(DIN, c0 + 512)
                                bank = gi % 6 if gi < 6 else gi - 6
                                for k in range(8):
                                    S.op("pe", C("matmul", ps[:, bank, 0:c1 - c0], lhsT=hT[:, k, t * 128:(t + 1) * 128], rhs=wis[:, k, c0:c1], start=(k == 0), stop=(k == 7)),
                                         reads=["hT", "wis"], writes=["b%d" % bank])
                                S.op("act", C("activation", out=prt[:, c0:c1], in_=ps[:, bank, 0:c1 - c0], func=AF.Copy), reads=["b%d" % bank], writes=[pn])
                            S.op("pool", C("tensor_tensor", out=sq[:], in0=prt[:, 0:1024], in1=prt[:, 0:1024], op=ALU.mult), reads=[pn], writes=["psq"])
                            S.op("dve", C("tensor_reduce", out=hs_[:], in_=sq[:].rearrange("p (h d) -> p h d", d=64), axis=AX.X, op=ALU.add), reads=["psq"], writes=["phs"])
                            S.op("act", C("activation", out=hs_[:], in_=hs_[:], func=AF.Ln, scale=1.0 / 64, bias=epsb[:]), reads=["phs", "epsb"], writes=["phs"])
                            S.op("act", C("activation", out=hs_[:], in_=hs_[:], func=AF.Exp, scale=-0.5), reads=["phs"], writes=["phs"])
                            S.op("dve", C("tensor_tensor", out=prt[:, 0:1024].rearrange("p (h d) -> p h d", d=64), in0=prt[:, 0:1024].rearrange("p (h d) -> p h d", d=64),
                                          in1=hs_[:].unsqueeze(2).to_broadcast([128, 16, 64]), op=ALU.mult), reads=[pn, "phs"], writes=[pn])
                            S.op("pool", C("tensor_tensor", out=prt[:, 0:1024], in0=prt[:, 0:1024], in1=qg[:], op=ALU.mult), reads=[pn, "qg"], writes=[pn])
                            hv = prt[:, 0:1600].rearrange("p (h d) -> p h d", d=64)
                            cosb = cst[:, t, 0:8].unsqueeze(1).to_broadcast([128, 25, 8]); sinb = cst[:, t, 8:16].unsqueeze(1).to_broadcast([128, 25, 8])
                            S.op("dve", C("tensor_tensor", out=r1[:], in0=hv[:, :, 0:8], in1=cosb, op=ALU.mult), reads=[pn, "cst"], writes=["r1"])
                            S.op("pool", C("tensor_tensor", out=r2[:], in0=hv[:, :, 8:16], in1=sinb, op=ALU.mult), reads=[pn, "cst"], writes=["r2"])
                            S.op("dve", C("tensor_tensor", out=r3[:], in0=hv[:, :, 8:16], in1=cosb, op=ALU.mult), reads=[pn, "cst"], writes=["r3"])
                            S.op("pool", C("tensor_tensor", out=r4[:], in0=hv[:, :, 0:8], in1=sinb, op=ALU.mult), reads=[pn, "cst"], writes=["r4"])
                            S.op("dve", C("tensor_tensor", out=hv[:, :, 0:8], in0=r1[:], in1=r2[:], op=ALU.subtract), reads=["r1", "r2"], writes=[pn])
                            S.op("pool", C("tensor_tensor", out=hv[:, :, 8:16], in0=r3[:], in1=r4[:], op=ALU.add), reads=["r3", "r4"], writes=[pn])
                            S.op("act", C("activation", out=prbt[:], in_=prt[:, 0:NPB], func=AF.Copy), reads=[pn], writes=[pbn])
                            S.op("pool", C("tensor_copy", out=vaug[t][:, :, 0:64], in_=prbt[:, 2624:3648].rearrange("p (h d) -> p h d", d=64)), reads=[pbn], writes=["vaug%d" % t])
                            S.op("sp", C("dma_start", out=Vscr[hs * 2 + t], in_=vaug[t][:].rearrange("p h d -> p (h d)")), reads=["vaug%d" % t], writes=["Vscr"], dsem="vaug%d" % t)
                            S.op("sp", C("dma_start", out=wiS[r0 + t * 128:r0 + (t + 1) * 128, :], in_=prt[:, NPB:DIN]), reads=[pn], writes=["wiS"], dsem="W%d" % t)
                        psT = ps[:, 6:8, :].rearrange("p a b -> p (a b)").bitcast(BF16)
                        for rnd in range(3):
                            b0 = rnd * 8; nb_ = min(8, NFB - b0)
                            for bi in range(nb_):
                                blk = b0 + bi
                                fw = 128 if blk < 20 else 64
                                for t in range(2):
                                    S.op("pe", C("transpose", out=psT[0:fw, bi * 256 + t * 128:bi * 256 + (t + 1) * 128], in_=prb[t][:, blk * 128:blk * 128 + fw], identity=ident[:]),
                                         reads=["prb%d" % t, "ident"], writes=["b6", "b7"])
                            if b0 + nb_ <= 20:
                                S.op("dve", C("tensor_copy", out=fT[:, b0:b0 + nb_, :].rearrange("p b n -> p (b n)"), in_=psT[:, 0:nb_ * 256]), reads=["b6", "b7"], writes=["fT"])
                            else:
                                S.op("dve", C("tensor_copy", out=fT[:, b0:b0 + nb_ - 1, :].rearrange("p b n -> p (b n)"), in_=psT[:, 0:(nb_ - 1) * 256]), reads=["b6", "b7"], writes=["fT"])
                                S.op("dve", C("tensor_copy", out=fT[0:64, NFB - 1, :], in_=psT[0:64, (nb_ - 1) * 256:nb_ * 256]), reads=["b6", "b7"], writes=["fT"])
                        S.op("sp", C("dma_start", out=FT.rearrange("(b p) n -> p b n", p=128)[:, :, r0:r0 + 256], in_=fT[:]), reads=["fT"], writes=["FT"], dsem="fT")
                    _barrier(S)

        def lb():
            with ExitStack() as lbs:
                sbl = lambda name, shape, dt=F32: lbs.enter_context(nc.sbuf_tensor(nm(name), list(shape), dt))
                mle = sbl("mle", [128, 4 * R, 512], BF16); mlt = sbl("mlt", [128, 4 * R, 512], BF16)
                S.op("sp", C("dma_start", out=mle[:], in_=mLE[:]), writes=["mle"], dsem="c_mle")
                S.op("sp", C("dma_start", out=mlt[:], in_=mLT[:]), writes=["mlt"], dsem="c_mlt")
                with ExitStack() as ph:
                    sbp = lambda name, shape, dt=F32: ph.enter_context(nc.sbuf_tensor(nm(name), list(shape), dt))
                    kis = sbp("kis", [128, SK // 2], BF16)
                    qis = sbp("qis", [128, 8, 512], BF16)
                    wis = sbp("wisb", [128, 4, 8])
                    cb = sbp("cb", [128, 4, R * 512])
                    pow2 = sbp("pow2", [128, NIT])
                    acc = sbp("acc", [128, SK]); nsel = sbp("nsel", [128, SK], BF16); junk = sbp("junkb", [128, SK], BF16)
                    rr = [sbp("rr%d" % i, [128, 512]) for i in range(2)]
                    stg = [sbp("stg%d" % i, [128, 4, 128], BF16) for i in range(2)]
                    mn = sbp("mn", [128, 16 * R]); m8 = sbp("m8", [128, 8]); lo = sbp("lo", [128, 1]); w0 = sbp("w0", [128, 1]); W = sbp("W", [128, NIT])
                    mid = sbp("mid", [128, 1]); cnt = sbp("cnt", [128, 1]); dl = sbp("dl", [128, 1])
                    S.op("sp", C("dma_start", out=kis[0:64, :], in_=FT[1536:1600, 0:SK // 2]), reads=["FT"], writes=["kis"], dsem="kis")
                    S.op("sp", C("dma_start", out=kis[64:128, :], in_=FT[1536:1600, SK // 2:SK]), reads=["FT"], writes=["kis"], dsem="kis")
                    S.op("sp", C("dma_start", out=cb[:], in_=cbd[:]), writes=["cb"], dsem="cb")
                    S.op("sp", C("dma_start", out=pow2[:], in_=pow2d[:]), writes=["pow2"], dsem="pow2")
                    tcount = 0
                    for j in range(NCH):
                        nkc = R * (j + 1)
                        Kmax = nkc * 512
                        qsrc = FT[1024:1536, j * 512:(j + 1) * 512].rearrange("(h d) n -> d h n", d=64)
                        S.op("sp", C("dma_start", out=qis[0:64], in_=qsrc), reads=["FT"], writes=["qis"], dsem="qis")
                        S.op("sp", C("dma_start", out=qis[64:128], in_=qsrc), reads=["FT"], writes=["qis"], dsem="qis")
                        S.op("sp", C("dma_start", out=wis[:], in_=wiS[j * 512:(j + 1) * 512, :].rearrange("(t p) h -> p t h", p=128)), reads=["wiS"], writes=["wisb"], dsem="wisb")
                        for qt in range(4):
                            for kc in range(nkc):
                                k0 = kc * 512
                                half = 0 if k0 < SK // 2 else 1
                                kcol = k0 - half * (SK // 2)
                                pb = half * 64
                                ab = 2 + (kc % 2)
                                for h in range(8):
                                    lb_ = h % 2
                                    S.op("pe", C("matmul", ps[:, lb_, :], lhsT=qis[pb:pb + 64, h, qt * 128:(qt + 1) * 128], rhs=kis[pb:pb + 64, kcol:kcol + 512], start=True, stop=True),
                                         reads=["qis", "kis"], writes=["b%d" % lb_])
                                    S.op("act", C("activation", out=rr[h % 2][:], in_=ps[:, lb_, :], func=AF.Relu), reads=["b%d" % lb_], writes=["rr%d" % (h % 2)])
                                    if h == 0:
                                        S.op("dve", C("tensor_scalar", out=ps[:, ab, :], in0=rr[0][:], scalar1=wis[:, qt, 0:1], scalar2=None, op0=ALU.mult),
                                             reads=["rr0", "wisb"], writes=["b%d" % ab])
                                    else:
                                        S.op("dve", C("scalar_tensor_tensor", out=ps[:, ab, :], in0=rr[h % 2][:], scalar=wis[:, qt, h:h + 1], in1=ps[:, ab, :], op0=ALU.mult, op1=ALU.add),
                                             reads=["rr%d" % (h % 2), "wisb", "b%d" % ab], writes=["b%d" % ab])
                                S.op("dve", C("tensor_scalar", out=acc[:, k0:k0 + 512], in0=ps[:, ab, :], scalar1=1.0, scalar2=None, op0=ALU.mult, op1=ALU.min, accum_out=mn[:, kc:kc + 1]),
                                     reads=["b%d" % ab], writes=["acc", "mn"])
                            S.op("pool", C("tensor_tensor", out=acc[:, Kmax - R * 512:Kmax], in0=acc[:, Kmax - R * 512:Kmax], in1=cb[:, qt, :], op=ALU.add), reads=["acc", "cb"], writes=["acc"])
                            S.op("dve", C("max", out=m8[:], in_=acc[:, 0:Kmax]), reads=["acc"], writes=["m8"])
                            if nkc > 1:
                                S.op("dve", C("tensor_reduce", out=lo[:], in_=mn[:, 0:nkc], axis=AX.X, op=ALU.min), reads=["mn"], writes=["lo"])
                                S.op("dve", C("tensor_scalar", out=lo[:], in0=lo[:], scalar1=-1.0, scalar2=None, op0=ALU.add), reads=["lo"], writes=["lo"])
                            else:
                                S.op("dve", C("tensor_scalar", out=lo[:], in0=mn[:, 0:1], scalar1=-1.0, scalar2=None, op0=ALU.add), reads=["mn"], writes=["lo"])
                            S.op("dve", C("tensor_tensor", out=w0[:], in0=m8[:, 0:1], in1=lo[:], op=ALU.subtract), reads=["m8", "lo"], writes=["w0"])
                            S.op("dve", C("tensor_scalar", out=W[:], in0=pow2[:], scalar1=w0[:, 0:1], scalar2=None, op0=ALU.mult), reads=["pow2", "w0"], writes=["W"])
                            for it in range(NIT):
                                S.op("dve", C("tensor_tensor", out=mid[:], in0=lo[:], in1=W[:, it:it + 1], op=ALU.add), reads=["lo", "W"], writes=["mid"])
                                S.op("dve", C("tensor_scalar", out=junk[:, 0:Kmax], in0=acc[:, 0:Kmax], scalar1=mid[:, 0:1], scalar2=None, op0=ALU.is_gt, op1=ALU.add, accum_out=cnt[:, 0:1]),
                                     reads=["acc", "mid"], writes=["junk", "cnt"])
                                S.op("dve", C("scalar_tensor_tensor", out=dl[:], in0=cnt[:], scalar=float(TOPK), in1=W[:, it:it + 1], op0=ALU.is_ge, op1=ALU.mult), reads=["cnt", "W"], writes=["dl"])
                                S.op("dve", C("tensor_tensor", out=lo[:], in0=lo[:], in1=dl[:], op=ALU.add), reads=["lo", "dl"], writes=["lo"])
                            S.op("dve", C("tensor_scalar", out=nsel[:, 0:Kmax], in0=acc[:, 0:Kmax], scalar1=lo[:, 0:1], scalar2=None, op0=ALU.is_le), reads=["acc", "lo"], writes=["nsel"])
                            for g4 in range(nkc):
                                tb = 4 + (tcount % 2)
                                sg = stg[tcount % 2]; sgn = "stg%d" % (tcount % 2)
                                tcount += 1
                                pT = ps[:, tb, :].bitcast(BF16)
                                for u in range(4):
                                    kt = g4 * 4 + u
                                    S.op("pe", C("transpose", out=pT[:, u * 128:(u + 1) * 128], in_=nsel[:, kt * 128:(kt + 1) * 128], identity=ident[:]), reads=["nsel", "ident"], writes=["b%d" % tb])
                                S.op("act", C("activation", out=sg[:].rearrange("p a b -> p (a b)"), in_=pT[:, 0:512], func=AF.Copy), reads=["b%d" % tb], writes=[sgn])
                                S.op("sp", C("dma_start", out=nsT[j, g4 * 4:(g4 + 1) * 4, :, qt * 128:(qt + 1) * 128].rearrange("a p q -> p a q"), in_=sg[:]), reads=[sgn], writes=["nsT"], dsem=sgn)
                    _barrier(S)

                def sweep_phase(kind):
                    nh = {"A": 4, "B2": 4, "C": 8}[kind]
                    KR = 96 if kind == "A" else 64
                    hbase = {"A": 0, "B2": 4, "C": 8}[kind]
                    qrow = {"A": 0, "B2": 512, "C": 1600}[kind]
                    krow = {"A": 256, "B2": 768, "C": 2112}[kind]
                    with ExitStack() as ph:
                        sbp = lambda name, shape, dt=F32: ph.enter_context(nc.sbuf_tensor(nm(name), list(shape), dt))
                        kT = [sbp("kT%d" % i, [KR, SK], BF16) for i in range(2)]
                        qS = [sbp("qS%d" % i, [KR, 512], BF16) for i in range(3)]
                        VG = sbp("VG", [128, NKT, 4, 65], BF16)
                        Pt = [sbp("P%d" % i, [128, 512], BF16) for i in range(3)]
                        osb = [sbp("osb%d" % i, [65, 512]) for i in range(2)]
                        if kind == "A":
                            pbs = sbp("pbs", [128, NCH, 4, 32]); pvs = sbp("pvs", [128, NCH, 4, 32]); fus = sbp("fus", [128, NCH, 4, 32])
                            km32 = sbp("km32", [64, 32]); kmT = sbp("kmT", [64, 32]); q32 = sbp("q32", [64, 512])
                            gm = sbp("gm", [128, 4, 32]); m8 = sbp("m8a", [128, 4, 8]); nM = sbp("nM", [128, 4, 32]); nMp = sbp("nMp", [128, 4, 96], BF16)
                            S.op("sp", C("dma_start", out=pbs[:], in_=pbd[:]), writes=["pbs"], dsem="pbs")
                            S.op("sp", C("dma_start", out=pvs[:], in_=pvd[:]), writes=["pvs"], dsem="pvs")
                            S.op("sp", C("dma_start", out=fus[:], in_=fud[:]), writes=["fus"], dsem="fus")
                            S.op("pool", C("memset", nMp[:], 0.0), writes=["nMp"])
                            S.op("pool", C("memset", km32[:], 0.0), writes=["km32"])
                        if kind == "C":
                            nkT = [sbp("nkT%d" % i, [64, SK], BF16) for i in range(2)]
                            e32 = [sbp("e32_%d" % i, [128, 512]) for i in range(2)]
                            sp_ = [sbp("sp%d" % i, [128, 512], BF16) for i in range(3)]
                            spacc = sbp("spacc", [128, 512], BF16)
                        if kind == "B2":
                            nst = [sbp("nst%d" % i, [128, 512], BF16) for i in range(3)]
                        ocount = 0
                        qcount = 0
                        for h in range(nh):
                            hb = h % 2
                            hh = h % 4
                            kTh = kT[hb]
                            kn = "kT%d" % hb
                            if hh == 0:
                                hg = (hbase + h)
                                nq = 4
                                for qd in range(nq):
                                    t0_ = qd * (NKT // nq); t1_ = (qd + 1) * (NKT // nq)
                                    S.op("sp", C("dma_start", out=VG[:, t0_:t1_].rearrange("p t h d -> p t (h d)"), in_=Vscr[t0_:t1_, :, hg * 65:(hg + 4) * 65].rearrange("t p f -> p t f")),
                                         reads=["Vscr"], writes=["VG"], dsem="VG")
                            S.op("sp", C("dma_start", out=kTh[0:64, :], in_=FT[krow + h * 64:krow + (h + 1) * 64, 0:SK]), reads=["FT"], writes=[kn], dsem=kn)
                            if kind == "A":
                                S.op("sp", C("dma_start", out=kTh[64:96, :], in_=ohd[:]), writes=[kn], dsem=kn)
                                S.op("dve", C("tensor_reduce", out=km32[:, 0:NB], in_=kTh[0:64, :].rearrange("p (b k) -> p b k", k=256), axis=AX.X, op=ALU.add), reads=[kn], writes=["km32"])
                                S.op("dve", C("tensor_scalar", out=kmT[:], in0=km32[:], scalar1=1.0 / 256, scalar2=None, op0=ALU.mult), reads=["km32"], writes=["kmT"])
                            if kind == "C":
                                nkTh = nkT[hb]; nkn = "nkT%d" % hb
                                S.op("pool", C("tensor_scalar", out=nkTh[:], in0=kTh[:], scalar1=-0.125, scalar2=None, op0=ALU.mult), reads=[kn], writes=[nkn])
                            for j in range(NCH):
                                nkt = 4 * R * (j + 1)
                                d0 = 4 * R * j
                                qsb = qS[qcount % 3]; qn = "qS%d" % (qcount % 3)
                                qcount += 1
                                S.op("sp", C("dma_start", out=qsb[0:64, :], in_=FT[qrow + h * 64:qrow + (h + 1) * 64, j * 512:(j + 1) * 512]), reads=["FT"], writes=[qn], dsem=qn)
                                qs = qsb[:, :]
                                ob = 6 + (ocount % 2); obn = "b%d" % ob
                                osbt = osb[ocount % 2]; osn = "osb%d" % (ocount % 2)
                                ocount += 1
                                if kind == "A":
                                    S.op("pool", C("tensor_copy", out=q32[:], in_=qsb[0:64, :]), reads=[qn], writes=["q32"])
                                    for qt in range(4):
                                        S.op("pe", C("matmul", ps[:, 5, qt * 32:(qt + 1) * 32], lhsT=q32[:, qt * 128:(qt + 1) * 128], rhs=kmT[:], start=True, stop=True),
                                             reads=["q32", "kmT"], writes=["b5"])
                                    S.op("dve", C("tensor_tensor", out=gm[:].rearrange("p a b -> p (a b)"), in0=ps[:, 5, 0:128], in1=pbs[:, j].rearrange("p a b -> p (a b)"), op=ALU.add),
                                         reads=["b5", "pbs"], writes=["gm"])
                                    for qt in range(4):
                                        S.op("dve", C("max", out=m8[:, qt, :], in_=gm[:, qt, :]), reads=["gm"], writes=["m8a"])
                                    for qt in range(4):
                                        S.op("dve", C("tensor_scalar", out=nM[:, qt, :], in0=gm[:, qt, :], scalar1=m8[:, qt, 2:3], scalar2=None, op0=ALU.is_lt), reads=["gm", "m8a"], writes=["nM"])
                                    S.op("dve", C("tensor_tensor", out=nM[:], in0=nM[:], in1=pvs[:, j], op=ALU.mult), reads=["nM", "pvs"], writes=["nM"])
                                    S.op("dve", C("tensor_tensor", out=nMp[:, :, 64:96], in0=nM[:], in1=fus[:, j], op=ALU.add), reads=["nM", "fus"], writes=["nMp"])
                                    for qt in range(4):
                                        S.op("pe", C("matmul", ps[0:96, 5, qt * 128:(qt + 1) * 128], lhsT=nMp[:, qt, :], rhs=ident[:], start=True, stop=True), reads=["nMp", "ident"], writes=["b5"])
                                    S.op("act", C("activation", out=qsb[64:96, :], in_=ps[64:96, 5, :], func=AF.Copy), reads=["b5"], writes=[qn])
                                if kind == "C":
                                    S.op("pool", C("memset", spacc[:], 0.0), writes=["spacc"])
                                order = list(range(nkt)) if kind != "C" else list(range(nkt - 1, -1, -1))
                                n = len(order)

                                def st0(i):
                                    kt = order[i]
                                    sbk = i % 2; sbn = "b%d" % sbk
                                    diag = kt >= d0
                                    u = kt - d0
                                    if kind == "B2":
                                        nb_ = nst[i % 3]; nbn = "nst%d" % (i % 3)
                                        S.op("sp", C("dma_start", out=nb_[:], in_=nsT[j, kt]), reads=["nsT"], writes=[nbn], dsem=nbn)
                                    S.op("pe", C("matmul", ps[:, sbk, :], lhsT=kTh[:, kt * 128:(kt + 1) * 128], rhs=qs, start=True, stop=(kind != "B2")), reads=[kn, qn], writes=[sbn])
                                    if kind == "B2":
                                        S.op("pe", C("matmul", ps[:, sbk, :], lhsT=nbigI[:], rhs=nb_[:], start=False, stop=True), reads=["nbigI", nbn], writes=[sbn])
                                    if kind in ("A", "B2"):
                                        pt = Pt[i % 3]; ptn = "P%d" % (i % 3)
                                        S.op("act", C("activation", out=pt[:], in_=ps[:, sbk, :], func=AF.Exp, scale=0.125), reads=[sbn], writes=[ptn])
                                        if diag and kind == "A":
                                            S.op("pool", C("tensor_tensor", out=pt[:], in0=pt[:], in1=mle[:, u, :], op=ALU.mult), reads=[ptn, "mle"], writes=[ptn])
                                    else:
                                        wbk = 2 + (i % 2); wbn = "b%d" % wbk
                                        S.op("pe", C("matmul", ps[:, wbk, :], lhsT=nkTh[:, kt * 128:(kt + 1) * 128], rhs=qs, start=True, stop=False), reads=[nkn, qn], writes=[wbn])
                                        eb = e32[i % 2]; ebn = "e32_%d" % (i % 2)
                                        spb = sp_[i % 3]; spn = "sp%d" % (i % 3)
                                        S.op("act", C("activation", out=eb[:], in_=ps[:, sbk, :], func=AF.Exp, scale=0.125), reads=[sbn], writes=[ebn])
                                        S.op("act", C("activation", out=spb[:], in_=eb[:], func=AF.Ln, bias=1.0), reads=[ebn], writes=[spn])
                                        if diag:
                                            S.op("pool", C("tensor_tensor", out=spb[:], in0=spb[:], in1=mlt[:, u, :], op=ALU.mult), reads=[spn, "mlt"], writes=[spn])

                                def st1(i):
                                    kt = order[i]
                                    diag = kt >= d0
                                    u = kt - d0
                                    wbk = 2 + (i % 2); wbn = "b%d" % wbk
                                    spb = sp_[i % 3]; spn = "sp%d" % (i % 3)
                                    pt = Pt[i % 3]; ptn = "P%d" % (i % 3)
                                    S.op("pe", C("matmul", ps[:, wbk, :], lhsT=uinc[:], rhs=spb[:], start=False, stop=(i == 0)), reads=["uinc", spn], writes=[wbn])
                                    if i > 0:
                                        S.op("pe", C("matmul", ps[:, wbk, :], lhsT=ones[:], rhs=spacc[:], start=False, stop=True), reads=["ones", "spacc"], writes=[wbn])
                                    S.op("pool", C("tensor_tensor", out=spacc[:], in0=spacc[:], in1=spb[:], op=ALU.add), reads=["spacc", spn], writes=["spacc"])
                                    S.op("act", C("activation", out=pt[:], in_=ps[:, wbk, :], func=AF.Exp, scale=-1.0), reads=[wbn], writes=[ptn])
                                    if diag:
                                        S.op("pool", C("tensor_tensor", out=pt[:], in0=pt[:], in1=mlt[:, u, :], op=ALU.mult), reads=[ptn, "mlt"], writes=[ptn])

                                def st2(i):
                                    kt = order[i]
                                    pt = Pt[i % 3]; ptn = "P%d" % (i % 3)
                                    S.op("pe", C("matmul", ps[0:65, ob, :], lhsT=VG[:, kt, hh, :], rhs=pt[:], start=(i == 0), stop=(i == n - 1)), reads=["VG", ptn], writes=[obn])

                                if kind == "C":
                                    for t in range(n + 2):
                                        if t < n:
                                            st0(t)
                                        if 1 <= t <= n:
                                            st1(t - 1)
                                        if t >= 2:
                                            st2(t - 2)
                                else:
                                    for t in range(n + 1):
                                        if t < n:
                                            st0(t)
                                        if t >= 1:
                                            st2(t - 1)
                                S.op("dve", C("tensor_copy", out=osbt[:], in_=ps[0:65, ob, :]), reads=[obn], writes=[osn])
                                if kind == "C":
                                    S.op("dve", C("memset", osbt[64:65, :], 1.0), reads=[osn], writes=[osn])
                                S.op("sp", C("dma_start", out=OT[hbase + h, :, j * 512:(j + 1) * 512], in_=osbt[:]), reads=[osn], writes=["OT"], dsem=osn)
                        _barrier(S)

                for kind in ("A", "C", "B2"):
                    sweep_phase(kind)

        cur = x_in
        for l in range(DEPTH):
            if l > 0:
                lt_mix(l - 1, cur, xA)
                cur = xA
            lt_pre(l, cur, xB)
            cur = xB
            lb()
        lt_mix(DEPTH - 1, cur, y_out)
        S.finish_wait("sp", ["xdst"])
        _barrier(S)
        S.emit()
    return nc

BF = ml_dtypes.bfloat16
NIT = 20
BIGA = 240000.0


def own_idx(r, NS):
    return np.concatenate([(2 * j + r) * 512 + np.arange(512) for j in range(NS)])


def lb_consts(r, NS):
    p = np.arange(128)
    d = {}
    u = np.arange(8); f = np.arange(512)
    kk = u[None, :, None] * 128 + p[:, None, None]
    qq = r * 512 + f[None, None, :]
    d["mLE"] = (kk <= qq).astype(BF)
    d["mLT"] = (kk < qq).astype(BF)
    col = np.arange(1024)
    qpos = r * 512 + np.arange(4)[None, :, None] * 128 + p[:, None, None]
    d["cb"] = np.where(col[None, None, :] <= qpos, 0.0, -1e30).astype(np.float32)
    j = np.arange(NS)[None, :, None, None]; qt = np.arange(4)[None, None, :, None]; blk = np.arange(32)[None, None, None, :]
    cur = 4 * j + 2 * r + qt // 2 + 0 * p[:, None, None, None]
    d["pastbias"] = np.where(blk < cur, 0.0, -1e30).astype(np.float32)
    d["pastvalid"] = (blk < cur).astype(np.float32)
    d["future"] = (blk > cur).astype(np.float32)
    d["pow2"] = np.tile((0.5 ** (np.arange(NIT) + 1)).astype(np.float32)[None, :], (128, 1))
    d["ident"] = np.eye(128, dtype=np.float32)
    d["uinc"] = (p[:, None] >= p[None, :]).astype(np.float32)
    return d


def lb_kside(P, NS):
    SK = NS * 1024
    NB = SK // 256
    d = {}
    ka = np.zeros((4, 96, SK), dtype=BF)
    ka[:, 0:64, :] = P[:, 256:512].reshape(SK, 4, 64).transpose(1, 2, 0)
    for b in range(NB):
        ka[:, 64 + b, b * 256:(b + 1) * 256] = BF(-BIGA)
    d["kaT"] = ka
    d["kbT"] = np.ascontiguousarray(P[:, 768:1024].reshape(SK, 4, 64).transpose(1, 2, 0))
    kiT = P[:, 1536:1600].T
    d["ki2"] = np.ascontiguousarray(np.concatenate([kiT[:, :SK // 2], kiT[:, SK // 2:]], 0))
    d["kcT"] = np.ascontiguousarray(P[:, 2112:2624].reshape(SK, 8, 64).transpose(1, 2, 0))
    one = np.ones((SK, 4, 1), dtype=BF)
    def vl(v):
        nh = v.shape[1]
        return np.ascontiguousarray(v.reshape(SK // 128, 128, nh, 65).transpose(2, 1, 0, 3))
    d["va"] = vl(np.concatenate([P[:, 2624:2880].reshape(SK, 4, 64), one], 2))
    d["vb"] = vl(np.concatenate([P[:, 2880:3136].reshape(SK, 4, 64), one], 2))
    d["vc"] = vl(np.concatenate([P[:, 3136:3648].reshape(SK, 8, 64), np.ones((SK, 8, 1), dtype=BF)], 2))
    return d


def lb_qside(Pown, wiown):
    NT = Pown.shape[0]
    d = {}
    d["qaT"] = np.ascontiguousarray(Pown[:, 0:256].reshape(NT, 4, 64).transpose(1, 2, 0))
    d["qbT"] = np.ascontiguousarray(Pown[:, 512:768].reshape(NT, 4, 64).transpose(1, 2, 0))
    qiT = Pown[:, 1024:1536].reshape(NT, 8, 64).transpose(2, 1, 0)
    d["qi2"] = np.ascontiguousarray(np.concatenate([qiT, qiT], 0))
    d["qcT"] = np.ascontiguousarray(Pown[:, 1600:2112].reshape(NT, 8, 64).transpose(1, 2, 0))
    d["wi"] = np.ascontiguousarray(wiown)
    return d


def ot_to_oa(OT):
    Oa = np.ascontiguousarray(OT.transpose(2, 0, 1)).copy()
    Oa[:, 8:, 64] = 1.0
    return Oa.reshape(Oa.shape[0], 16 * 65)


def fused_consts(NCH):
    p = np.arange(128)
    d = {}
    u = np.arange(4); f = np.arange(512)
    kk = u[None, :, None] * 128 + p[:, None, None]
    qq = f[None, None, :]
    d["mLE"] = (kk <= qq).astype(BF)
    d["mLT"] = (kk < qq).astype(BF)
    col = np.arange(512)
    qpos = np.arange(4)[None, :, None] * 128 + p[:, None, None]
    d["cb"] = np.where(col[None, None, :] <= qpos, 0.0, -1e30).astype(np.float32)
    j = np.arange(NCH)[None, :, None, None]; qt = np.arange(4)[None, None, :, None]; blk = np.arange(32)[None, None, None, :]
    cur = 2 * j + qt // 2 + 0 * p[:, None, None, None]
    d["pastbias"] = np.where(blk < cur, 0.0, -1e30).astype(np.float32)
    d["pastvalid"] = (blk < cur).astype(np.float32)
    d["future"] = (blk > cur).astype(np.float32)
    d["pow2"] = np.tile((0.5 ** (np.arange(NIT) + 1)).astype(np.float32)[None, :], (128, 1))
    d["ident"] = np.eye(128, dtype=np.float32)
    d["uinc"] = (p[:, None] >= p[None, :]).astype(np.float32)
    SK = NCH * 512
    oh = np.zeros((32, SK), dtype=BF)
    for b in range(SK // 256):
        oh[b, b * 256:(b + 1) * 256] = BF(-BIGA)
    d["oh"] = oh
    return d

PERM = np.concatenate([np.arange(0, 256), np.arange(256, 512), np.arange(768, 1024), np.arange(1024, 1280),
                       np.arange(1536, 2048), np.arange(2048, 2112), np.arange(2120, 2632), np.arange(2632, 3144),
                       np.arange(512, 768), np.arange(1280, 1536), np.arange(3144, 3656), np.arange(2112, 2120)])
ROPE_THETA = 500000.0
_CACHE = {}


def kernel(**inputs):
    inp = {k: np.asarray(v) for k, v in inputs.items()}
    x = inp["x"]
    B, S_, Dm = x.shape
    NCH = S_ // 512
    depth = inp["w_ada"].shape[0]
    ncores = 2 * B
    key = (NCH, depth)
    if key not in _CACHE:
        _CACHE[key] = build_fused(NCH, depth)
    nc = _CACHE[key]
    shared = fused_consts(NCH)
    inv = (ROPE_THETA ** (-np.arange(0, 16, 2, dtype=np.float32) / np.float32(16))).astype(np.float32)
    ang = np.arange(S_, dtype=np.float32)[:, None] * inv[None, :]
    shared["cs"] = np.ascontiguousarray(np.concatenate([np.cos(ang), np.sin(ang)], 1).astype(np.float32))
    for l in range(depth):
        shared["wada%d" % l] = np.ascontiguousarray(inp["w_ada"][l])
        shared["badar%d" % l] = np.ascontiguousarray(inp["b_ada"][l][None, :])
        shared["badac%d" % l] = np.ascontiguousarray(inp["b_ada"][l].reshape(72, 128).T)
        shared["ng%d" % l] = np.ascontiguousarray(inp["norm_g"][l].reshape(24, 128).T)
        shared["w_in%d" % l] = np.ascontiguousarray(inp["w_in"][l][:, PERM])
        shared["qkg%d" % l] = np.ascontiguousarray(np.repeat(inp["qk_g"][l], 4, axis=0).reshape(1, 1024))
        shared["w_out%d" % l] = np.ascontiguousarray(inp["w_out"][l])
        shared["outg%d" % l] = np.ascontiguousarray(inp["out_g"][l][None, :])
        for i in range(2):
            shared["w1_%d_%d" % (l, i)] = np.ascontiguousarray(inp["ffn_w1"][l, i])
            shared["w3_%d_%d" % (l, i)] = np.ascontiguousarray(inp["ffn_w3"][l, i])
            shared["w2_%d_%d" % (l, i)] = np.ascontiguousarray(inp["ffn_w2"][l, i])
    maps = []
    for ci in range(ncores):
        b = ci // 2
        d = dict(shared)
        d["x"] = np.ascontiguousarray(x[b])
        d["cT"] = np.ascontiguousarray(inp["c"][b].reshape(8, 128).T)
        maps.append(d)
    res = run_bass_kernel_spmd(nc, maps, core_ids=list(range(ncores)))
    out = np.empty((B, S_, Dm), dtype=np.float32)
    for b in range(B):
        out[b] = np.asarray(res.results[2 * b]["y"])
    return out
```

```python
import numpy as np
import ml_dtypes
from contextlib import ExitStack
import concourse.bass as bass
import concourse.mybir as mybir
from concourse.bass_utils import run_bass_kernel_spmd


F32 = mybir.dt.float32
BF16 = mybir.dt.bfloat16
AF = mybir.ActivationFunctionType
ALU = mybir.AluOpType
AX = mybir.AxisListType


def C(name, *a, **kw):
    return lambda e: getattr(e, name)(*a, **kw)


class Sched:
    COMPUTE = ("pe", "act", "dve", "pool")

    def __init__(self, nc, es, n_dma_sems=24):
        self.nc = nc
        self.es = es
        self.engs = {"pe": nc.tensor, "act": nc.scalar, "dve": nc.vector, "pool": nc.gpsimd, "sp": nc.sync}
        self.sem = {e: es.enter_context(nc.semaphore("s_" + e)) for e in self.COMPUTE}
        self.cnt = {e: 0 for e in self.COMPUTE}
        self.known = {e: {} for e in self.engs}
        self.dsems = {}
        self.regW = {}
        self.regR = {}
        self.ops = {e: [] for e in self.engs}
        self.nops = 0

    def _dsem(self, key):
        if key not in self.dsems:
            self.dsems[key] = [self.es.enter_context(self.nc.semaphore("d_%d" % len(self.dsems))), 0]
        return self.dsems[key]

    def _need(self, eng, deps, same_engine_ok):
        waits = []
        kn = self.known[eng]
        for sid, (sem, val) in deps.items():
            if same_engine_ok and sid == eng:
                continue
            if kn.get(sid, 0) >= val:
                continue
            kn[sid] = val
            waits.append((sem, val))
        return waits

    _defer = None

    def begin_defer(self):
        self._defer = []

    def end_defer(self):
        l = self._defer
        self._defer = None
        return l

    def replay(self, X, Y=()):
        ix = iy = 0
        nx, ny = len(X), len(Y)
        while ix < nx or iy < ny:
            if iy >= ny or (ix < nx and ix * ny <= iy * nx):
                self.op(*X[ix]); ix += 1
            else:
                self.op(*Y[iy]); iy += 1

    def op(self, eng, fn, reads=(), writes=(), dsem=None, nosame=None):
        if self._defer is not None:
            self._defer.append((eng, fn, tuple(reads), tuple(writes), dsem, nosame))
            return None
        if nosame is None:
            nosame = (eng == "pe")
        deps = {}
        for r in reads:
            for sid, sv in self.regW.get(r, {}).items():
                if deps.get(sid, (None, 0))[1] < sv[1]:
                    deps[sid] = sv
        for w in writes:
            for d in (self.regW.get(w, {}), self.regR.get(w, {})):
                for sid, sv in d.items():
                    if deps.get(sid, (None, 0))[1] < sv[1]:
                        deps[sid] = sv
        waits = self._need(eng, deps, nosame)
        if dsem is None:
            self.cnt[eng] += 1
            sid, sem, val, inc = eng, self.sem[eng], self.cnt[eng], 1
        else:
            ds = self._dsem(dsem)
            ds[1] += 16
            sid, sem, val, inc = ("d", dsem), ds[0], ds[1], 16
        for r in reads:
            self.regR.setdefault(r, {})[sid] = (sem, val)
        for w in writes:
            self.regW.setdefault(w, {})[sid] = (sem, val)
        self.ops[eng].append((waits, fn, sem, inc))
        self.nops += 1
        return (sid, sem, val)

    def barrier(self, roll=30000):
        deps = {}
        for e in self.COMPUTE:
            if self.cnt[e] > 0:
                deps[e] = (self.sem[e], self.cnt[e])
        for k, (sem, c) in self.dsems.items():
            if c > 0:
                deps[("d", k)] = (sem, c)
        for e in self.engs:
            waits = self._need(e, dict(deps), False)
            self.ops[e].append((waits, None, None, 0))
        self.regW.clear(); self.regR.clear()
        for e in self.COMPUTE:
            if self.cnt[e] > roll:
                self.nroll = getattr(self, "nroll", 0) + 1
                self.sem[e] = self.es.enter_context(self.nc.semaphore("s_%s_%d" % (e, self.nroll)))
                self.cnt[e] = 0
                for e2 in self.engs:
                    self.known[e2].pop(e, None)
        for k in list(self.dsems.keys()):
            if self.dsems[k][1] > roll:
                self.nroll = getattr(self, "nroll", 0) + 1
                self.dsems[k] = [self.es.enter_context(self.nc.semaphore("d_r%d" % self.nroll)), 0]
                for e2 in self.engs:
                    self.known[e2].pop(("d", k), None)

    def finish_wait(self, eng, regions):
        deps = {}
        for r in regions:
            for sid, sv in self.regW.get(r, {}).items():
                if deps.get(sid, (None, 0))[1] < sv[1]:
                    deps[sid] = sv
        waits = self._need(eng, deps, False)
        self.ops[eng].append((waits, None, None, 0))

    def emit(self):
        nc = self.nc
        with nc.Block() as block:
            def mk(ename):
                lst = self.ops[ename]

                def body(e):
                    for waits, fn, sem, inc in lst:
                        for (ws, wv) in waits:
                            e.wait_ge(ws, wv)
                        if fn is not None:
                            ins = fn(e)
                            ins.then_inc(sem, inc)
                return body
            block.tensor(mk("pe"))
            block.scalar(mk("act"))
            block.vector(mk("dve"))
            block.gpsimd(mk("pool"))
            block.sync(mk("sp"))


D = 1024
DFF = 2816
NFC = DFF // 128
DIN = 3656
NPB = 3648
EPS = 1e-6
NIT = 20
BIGA = 240000.0
TOPK = 256
NFB = 21


def _barrier(S):
    S.barrier()


def build_fused(NCH, DEPTH=2, R=1):
    NT = NCH * 512
    SK = NCH * R * 512
    NKT = SK // 128
    NB = SK // 256
    nc = bass.Bass("TRN2", target_bir_lowering=False)
    din = lambda name, shape, dt=F32: nc.dram_tensor(name, list(shape), dt, kind="ExternalInput").ap()
    dout = lambda name, shape, dt=F32: nc.dram_tensor(name, list(shape), dt, kind="ExternalOutput").ap()
    dscr = lambda name, shape, dt=F32: nc.dram_tensor(name, list(shape), dt, kind="Internal").ap()
    uid = [0]

    def nm(name):
        uid[0] += 1
        return "s%d_%s" % (uid[0], name)

    x_in = din("x", [NT, D]); cT = din("cT", [128, 8]); identd = din("ident", [128, 128]); uincd = din("uinc", [128, 128])
    cs = din("cs", [NT, 16])
    L = []
    for l in range(DEPTH):
        p = {}
        p["wada"] = din("wada%d" % l, [D, 9 * D]); p["badar"] = din("badar%d" % l, [1, 9 * D]); p["badac"] = din("badac%d" % l, [128, 72])
        p["ng"] = din("ng%d" % l, [128, 24]); p["w_in"] = din("w_in%d" % l, [D, DIN]); p["qkg"] = din("qkg%d" % l, [1, 1024])
        p["w_out"] = din("w_out%d" % l, [D, D]); p["outg"] = din("outg%d" % l, [1, D])
        for i in range(2):
            p["w1_%d" % i] = din("w1_%d_%d" % (l, i), [D, DFF]); p["w3_%d" % i] = din("w3_%d_%d" % (l, i), [D, DFF]); p["w2_%d" % i] = din("w2_%d_%d" % (l, i), [DFF, D])
        L.append(p)
    mLE = din("mLE", [128, 4 * R, 512], BF16); mLT = din("mLT", [128, 4 * R, 512], BF16)
    cbd = din("cb", [128, 4, R * 512]); ohd = din("oh", [32, SK], BF16)
    pbd = din("pastbias", [128, NCH, 4, 32]); pvd = din("pastvalid", [128, NCH, 4, 32]); fud = din("future", [128, NCH, 4, 32])
    pow2d = din("pow2", [128, NIT + 1])
    y_out = dout("y", [NT, D])
    xA = dscr("xA", [NT, D]); xB = dscr("xB", [NT, D]); xC = dscr("xC", [NT, D])
    FT = dscr("FT", [NFB * 128, SK], BF16); Vscr = dscr("Vscr", [NKT, 128, 16 * 65], BF16); wiS = dscr("wiS", [NT, 8])
    OT = dscr("OT", [16, 65, NT]); nsT = dscr("nsT", [NCH, NKT, 128, 512], BF16)

    with ExitStack() as es:
        S = Sched(nc, es)
        ps = es.enter_context(nc.psum_tensor("ps", [128, 8, 512], F32))
        sbg = lambda name, shape, dt=F32: es.enter_context(nc.sbuf_tensor(nm(name), list(shape), dt))
        ident32 = sbg("ident32", [128, 128]); ident = sbg("identb", [128, 128], BF16); id32b = sbg("id32b", [128, 128])
        nbigI = sbg("nbigI", [128, 128], BF16); uinc = sbg("uinc", [128, 128], BF16); ones = sbg("ones", [128, 128], BF16)
        sc = sbg("sc", [128, 8]); scb = sbg("scb", [128, 8, 128]); epsb = sbg("epsb", [128, 1])
        S.op("sp", C("dma_start", out=ident32[:], in_=identd[:]), writes=["ident32"], dsem="c_id")
        S.op("sp", C("dma_start", out=id32b[:], in_=uincd[:]), writes=["id32b"], dsem="c_id2")
        S.op("sp", C("dma_start", out=sc[:], in_=cT[:]), writes=["sc"], dsem="c_sc")
        S.op("dve", C("tensor_copy", out=ident[:], in_=ident32[:]), reads=["ident32"], writes=["ident"])
        S.op("dve", C("tensor_scalar", out=nbigI[:], in0=ident32[:], scalar1=-BIGA, scalar2=None, op0=ALU.mult), reads=["ident32"], writes=["nbigI"])
        S.op("dve", C("tensor_copy", out=uinc[:], in_=id32b[:]), reads=["id32b"], writes=["uinc"])
        S.op("dve", C("memset", ones[:], 1.0), writes=["ones"])
        S.op("dve", C("memset", epsb[:], EPS), writes=["epsb"])
        S.op("act", C("activation", out=sc[:], in_=sc[:], func=AF.Silu), reads=["sc"], writes=["sc"])
        S.op("dve", C("tensor_copy", out=scb[:], in_=sc[:].unsqueeze(2).to_broadcast([128, 8, 128])), reads=["sc"], writes=["scb"])

        def mod_phase(p, cols, rows, scope):
            out = {}
            sbs = lambda name, shape, dt=F32: scope.enter_context(nc.sbuf_tensor(nm(name), list(shape), dt))
            for (m, name) in cols:
                out[name] = sbs(name, [128, 8])
            for (m, name, scl) in rows:
                out[name] = sbs(name, [128, 1024])
            ngt = sbs("ng", [128, 24])
            with ExitStack() as ph:
                wa = ph.enter_context(nc.sbuf_tensor(nm("wa"), [128, 8, 1024], F32))
                bc = ph.enter_context(nc.sbuf_tensor(nm("bc"), [128, 72], F32))
                br = ph.enter_context(nc.sbuf_tensor(nm("br"), [128, 1024], F32))
                S.op("sp", C("dma_start", out=bc[:], in_=p["badac"][:]), writes=["bc"], dsem="bc")
                for (m, name) in cols:
                    t = out[name]
                    S.op("sp", C("dma_start", out=wa[:], in_=p["wada"][:, m * 1024:(m + 1) * 1024].rearrange("(k p) n -> p k n", p=128)), writes=["wa"], dsem="wa")
                    for kk in range(8):
                        for k in range(8):
                            S.op("pe", C("matmul", ps[:, 0, kk:kk + 1], lhsT=wa[:, k, kk * 128:(kk + 1) * 128], rhs=sc[:, k:k + 1], start=(k == 0), stop=(k == 7)),
                                 reads=["wa", "sc"], writes=["b0"])
                    S.op("dve", C("tensor_tensor", out=t[:], in0=ps[:, 0, 0:8], in1=bc[:, m * 8:(m + 1) * 8], op=ALU.add), reads=["b0", "bc"], writes=["modv"])
                for (m, name, scl) in rows:
                    t = out[name]
                    S.op("sp", C("dma_start", out=wa[:], in_=p["wada"][:, m * 1024:(m + 1) * 1024].rearrange("(k p) n -> p k n", p=128)), writes=["wa"], dsem="wa")
                    S.op("sp", C("dma_start", out=br[:], in_=p["badar"][:, m * 1024:(m + 1) * 1024].partition_broadcast(128)), writes=["br"], dsem="br")
                    for hf in range(2):
                        for k in range(8):
                            S.op("pe", C("matmul", ps[:, 1 + hf, :], lhsT=scb[:, k, :], rhs=wa[:, k, hf * 512:(hf + 1) * 512], start=(k == 0), stop=(k == 7)),
                                 reads=["wa", "scb"], writes=["b%d" % (1 + hf)])
                    S.op("dve", C("tensor_tensor", out=t[:], in0=ps[:, 1:3, :].rearrange("p a b -> p (a b)"), in1=br[:], op=ALU.add), reads=["b1", "b2", "br"], writes=["modv"])
                    if scl != 1.0:
                        S.op("dve", C("tensor_scalar", out=t[:], in0=t[:], scalar1=scl, scalar2=None, op0=ALU.mult), reads=["modv"], writes=["modv"])
                S.op("sp", C("dma_start", out=ngt[:], in_=p["ng"][:]), writes=["modv"], dsem="c_ng")
                out["ng"] = ngt
                _barrier(S)
            return out

        def make_A(scope, sct, ngt, which):
            A = scope.enter_context(nc.sbuf_tensor(nm("A"), [128, 8], F32))
            S.op("dve", C("scalar_tensor_tensor", out=A[:], in0=sct[:], scalar=1.0, in1=ngt[:, which * 8:(which + 1) * 8], op0=ALU.add, op1=ALU.mult), reads=["modv"], writes=["modv"])
            return A

        def norm_T(tiles, xs, A, B, hT):
            ss, rs, xn = tiles
            for t in range(2):
                S.op("act", C("activation", out=xn[:, t, :], in_=xs[:, t, :], func=AF.Square, accum_out=ss[:, t:t + 1]), reads=["xs"], writes=["xn", "ss"])
            S.op("act", C("activation", out=rs[:], in_=ss[:], func=AF.Ln, scale=1.0 / D, bias=epsb[:]), reads=["ss", "epsb"], writes=["rs"])
            S.op("act", C("activation", out=rs[:], in_=rs[:], func=AF.Exp, scale=-0.5), reads=["rs"], writes=["rs"])
            for t in range(2):
                S.op("dve", C("tensor_scalar", out=xn[:, t, :], in0=xs[:, t, :], scalar1=rs[:, t:t + 1], scalar2=None, op0=ALU.mult), reads=["xs", "rs"], writes=["xn"])
            psT = ps[:, 6:8, :].rearrange("p a b -> p (a b)").bitcast(BF16)
            for k in range(8):
                for t in range(2):
                    S.op("pe", C("transpose", out=psT[:, k * 256 + t * 128:k * 256 + (t + 1) * 128], in_=xn[:, t, k * 128:(k + 1) * 128], identity=ident[:]),
                         reads=["xn", "ident"], writes=["b6", "b7"])
            for k in range(8):
                if k % 2 == 0:
                    S.op("dve", C("tensor_scalar", out=hT[:, k, :], in0=psT[:, k * 256:(k + 1) * 256], scalar1=A[:, k:k + 1], scalar2=B[:, k:k + 1], op0=ALU.mult, op1=ALU.add),
                         reads=["b6", "b7", "modv"], writes=["hT"])
                else:
                    S.op("act", C("activation", out=hT[:, k, :], in_=psT[:, k * 256:(k + 1) * 256], func=AF.Identity, scale=A[:, k:k + 1], bias=B[:, k:k + 1]),
                         reads=["b6", "b7", "modv"], writes=["hT"])

        def ffn_phase(src, dst, A, B, G, w1, w3, w2):
            with ExitStack() as ph:
                sbp = lambda name, shape, dt=F32: ph.enter_context(nc.sbuf_tensor(nm(name), list(shape), dt))
                w1s = sbp("w1s", [128, 8, DFF], BF16); w3s = sbp("w3s", [128, 8, DFF], BF16); w2s = sbp("w2s", [128, NFC, D], BF16)
                xs = sbp("xs", [128, 2, D]); xn = sbp("xn", [128, 2, D], BF16)
                ss = sbp("ss", [128, 2]); rs = sbp("rs", [128, 2])
                hT = sbp("hT", [128, 8, 256], BF16)
                s1 = [sbp("s1_%d" % i, [128, 256]) for i in range(2)]
                g = [sbp("g_%d" % i, [128, 256], BF16) for i in range(2)]
                tmp = sbp("tmp", [128, D])
                hw = DFF // 2
                for k in range(8):
                    for hf in range(2):
                        S.op("pool", C("dma_start", out=w1s[:, k, hf * hw:(hf + 1) * hw], in_=w1[k * 128:(k + 1) * 128, hf * hw:(hf + 1) * hw]), writes=["w1s"], dsem="w1s")
                        S.op("pool", C("dma_start", out=w3s[:, k, hf * hw:(hf + 1) * hw], in_=w3[k * 128:(k + 1) * 128, hf * hw:(hf + 1) * hw]), writes=["w3s"], dsem="w3s")
                for fc in range(NFC):
                    S.op("pool", C("dma_start", out=w2s[:, fc, :], in_=w2[fc * 128:(fc + 1) * 128, :]), writes=["w2s"], dsem="w2s")
                for hs in range(NT // 256):
                    r0 = hs * 256
                    S.op("sp", C("dma_start", out=xs[:], in_=src[r0:r0 + 256, :].rearrange("(t p) d -> p t d", p=128)), writes=["xs"], dsem="xs")
                    norm_T((ss, rs, xn), xs, A, B, hT)

                    def ymm(fc):
                        gb = g[fc % 2]
                        for t in range(2):
                            for hf in range(2):
                                S.op("pe", C("matmul", ps[:, t * 2 + hf, :], lhsT=gb[:, t * 128:(t + 1) * 128], rhs=w2s[:, fc, hf * 512:(hf + 1) * 512],
                                             start=(fc == 0), stop=(fc == NFC - 1)), reads=["g%d" % (fc % 2), "w2s"], writes=["b%d" % (t * 2 + hf)])
                    for fc in range(NFC):
                        ub = 4 + 2 * (fc % 2)
                        for (wsb, wn, bank) in ((w1s, "w1s", ub), (w3s, "w3s", ub + 1)):
                            for k in range(8):
                                S.op("pe", C("matmul", ps[:, bank, 0:256], lhsT=wsb[:, k, fc * 128:(fc + 1) * 128], rhs=hT[:, k, :], start=(k == 0), stop=(k == 7)),
                                     reads=[wn, "hT"], writes=["b%d" % bank])
                        if fc > 0:
                            ymm(fc - 1)
                        S.op("act", C("activation", out=s1[fc % 2][:], in_=ps[:, ub, 0:256], func=AF.Silu), reads=["b%d" % ub], writes=["s1_%d" % (fc % 2)])
                        S.op("dve", C("tensor_tensor", out=g[fc % 2][:], in0=ps[:, ub + 1, 0:256], in1=s1[fc % 2][:], op=ALU.mult),
                             reads=["b%d" % (ub + 1), "s1_%d" % (fc % 2)], writes=["g%d" % (fc % 2)])
                    ymm(NFC - 1)
                    for t in range(2):
                        S.op("dve", C("tensor_tensor", out=tmp[:], in0=ps[:, 2 * t:2 * t + 2, :].rearrange("p a b -> p (a b)"), in1=G[:], op=ALU.mult),
                             reads=["b%d" % (2 * t), "b%d" % (2 * t + 1), "modv"], writes=["tmp"])
                        S.op("pool", C("tensor_tensor", out=xs[:, t, :], in0=tmp[:], in1=xs[:, t, :], op=ALU.add), reads=["tmp", "xs"], writes=["xs"])
                    S.op("sp", C("dma_start", out=dst[r0:r0 + 256, :].rearrange("(t p) d -> p t d", p=128), in_=xs[:]), reads=["xs"], writes=["xdst"], dsem="xo")
                _barrier(S)

        def lt_mix(l, src, dst):
            p = L[l]
            with ExitStack() as mixscope:
                mm = mod_phase(p, [(6, "sh3"), (7, "sc3")], [(5, "g2", 1.0), (8, "g3", 0.5)], mixscope)
                A3 = make_A(mixscope, mm["sc3"], mm["ng"], 2)
                with ExitStack() as ph:
                    sbp = lambda name, shape, dt=F32: ph.enter_context(nc.sbuf_tensor(nm(name), list(shape), dt))
                    wos = sbp("wos", [128, 8, D], BF16)
                    og = sbp("og", [128, D])
                    ot65 = sbp("ot65", [65, 16, 128])
                    oa = sbp("oa", [128, 16, 65]); sq = sbp("osq", [128, 16, 64]); oss = sbp("oss", [128, 16]); z2 = sbp("z2", [128, 16])
                    on = sbp("on", [128, D], BF16); on32 = sbp("on32", [128, D]); oT = sbp("oT", [128, 8, 128], BF16)
                    xs = sbp("mxs", [128, D]); tmp = sbp("mtmp", [128, D]); xo = sbp("mxo", [128, D])
                    for k in range(8):
                        S.op("pool", C("dma_start", out=wos[:, k, :], in_=p["w_out"][k * 128:(k + 1) * 128, :]), writes=["wos"], dsem="wos")
                    S.op("sp", C("dma_start", out=og[:], in_=p["outg"][:, :].partition_broadcast(128)), writes=["og"], dsem="og")
                    for tt in range(NT // 128):
                        r0 = tt * 128
                        S.op("sp", C("dma_start", out=ot65[:], in_=OT[:, :, r0:r0 + 128].rearrange("h r n -> r h n")), writes=["ot65"], dsem="ot65")
                        S.op("sp", C("dma_start", out=xs[:], in_=src[r0:r0 + 128, :]), writes=["mxs"], dsem="mxs")
                        for h in range(16):
                            bank = 2 + h // 7; off = (h % 7) * 65
                            S.op("pe", C("transpose", out=ps[:, bank, off:off + 65], in_=ot65[:, h, :], identity=ident32[0:65, 0:65]), reads=["ot65", "ident32"], writes=["b%d" % bank])
                        for (bank, h0, nh_) in ((2, 0, 7), (3, 7, 7), (4, 14, 2)):
                            S.op("act", C("activation", out=oa[:, h0:h0 + nh_, :].rearrange("p h d -> p (h d)"), in_=ps[:, bank, 0:nh_ * 65], func=AF.Copy),
                                 reads=["b%d" % bank], writes=["oa"])
                        S.op("pool", C("tensor_tensor", out=sq[:], in0=oa[:, :, 0:64], in1=oa[:, :, 0:64], op=ALU.mult), reads=["oa"], writes=["osq"])
                        S.op("dve", C("tensor_reduce", out=oss[:], in_=sq[:, :, :], axis=AX.X, op=ALU.add, opt_input=False, opt_output=False), reads=["osq"], writes=["oss"])
                        S.op("dve", C("scalar_tensor_tensor", out=z2[:], in0=oa[:, :, 64], scalar=EPS, in1=oa[:, :, 64], op0=ALU.mult, op1=ALU.mult), reads=["oa"], writes=["z2"])
                        S.op("dve", C("scalar_tensor_tensor", out=oss[:], in0=oss[:], scalar=1.0 / 64, in1=z2[:], op0=ALU.mult, op1=ALU.add), reads=["oss", "z2"], writes=["oss"])
                        S.op("act", C("activation", out=oss[:], in_=oss[:], func=AF.Ln), reads=["oss"], writes=["oss"])
                        S.op("act", C("activation", out=oss[:], in_=oss[:], func=AF.Exp, scale=-0.5), reads=["oss"], writes=["oss"])
                        S.op("dve", C("tensor_tensor", out=on32[:].rearrange("p (h d) -> p h d", d=64), in0=oa[:, :, 0:64],
                                      in1=oss[:].unsqueeze(2).to_broadcast([128, 16, 64]), op=ALU.mult), reads=["oa", "oss"], writes=["on32"])
                        S.op("pool", C("tensor_tensor", out=on[:], in0=on32[:], in1=og[:], op=ALU.mult), reads=["on32", "og"], writes=["on"])
                        psT = ps[:, 6, :].bitcast(BF16)
                        for k in range(8):
                            S.op("pe", C("transpose", out=psT[:, k * 128:(k + 1) * 128], in_=on[:, k * 128:(k + 1) * 128], identity=ident[:]), reads=["on", "ident"], writes=["b6"])
                        S.op("act", C("activation", out=oT[:].rearrange("p k t -> p (k t)"), in_=psT[:, 0:1024], func=AF.Copy), reads=["b6"], writes=["oT"])
                        for hf in range(2):
                            for k in range(8):
                                S.op("pe", C("matmul", ps[:, hf, :], lhsT=oT[:, k, :], rhs=wos[:, k, hf * 512:(hf + 1) * 512], start=(k == 0), stop=(k == 7)),
                                     reads=["oT", "wos"], writes=["b%d" % hf])
                        S.op("dve", C("tensor_tensor", out=tmp[:], in0=ps[:, 0:2, :].rearrange("p a b -> p (a b)"), in1=mm["g2"][:], op=ALU.mult),
                             reads=["b0", "b1", "modv"], writes=["mtmp"])
                        S.op("pool", C("tensor_tensor", out=xo[:], in0=tmp[:], in1=xs[:], op=ALU.add), reads=["mtmp", "mxs"], writes=["mxo"])
                        S.op("sp", C("dma_start", out=xC[r0:r0 + 128, :], in_=xo[:]), reads=["mxo"], writes=["xdst"], dsem="mxo")
                    _barrier(S)
                ffn_phase(xC, dst, A3, mm["sh3"], mm["g3"], p["w1_1"], p["w3_1"], p["w2_1"])

        def lt_pre(l, src, dst):
            p = L[l]
            with ExitStack() as prescope:
                pm = mod_phase(p, [(0, "sh1"), (1, "sc1"), (3, "sh2"), (4, "sc2")], [(2, "g1", 0.5)], prescope)
                A1 = make_A(prescope, pm["sc1"], pm["ng"], 0)
                A2 = make_A(prescope, pm["sc2"], pm["ng"], 1)
                ffn_phase(src, dst, A1, pm["sh1"], pm["g1"], p["w1_0"], p["w3_0"], p["w2_0"])
                with ExitStack() as ph:
                    sbp = lambda name, shape, dt=F32: ph.enter_context(nc.sbuf_tensor(nm(name), list(shape), dt))
                    wis = sbp("wis", [128, 8, DIN], BF16)
                    qg = sbp("qg", [128, 1024])
                    xs = sbp("pxs", [128, 2, D]); xn = sbp("pxn", [128, 2, D], BF16)
                    ss = sbp("pss", [128, 2]); rs = sbp("prs", [128, 2])
                    hT = sbp("phT", [128, 8, 256], BF16)
                    pr = [sbp("pr%d" % i, [128, DIN]) for i in range(2)]
                    prb = [sbp("prb%d" % i, [128, NPB], BF16) for i in range(2)]
                    vaug = [sbp("vaug%d" % i, [128, 16, 65], BF16) for i in range(2)]
                    fT = sbp("fT", [128, NFB, 256], BF16)
                    sq = sbp("psq", [128, 1024]); hs_ = sbp("phs", [128, 16])
                    cst = sbp("cst", [128, 2, 16])
                    r1 = sbp("r1", [128, 25, 8]); r2 = sbp("r2", [128, 25, 8]); r3 = sbp("r3", [128, 25, 8]); r4 = sbp("r4", [128, 25, 8])
                    cw = 1828
                    for k in range(8):
                        for hf in range(2):
                            S.op("pool", C("dma_start", out=wis[:, k, hf * cw:(hf + 1) * cw], in_=p["w_in"][k * 128:(k + 1) * 128, hf * cw:(hf + 1) * cw]), writes=["wis"], dsem="wis")
                    S.op("sp", C("dma_start", out=qg[:], in_=p["qkg"][:, :].partition_broadcast(128)), writes=["qg"], dsem="qg")
                    for t in range(2):
                        S.op("pool", C("memset", vaug[t][:], 1.0), writes=["vaug%d" % t])
                    S.op("pool", C("memset", fT[:], 0.0), writes=["fT"])
                    for hs in range(NT // 256):
                        r0 = hs * 256
                        S.op("sp", C("dma_start", out=xs[:], in_=dst[r0:r0 + 256, :].rearrange("(t p) d -> p t d", p=128)), writes=["xs"], dsem="pxs")
                        S.op("sp", C("dma_start", out=cst[:], in_=cs[r0:r0 + 256, :].rearrange("(t p) d -> p t d", p=128)), writes=["cst"], dsem="cst")
                        norm_T((ss, rs, xn), xs, A2, pm["sh2"], hT)
                        for t in range(2):
                            prt = pr[t]; prbt = prb[t]; pn = "pr%d" % t; pbn = "prb%d" % t
                            ngrp = (DIN + 511) // 512
                            for gi in range(ngrp):
                                c0 = gi * 512; c1 = min(DIN, c0 + 512)
                                bank = gi % 6 if gi < 6 else gi - 6
                                for k in range(8):
                                    S.op("pe", C("matmul", ps[:, bank, 0:c1 - c0], lhsT=hT[:, k, t * 128:(t + 1) * 128], rhs=wis[:, k, c0:c1], start=(k == 0), stop=(k == 7)),
                                         reads=["hT", "wis"], writes=["b%d" % bank])
                                S.op("act", C("activation", out=prt[:, c0:c1], in_=ps[:, bank, 0:c1 - c0], func=AF.Copy), reads=["b%d" % bank], writes=[pn])
                            S.op("pool", C("tensor_tensor", out=sq[:], in0=prt[:, 0:1024], in1=prt[:, 0:1024], op=ALU.mult), reads=[pn], writes=["psq"])
                            S.op("dve", C("tensor_reduce", out=hs_[:], in_=sq[:].rearrange("p (h d) -> p h d", d=64), axis=AX.X, op=ALU.add), reads=["psq"], writes=["phs"])
                            S.op("act", C("activation", out=hs_[:], in_=hs_[:], func=AF.Ln, scale=1.0 / 64, bias=epsb[:]), reads=["phs", "epsb"], writes=["phs"])
                            S.op("act", C("activation", out=hs_[:], in_=hs_[:], func=AF.Exp, scale=-0.5), reads=["phs"], writes=["phs"])
                            S.op("dve", C("tensor_tensor", out=prt[:, 0:1024].rearrange("p (h d) -> p h d", d=64), in0=prt[:, 0:1024].rearrange("p (h d) -> p h d", d=64),
                                          in1=hs_[:].unsqueeze(2).to_broadcast([128, 16, 64]), op=ALU.mult), reads=[pn, "phs"], writes=[pn])
                            S.op("pool", C("tensor_tensor", out=prt[:, 0:1024], in0=prt[:, 0:1024], in1=qg[:], op=ALU.mult), reads=[pn, "qg"], writes=[pn])
                            hv = prt[:, 0:1600].rearrange("p (h d) -> p h d", d=64)
                            cosb = cst[:, t, 0:8].unsqueeze(1).to_broadcast([128, 25, 8]); sinb = cst[:, t, 8:16].unsqueeze(1).to_broadcast([128, 25, 8])
                            S.op("dve", C("tensor_tensor", out=r1[:], in0=hv[:, :, 0:8], in1=cosb, op=ALU.mult), reads=[pn, "cst"], writes=["r1"])
                            S.op("pool", C("tensor_tensor", out=r2[:], in0=hv[:, :, 8:16], in1=sinb, op=ALU.mult), reads=[pn, "cst"], writes=["r2"])
                            S.op("dve", C("tensor_tensor", out=r3[:], in0=hv[:, :, 8:16], in1=cosb, op=ALU.mult), reads=[pn, "cst"], writes=["r3"])
                            S.op("pool", C("tensor_tensor", out=r4[:], in0=hv[:, :, 0:8], in1=sinb, op=ALU.mult), reads=[pn, "cst"], writes=["r4"])
                            S.op("dve", C("tensor_tensor", out=hv[:, :, 0:8], in0=r1[:], in1=r2[:], op=ALU.subtract), reads=["r1", "r2"], writes=[pn])
                            S.op("pool", C("tensor_tensor", out=hv[:, :, 8:16], in0=r3[:], in1=r4[:], op=ALU.add), reads=["r3", "r4"], writes=[pn])
                            S.op("act", C("activation", out=prbt[:], in_=prt[:, 0:NPB], func=AF.Copy), reads=[pn], writes=[pbn])
                            S.op("pool", C("tensor_copy", out=vaug[t][:, :, 0:64], in_=prbt[:, 2624:3648].rearrange("p (h d) -> p h d", d=64)), reads=[pbn], writes=["vaug%d" % t])
                            S.op("sp", C("dma_start", out=Vscr[hs * 2 + t], in_=vaug[t][:].rearrange("p h d -> p (h d)")), reads=["vaug%d" % t], writes=["Vscr"], dsem="vaug%d" % t)
                            S.op("sp", C("dma_start", out=wiS[r0 + t * 128:r0 + (t + 1) * 128, :], in_=prt[:, NPB:DIN]), reads=[pn], writes=["wiS"], dsem="W%d" % t)
                        psT = ps[:, 6:8, :].rearrange("p a b -> p (a b)").bitcast(BF16)
                        for rnd in range(3):
                            b0 = rnd * 8; nb_ = min(8, NFB - b0)
                            for bi in range(nb_):
                                blk = b0 + bi
                                fw = 128 if blk < 20 else 64
                                for t in range(2):
                                    S.op("pe", C("transpose", out=psT[0:fw, bi * 256 + t * 128:bi * 256 + (t + 1) * 128], in_=prb[t][:, blk * 128:blk * 128 + fw], identity=ident[:]),
                                         reads=["prb%d" % t, "ident"], writes=["b6", "b7"])
                            if b0 + nb_ <= 20:
                                S.op("dve", C("tensor_copy", out=fT[:, b0:b0 + nb_, :].rearrange("p b n -> p (b n)"), in_=psT[:, 0:nb_ * 256]), reads=["b6", "b7"], writes=["fT"])
                            else:
                                S.op("dve", C("tensor_copy", out=fT[:, b0:b0 + nb_ - 1, :].rearrange("p b n -> p (b n)"), in_=psT[:, 0:(nb_ - 1) * 256]), reads=["b6", "b7"], writes=["fT"])
                                S.op("dve", C("tensor_copy", out=fT[0:64, NFB - 1, :], in_=psT[0:64, (nb_ - 1) * 256:nb_ * 256]), reads=["b6", "b7"], writes=["fT"])
                        S.op("sp", C("dma_start", out=FT.rearrange("(b p) n -> p b n", p=128)[:, :, r0:r0 + 256], in_=fT[:]), reads=["fT"], writes=["FT"], dsem="fT")
                    _barrier(S)

        def lb():
            with ExitStack() as lbs:
                sbl = lambda name, shape, dt=F32: lbs.enter_context(nc.sbuf_tensor(nm(name), list(shape), dt))
                mle = sbl("mle", [128, 4 * R, 512], BF16); mlt = sbl("mlt", [128, 4 * R, 512], BF16)
                S.op("sp", C("dma_start", out=mle[:], in_=mLE[:]), writes=["mle"], dsem="c_mle")
                S.op("sp", C("dma_start", out=mlt[:], in_=mLT[:]), writes=["mlt"], dsem="c_mlt")
                with ExitStack() as ph:
                    sbp = lambda name, shape, dt=F32: ph.enter_context(nc.sbuf_tensor(nm(name), list(shape), dt))
                    kis = sbp("kis", [128, SK // 2], BF16)
                    qis = sbp("qis", [128, 8, 512], BF16)
                    wis = sbp("wisb", [128, 4, 8])
                    cb = sbp("cb", [128, 4, R * 512])
                    pow2 = sbp("pow2", [128, NIT + 1])
                    accs = [sbp("acc%d" % i, [128, SK]) for i in range(2)]
                    nsel = sbp("nsel", [128, SK], BF16); junk = sbp("junkb", [128, SK], BF16)
                    rr = [sbp("rr%d" % i, [128, 512]) for i in range(2)]
                    stg = [sbp("stg%d" % i, [128, 4, 128], BF16) for i in range(2)]
                    mns = [sbp("mn%d" % i, [128, 16 * R]) for i in range(2)]
                    m8s = [sbp("m8_%d" % i, [128, 8]) for i in range(2)]
                    los = [sbp("lo%d" % i, [128, 1]) for i in range(2)]
                    w0 = sbp("w0", [128, 1]); W = sbp("W", [128, NIT + 1])
                    negmid = sbp("negmid", [128, 1]); ssum = sbp("ssum", [128, 1]); dl = sbp("dl", [128, 1]); thr = sbp("thr", [128, 1])
                    S.op("sp", C("dma_start", out=kis[0:64, :], in_=FT[1536:1600, 0:SK // 2]), reads=["FT"], writes=["kis"], dsem="kis")
                    S.op("sp", C("dma_start", out=kis[64:128, :], in_=FT[1536:1600, SK // 2:SK]), reads=["FT"], writes=["kis"], dsem="kis")
                    S.op("sp", C("dma_start", out=cb[:], in_=cbd[:]), writes=["cb"], dsem="cb")
                    S.op("sp", C("dma_start", out=pow2[:], in_=pow2d[:]), writes=["pow2"], dsem="pow2")
                    tcount = [0]

                    def gen_index(n):
                        j, qt = n // 4, n % 4
                        nkc = R * (j + 1)
                        Kmax = nkc * 512
                        acc = accs[n % 2]; an = "acc%d" % (n % 2); mn = mns[n % 2]; mnn = "mn%d" % (n % 2); m8 = m8s[n % 2]; lo = los[n % 2]; bn = "br%d" % (n % 2)
                        if qt == 0:
                            qsrc = FT[1024:1536, j * 512:(j + 1) * 512].rearrange("(h d) n -> d h n", d=64)
                            S.op("sp", C("dma_start", out=qis[0:64], in_=qsrc), reads=["FT"], writes=["qis"], dsem="qis")
                            S.op("sp", C("dma_start", out=qis[64:128], in_=qsrc), reads=["FT"], writes=["qis"], dsem="qis")
                            S.op("sp", C("dma_start", out=wis[:], in_=wiS[j * 512:(j + 1) * 512, :].rearrange("(t p) h -> p t h", p=128)), reads=["wiS"], writes=["wisb"], dsem="wisb")
                        for kc in range(nkc):
                            k0 = kc * 512
                            half = 0 if k0 < SK // 2 else 1
                            kcol = k0 - half * (SK // 2)
                            pb = half * 64
                            ab = 2 + (kc % 2)
                            for h in range(8):
                                lb_ = h % 2
                                S.op("pe", C("matmul", ps[:, lb_, :], lhsT=qis[pb:pb + 64, h, qt * 128:(qt + 1) * 128], rhs=kis[pb:pb + 64, kcol:kcol + 512], start=True, stop=True),
                                     reads=["qis", "kis"], writes=["b%d" % lb_])
                                S.op("act", C("activation", out=rr[h % 2][:], in_=ps[:, lb_, :], func=AF.Relu), reads=["b%d" % lb_], writes=["rr%d" % (h % 2)])
                                if h == 0:
                                    S.op("dve", C("tensor_scalar", out=ps[:, ab, :], in0=rr[0][:], scalar1=wis[:, qt, 0:1], scalar2=None, op0=ALU.mult),
                                         reads=["rr0", "wisb"], writes=["b%d" % ab])
                                else:
                                    S.op("dve", C("scalar_tensor_tensor", out=ps[:, ab, :], in0=rr[h % 2][:], scalar=wis[:, qt, h:h + 1], in1=ps[:, ab, :], op0=ALU.mult, op1=ALU.add),
                                         reads=["rr%d" % (h % 2), "wisb", "b%d" % ab], writes=["b%d" % ab])
                            S.op("dve", C("tensor_scalar", out=acc[:, k0:k0 + 512], in0=ps[:, ab, :], scalar1=1.0, scalar2=None, op0=ALU.mult, op1=ALU.min, accum_out=mn[:, kc:kc + 1]),
                                 reads=["b%d" % ab], writes=[an, mnn])
                        S.op("pool", C("tensor_tensor", out=acc[:, Kmax - R * 512:Kmax], in0=acc[:, Kmax - R * 512:Kmax], in1=cb[:, qt, :], op=ALU.add), reads=[an, "cb"], writes=[an])
                        S.op("dve", C("max", out=m8[:], in_=acc[:, 0:Kmax]), reads=[an], writes=[bn])
                        if nkc > 1:
                            S.op("dve", C("tensor_reduce", out=lo[:], in_=mn[:, 0:nkc], axis=AX.X, op=ALU.min), reads=[mnn], writes=[bn])
                            S.op("dve", C("tensor_scalar", out=lo[:], in0=lo[:], scalar1=-1.0, scalar2=None, op0=ALU.add), reads=[bn], writes=[bn])
                        else:
                            S.op("dve", C("tensor_scalar", out=lo[:], in0=mn[:, 0:1], scalar1=-1.0, scalar2=None, op0=ALU.add), reads=[mnn], writes=[bn])

                    def gen_select(n):
                        j, qt = n // 4, n % 4
                        nkc = R * (j + 1)
                        Kmax = nkc * 512
                        acc = accs[n % 2]; an = "acc%d" % (n % 2); m8 = m8s[n % 2]; lo = los[n % 2]; bn = "br%d" % (n % 2)
                        S.op("dve", C("tensor_tensor", out=w0[:], in0=m8[:, 0:1], in1=lo[:], op=ALU.subtract), reads=[bn], writes=["w0"])
                        S.op("dve", C("tensor_scalar", out=W[:], in0=pow2[:], scalar1=w0[:, 0:1], scalar2=None, op0=ALU.mult), reads=["pow2", "w0"], writes=["W"])
                        S.op("dve", C("scalar_tensor_tensor", out=negmid[:], in0=lo[:], scalar=-1.0, in1=W[:, 0:1], op0=ALU.mult, op1=ALU.subtract), reads=[bn, "W"], writes=["negmid"])
                        for it in range(NIT):
                            S.op("act", C("activation", out=junk[:, 0:Kmax], in_=acc[:, 0:Kmax], func=AF.Sign, bias=negmid[:, 0:1], scale=1.0, accum_out=ssum[:, 0:1]),
                                 reads=[an, "negmid"], writes=["junk", "ssum"])
                            S.op("dve", C("scalar_tensor_tensor", out=dl[:], in0=ssum[:], scalar=float(2 * TOPK - Kmax), in1=W[:, it:it + 1], op0=ALU.is_ge, op1=ALU.mult), reads=["ssum", "W"], writes=["dl"])
                            S.op("dve", C("scalar_tensor_tensor", out=negmid[:], in0=negmid[:], scalar=W[:, it + 1:it + 2], in1=dl[:], op0=ALU.add, op1=ALU.subtract),
                                 reads=["negmid", "W", "dl"], writes=["negmid"])
                        S.op("dve", C("scalar_tensor_tensor", out=thr[:], in0=negmid[:], scalar=-1.0, in1=W[:, NIT:NIT + 1], op0=ALU.mult, op1=ALU.subtract), reads=["negmid", "W"], writes=["thr"])
                        S.op("dve", C("tensor_scalar", out=nsel[:, 0:Kmax], in0=acc[:, 0:Kmax], scalar1=thr[:, 0:1], scalar2=None, op0=ALU.is_le), reads=[an, "thr"], writes=["nsel"])
                        for g4 in range(nkc):
                            tb = 4 + (tcount[0] % 2)
                            sg = stg[tcount[0] % 2]; sgn = "stg%d" % (tcount[0] % 2)
                            tcount[0] += 1
                            pT = ps[:, tb, :].bitcast(BF16)
                            for u in range(4):
                                kt = g4 * 4 + u
                                S.op("pe", C("transpose", out=pT[:, u * 128:(u + 1) * 128], in_=nsel[:, kt * 128:(kt + 1) * 128], identity=ident[:]), reads=["nsel", "ident"], writes=["b%d" % tb])
                            S.op("dve", C("tensor_copy", out=sg[:].rearrange("p a b -> p (a b)"), in_=pT[:, 0:512]), reads=["b%d" % tb], writes=[sgn])
                            S.op("sp", C("dma_start", out=nsT[j, g4 * 4:(g4 + 1) * 4, :, qt * 128:(qt + 1) * 128].rearrange("a p q -> p a q"), in_=sg[:]), reads=[sgn], writes=["nsT"], dsem=sgn)

                    NQ = 4 * NCH
                    gen_index(0)
                    for n in range(NQ):
                        S.begin_defer(); gen_select(n); Y = S.end_defer()
                        X = []
                        if n + 1 < NQ:
                            S.begin_defer(); gen_index(n + 1); X = S.end_defer()
                        S.replay(X, Y)
                    _barrier(S)

                def sweep_phase(kind):
                    nh = {"A": 4, "B2": 4, "C": 8}[kind]
                    KR = 96 if kind == "A" else 64
                    hbase = {"A": 0, "B2": 4, "C": 8}[kind]
                    qrow = {"A": 0, "B2": 512, "C": 1600}[kind]
                    krow = {"A": 256, "B2": 768, "C": 2112}[kind]
                    with ExitStack() as ph:
                        sbp = lambda name, shape, dt=F32: ph.enter_context(nc.sbuf_tensor(nm(name), list(shape), dt))
                        kT = [sbp("kT%d" % i, [KR, SK], BF16) for i in range(2)]
                        qS = [sbp("qS%d" % i, [KR, 512], BF16) for i in range(3)]
                        VG = sbp("VG", [128, NKT, 4, 65], BF16)
                        Pt = [sbp("P%d" % i, [128, 512], BF16) for i in range(3)]
                        osb = [sbp("osb%d" % i, [65, 512]) for i in range(2)]
                        if kind == "A":
                            pbs = sbp("pbs", [128, NCH, 4, 32]); pvs = sbp("pvs", [128, NCH, 4, 32]); fus = sbp("fus", [128, NCH, 4, 32])
                            km32 = sbp("km32", [64, 32]); kmT = sbp("kmT", [64, 32]); q32 = sbp("q32", [64, 512])
                            gm = sbp("gm", [128, 4, 32]); m8 = sbp("m8a", [128, 4, 8]); nM = sbp("nM", [128, 4, 32]); nMp = sbp("nMp", [128, 4, 96], BF16)
                            S.op("sp", C("dma_start", out=pbs[:], in_=pbd[:]), writes=["pbs"], dsem="pbs")
                            S.op("sp", C("dma_start", out=pvs[:], in_=pvd[:]), writes=["pvs"], dsem="pvs")
                            S.op("sp", C("dma_start", out=fus[:], in_=fud[:]), writes=["fus"], dsem="fus")
                            S.op("pool", C("memset", nMp[:], 0.0), writes=["nMp"])
                            S.op("pool", C("memset", km32[:], 0.0), writes=["km32"])
                        if kind == "C":
                            e32 = [sbp("e32_%d" % i, [128, 512]) for i in range(3)]
                            sp_ = [sbp("sp%d" % i, [128, 512], BF16) for i in range(3)]
                            spa = [sbp("spacc%d" % i, [128, 512], BF16) for i in range(3)]
                        if kind == "B2":
                            nst = [sbp("nst%d" % i, [128, 512], BF16) for i in range(3)]
                        ocount = 0
                        qcount = 0
                        for h in range(nh):
                            hb = h % 2
                            hh = h % 4
                            kTh = kT[hb]
                            kn = "kT%d" % hb
                            if hh == 0:
                                hg = (hbase + h)
                                nq = 4
                                for qd in range(nq):
                                    t0_ = qd * (NKT // nq); t1_ = (qd + 1) * (NKT // nq)
                                    S.op("sp", C("dma_start", out=VG[:, t0_:t1_].rearrange("p t h d -> p t (h d)"), in_=Vscr[t0_:t1_, :, hg * 65:(hg + 4) * 65].rearrange("t p f -> p t f")),
                                         reads=["Vscr"], writes=["VG"], dsem="VG")
                            S.op("sp", C("dma_start", out=kTh[0:64, :], in_=FT[krow + h * 64:krow + (h + 1) * 64, 0:SK]), reads=["FT"], writes=[kn], dsem=kn)
                            if kind == "A":
                                S.op("sp", C("dma_start", out=kTh[64:96, :], in_=ohd[:]), writes=[kn], dsem=kn)
                                S.op("dve", C("tensor_reduce", out=km32[:, 0:NB], in_=kTh[0:64, :].rearrange("p (b k) -> p b k", k=256), axis=AX.X, op=ALU.add), reads=[kn], writes=["km32"])
                                S.op("dve", C("tensor_scalar", out=kmT[:], in0=km32[:], scalar1=1.0 / 256, scalar2=None, op0=ALU.mult), reads=["km32"], writes=["kmT"])
                            for j in range(NCH):
                                nkt = 4 * R * (j + 1)
                                d0 = 4 * R * j
                                qsb = qS[qcount % 3]; qn = "qS%d" % (qcount % 3)
                                qcount += 1
                                S.op("sp", C("dma_start", out=qsb[0:64, :], in_=FT[qrow + h * 64:qrow + (h + 1) * 64, j * 512:(j + 1) * 512]), reads=["FT"], writes=[qn], dsem=qn)
                                qs = qsb[:, :]
                                ob = 6 + (ocount % 2); obn = "b%d" % ob
                                osbt = osb[ocount % 2]; osn = "osb%d" % (ocount % 2)
                                ocount += 1
                                if kind == "A":
                                    S.op("pool", C("tensor_copy", out=q32[:], in_=qsb[0:64, :]), reads=[qn], writes=["q32"])
                                    for qt in range(4):
                                        S.op("pe", C("matmul", ps[:, 5, qt * 32:(qt + 1) * 32], lhsT=q32[:, qt * 128:(qt + 1) * 128], rhs=kmT[:], start=True, stop=True),
                                             reads=["q32", "kmT"], writes=["b5"])
                                    S.op("dve", C("tensor_tensor", out=gm[:].rearrange("p a b -> p (a b)"), in0=ps[:, 5, 0:128], in1=pbs[:, j].rearrange("p a b -> p (a b)"), op=ALU.add),
                                         reads=["b5", "pbs"], writes=["gm"])
                                    for qt in range(4):
                                        S.op("dve", C("max", out=m8[:, qt, :], in_=gm[:, qt, :]), reads=["gm"], writes=["m8a"])
                                    for qt in range(4):
                                        S.op("dve", C("tensor_scalar", out=nM[:, qt, :], in0=gm[:, qt, :], scalar1=m8[:, qt, 2:3], scalar2=None, op0=ALU.is_lt), reads=["gm", "m8a"], writes=["nM"])
                                    S.op("dve", C("tensor_tensor", out=nM[:], in0=nM[:], in1=pvs[:, j], op=ALU.mult), reads=["nM", "pvs"], writes=["nM"])
                                    S.op("dve", C("tensor_tensor", out=nMp[:, :, 64:96], in0=nM[:], in1=fus[:, j], op=ALU.add), reads=["nM", "fus"], writes=["nMp"])
                                    for qt in range(4):
                                        S.op("pe", C("matmul", ps[0:96, 5, qt * 128:(qt + 1) * 128], lhsT=nMp[:, qt, :], rhs=ident[:], start=True, stop=True), reads=["nMp", "ident"], writes=["b5"])
                                    S.op("act", C("activation", out=qsb[64:96, :], in_=ps[64:96, 5, :], func=AF.Copy), reads=["b5"], writes=[qn])
                                if kind == "C":
                                    S.op("pool", C("memset", spa[0][:], 0.0), writes=["spacc0"])
                                order = list(range(nkt)) if kind != "C" else list(range(nkt - 1, -1, -1))
                                n = len(order)

                                def st0(i):
                                    kt = order[i]
                                    sbk = i % 2; sbn = "b%d" % sbk
                                    diag = kt >= d0
                                    u = kt - d0
                                    if kind == "B2":
                                        nb_ = nst[i % 3]; nbn = "nst%d" % (i % 3)
                                        S.op("sp", C("dma_start", out=nb_[:], in_=nsT[j, kt]), reads=["nsT"], writes=[nbn], dsem=nbn)
                                    S.op("pe", C("matmul", ps[:, sbk, :], lhsT=kTh[:, kt * 128:(kt + 1) * 128], rhs=qs, start=True, stop=(kind != "B2")), reads=[kn, qn], writes=[sbn])
                                    if kind == "B2":
                                        S.op("pe", C("matmul", ps[:, sbk, :], lhsT=nbigI[:], rhs=nb_[:], start=False, stop=True), reads=["nbigI", nbn], writes=[sbn])
                                    if kind in ("A", "B2"):
                                        pt = Pt[i % 3]; ptn = "P%d" % (i % 3)
                                        S.op("act", C("activation", out=pt[:], in_=ps[:, sbk, :], func=AF.Exp, scale=0.125), reads=[sbn], writes=[ptn])
                                        if diag and kind == "A":
                                            S.op("pool", C("tensor_tensor", out=pt[:], in0=pt[:], in1=mle[:, u, :], op=ALU.mult), reads=[ptn, "mle"], writes=[ptn])
                                    else:
                                        eb = e32[i % 3]; ebn = "e32_%d" % (i % 3)
                                        spb = sp_[i % 3]; spn = "sp%d" % (i % 3)
                                        S.op("act", C("activation", out=eb[:], in_=ps[:, sbk, :], func=AF.Exp, scale=0.125), reads=[sbn], writes=[ebn])
                                        if diag:
                                            S.op("pool", C("tensor_tensor", out=eb[:], in0=eb[:], in1=mlt[:, u, :], op=ALU.mult), reads=[ebn, "mlt"], writes=[ebn])
                                        S.op("act", C("activation", out=spb[:], in_=eb[:], func=AF.Ln, bias=1.0), reads=[ebn], writes=[spn])
                                        S.op("pool", C("tensor_tensor", out=spa[(i + 1) % 3][:], in0=spa[i % 3][:], in1=spb[:], op=ALU.add),
                                             reads=["spacc%d" % (i % 3), spn], writes=["spacc%d" % ((i + 1) % 3)])

                                def st1(i):
                                    kt = order[i]
                                    diag = kt >= d0
                                    u = kt - d0
                                    wbk = 2 + (i % 2); wbn = "b%d" % wbk
                                    xbk = 4 + (i % 2); xbn = "b%d" % xbk
                                    spb = sp_[i % 3]; spn = "sp%d" % (i % 3)
                                    eb = e32[i % 3]; ebn = "e32_%d" % (i % 3)
                                    pt = Pt[i % 3]; ptn = "P%d" % (i % 3)
                                    S.op("pe", C("matmul", ps[:, wbk, :], lhsT=uinc[:], rhs=spb[:], start=True, stop=(i == 0)), reads=["uinc", spn], writes=[wbn])
                                    if i > 0:
                                        S.op("pe", C("matmul", ps[:, wbk, :], lhsT=ones[:], rhs=spa[i % 3][:], start=False, stop=True), reads=["ones", "spacc%d" % (i % 3)], writes=[wbn])
                                    S.op("act", C("activation", out=ps[:, xbk, :], in_=ps[:, wbk, :], func=AF.Exp, scale=-1.0), reads=[wbn], writes=[xbn])
                                    S.op("dve", C("tensor_tensor", out=pt[:], in0=ps[:, xbk, :], in1=eb[:], op=ALU.mult), reads=[xbn, ebn], writes=[ptn])

                                def st2(i):
                                    kt = order[i]
                                    pt = Pt[i % 3]; ptn = "P%d" % (i % 3)
                                    S.op("pe", C("matmul", ps[0:65, ob, :], lhsT=VG[:, kt, hh, :], rhs=pt[:], start=(i == 0), stop=(i == n - 1)), reads=["VG", ptn], writes=[obn])

                                if kind == "C":
                                    for t in range(n + 2):
                                        if t < n:
                                            st0(t)
                                        if 1 <= t <= n:
                                            st1(t - 1)
                                        if t >= 2:
                                            st2(t - 2)
                                else:
                                    for t in range(n + 1):
                                        if t < n:
                                            st0(t)
                                        if t >= 1:
                                            st2(t - 1)
                                S.op("dve", C("tensor_copy", out=osbt[:], in_=ps[0:65, ob, :]), reads=[obn], writes=[osn])
                                if kind == "C":
                                    S.op("dve", C("memset", osbt[64:65, :], 1.0), reads=[osn], writes=[osn])
                                S.op("sp", C("dma_start", out=OT[hbase + h, :, j * 512:(j + 1) * 512], in_=osbt[:]), reads=[osn], writes=["OT"], dsem=osn)
                        _barrier(S)

                for kind in ("A", "C", "B2"):
                    sweep_phase(kind)

        cur = x_in
        for l in range(DEPTH):
            if l > 0:
                lt_mix(l - 1, cur, xA)
                cur = xA
            lt_pre(l, cur, xB)
            cur = xB
            lb()
        lt_mix(DEPTH - 1, cur, y_out)
        S.finish_wait("sp", ["xdst"])
        _barrier(S)
        S.emit()
    return nc

BF = ml_dtypes.bfloat16
NIT = 20
BIGA = 240000.0


def own_idx(r, NS):
    return np.concatenate([(2 * j + r) * 512 + np.arange(512) for j in range(NS)])


def lb_consts(r, NS):
    p = np.arange(128)
    d = {}
    u = np.arange(8); f = np.arange(512)
    kk = u[None, :, None] * 128 + p[:, None, None]
    qq = r * 512 + f[None, None, :]
    d["mLE"] = (kk <= qq).astype(BF)
    d["mLT"] = (kk < qq).astype(BF)
    col = np.arange(1024)
    qpos = r * 512 + np.arange(4)[None, :, None] * 128 + p[:, None, None]
    d["cb"] = np.where(col[None, None, :] <= qpos, 0.0, -1e30).astype(np.float32)
    j = np.arange(NS)[None, :, None, None]; qt = np.arange(4)[None, None, :, None]; blk = np.arange(32)[None, None, None, :]
    cur = 4 * j + 2 * r + qt // 2 + 0 * p[:, None, None, None]
    d["pastbias"] = np.where(blk < cur, 0.0, -1e30).astype(np.float32)
    d["pastvalid"] = (blk < cur).astype(np.float32)
    d["future"] = (blk > cur).astype(np.float32)
    d["pow2"] = np.tile((0.5 ** (np.arange(NIT) + 1)).astype(np.float32)[None, :], (128, 1))
    d["ident"] = np.eye(128, dtype=np.float32)
    d["uinc"] = (p[:, None] >= p[None, :]).astype(np.float32)
    return d


def lb_kside(P, NS):
    SK = NS * 1024
    NB = SK // 256
    d = {}
    ka = np.zeros((4, 96, SK), dtype=BF)
    ka[:, 0:64, :] = P[:, 256:512].reshape(SK, 4, 64).transpose(1, 2, 0)
    for b in range(NB):
        ka[:, 64 + b, b * 256:(b + 1) * 256] = BF(-BIGA)
    d["kaT"] = ka
    d["kbT"] = np.ascontiguousarray(P[:, 768:1024].reshape(SK, 4, 64).transpose(1, 2, 0))
    kiT = P[:, 1536:1600].T
    d["ki2"] = np.ascontiguousarray(np.concatenate([kiT[:, :SK // 2], kiT[:, SK // 2:]], 0))
    d["kcT"] = np.ascontiguousarray(P[:, 2112:2624].reshape(SK, 8, 64).transpose(1, 2, 0))
    one = np.ones((SK, 4, 1), dtype=BF)
    def vl(v):
        nh = v.shape[1]
        return np.ascontiguousarray(v.reshape(SK // 128, 128, nh, 65).transpose(2, 1, 0, 3))
    d["va"] = vl(np.concatenate([P[:, 2624:2880].reshape(SK, 4, 64), one], 2))
    d["vb"] = vl(np.concatenate([P[:, 2880:3136].reshape(SK, 4, 64), one], 2))
    d["vc"] = vl(np.concatenate([P[:, 3136:3648].reshape(SK, 8, 64), np.ones((SK, 8, 1), dtype=BF)], 2))
    return d


def lb_qside(Pown, wiown):
    NT = Pown.shape[0]
    d = {}
    d["qaT"] = np.ascontiguousarray(Pown[:, 0:256].reshape(NT, 4, 64).transpose(1, 2, 0))
    d["qbT"] = np.ascontiguousarray(Pown[:, 512:768].reshape(NT, 4, 64).transpose(1, 2, 0))
    qiT = Pown[:, 1024:1536].reshape(NT, 8, 64).transpose(2, 1, 0)
    d["qi2"] = np.ascontiguousarray(np.concatenate([qiT, qiT], 0))
    d["qcT"] = np.ascontiguousarray(Pown[:, 1600:2112].reshape(NT, 8, 64).transpose(1, 2, 0))
    d["wi"] = np.ascontiguousarray(wiown)
    return d


def ot_to_oa(OT):
    Oa = np.ascontiguousarray(OT.transpose(2, 0, 1)).copy()
    Oa[:, 8:, 64] = 1.0
    return Oa.reshape(Oa.shape[0], 16 * 65)


def fused_consts(NCH):
    p = np.arange(128)
    d = {}
    u = np.arange(4); f = np.arange(512)
    kk = u[None, :, None] * 128 + p[:, None, None]
    qq = f[None, None, :]
    d["mLE"] = (kk <= qq).astype(BF)
    d["mLT"] = (kk < qq).astype(BF)
    col = np.arange(512)
    qpos = np.arange(4)[None, :, None] * 128 + p[:, None, None]
    d["cb"] = np.where(col[None, None, :] <= qpos, 0.0, -1e30).astype(np.float32)
    j = np.arange(NCH)[None, :, None, None]; qt = np.arange(4)[None, None, :, None]; blk = np.arange(32)[None, None, None, :]
    cur = 2 * j + qt // 2 + 0 * p[:, None, None, None]
    d["pastbias"] = np.where(blk < cur, 0.0, -1e30).astype(np.float32)
    d["pastvalid"] = (blk < cur).astype(np.float32)
    d["future"] = (blk > cur).astype(np.float32)
    d["pow2"] = np.tile((0.5 ** (np.arange(NIT + 1) + 1)).astype(np.float32)[None, :], (128, 1))
    d["ident"] = np.eye(128, dtype=np.float32)
    d["uinc"] = (p[:, None] >= p[None, :]).astype(np.float32)
    SK = NCH * 512
    oh = np.zeros((32, SK), dtype=BF)
    for b in range(SK // 256):
        oh[b, b * 256:(b + 1) * 256] = BF(-BIGA)
    d["oh"] = oh
    return d

PERM = np.concatenate([np.arange(0, 256), np.arange(256, 512), np.arange(768, 1024), np.arange(1024, 1280),
                       np.arange(1536, 2048), np.arange(2048, 2112), np.arange(2120, 2632), np.arange(2632, 3144),
                       np.arange(512, 768), np.arange(1280, 1536), np.arange(3144, 3656), np.arange(2112, 2120)])
ROPE_THETA = 500000.0
_CACHE = {}


def kernel(**inputs):
    inp = {k: np.asarray(v) for k, v in inputs.items()}
    x = inp["x"]
    B, S_, Dm = x.shape
    NCH = S_ // 512
    depth = inp["w_ada"].shape[0]
    ncores = 2 * B
    key = (NCH, depth)
    if key not in _CACHE:
        _CACHE[key] = build_fused(NCH, depth)
    nc = _CACHE[key]
    shared = fused_consts(NCH)
    inv = (ROPE_THETA ** (-np.arange(0, 16, 2, dtype=np.float32) / np.float32(16))).astype(np.float32)
    ang = np.arange(S_, dtype=np.float32)[:, None] * inv[None, :]
    shared["cs"] = np.ascontiguousarray(np.concatenate([np.cos(ang), np.sin(ang)], 1).astype(np.float32))
    for l in range(depth):
        shared["wada%d" % l] = np.ascontiguousarray(inp["w_ada"][l])
        shared["badar%d" % l] = np.ascontiguousarray(inp["b_ada"][l][None, :])
        shared["badac%d" % l] = np.ascontiguousarray(inp["b_ada"][l].reshape(72, 128).T)
        shared["ng%d" % l] = np.ascontiguousarray(inp["norm_g"][l].reshape(24, 128).T)
        shared["w_in%d" % l] = np.ascontiguousarray(inp["w_in"][l][:, PERM])
        shared["qkg%d" % l] = np.ascontiguousarray(np.repeat(inp["qk_g"][l], 4, axis=0).reshape(1, 1024))
        shared["w_out%d" % l] = np.ascontiguousarray(inp["w_out"][l])
        shared["outg%d" % l] = np.ascontiguousarray(inp["out_g"][l][None, :])
        for i in range(2):
            shared["w1_%d_%d" % (l, i)] = np.ascontiguousarray(inp["ffn_w1"][l, i])
            shared["w3_%d_%d" % (l, i)] = np.ascontiguousarray(inp["ffn_w3"][l, i])
            shared["w2_%d_%d" % (l, i)] = np.ascontiguousarray(inp["ffn_w2"][l, i])
    maps = []
    for ci in range(ncores):
        b = ci // 2
        d = dict(shared)
        d["x"] = np.ascontiguousarray(x[b])
        d["cT"] = np.ascontiguousarray(inp["c"][b].reshape(8, 128).T)
        maps.append(d)
    res = run_bass_kernel_spmd(nc, maps, core_ids=list(range(ncores)))
    out = np.empty((B, S_, Dm), dtype=np.float32)
    for b in range(B):
        out[b] = np.asarray(res.results[2 * b]["y"])
    return out
```

```python
import numpy as np
import ml_dtypes
from contextlib import ExitStack
import concourse.bass as bass
import concourse.mybir as mybir
from concourse.bass_utils import run_bass_kernel_spmd


F32 = mybir.dt.float32
BF16 = mybir.dt.bfloat16
AF = mybir.ActivationFunctionType
ALU = mybir.AluOpType
AX = mybir.AxisListType


def C(name, *a, **kw):
    return lambda e: getattr(e, name)(*a, **kw)


class Sched:
    COMPUTE = ("pe", "act", "dve", "pool")

    def __init__(self, nc, es, n_dma_sems=24):
        self.nc = nc
        self.es = es
        self.engs = {"pe": nc.tensor, "act": nc.scalar, "dve": nc.vector, "pool": nc.gpsimd, "sp": nc.sync}
        self.sem = {e: es.enter_context(nc.semaphore("s_" + e)) for e in self.COMPUTE}
        self.cnt = {e: 0 for e in self.COMPUTE}
        self.known = {e: {} for e in self.engs}
        self.dsems = {}
        self.regW = {}
        self.regR = {}
        self.ops = {e: [] for e in self.engs}
        self.nops = 0

    def _dsem(self, key):
        if key not in self.dsems:
            self.dsems[key] = [self.es.enter_context(self.nc.semaphore("d_%d" % len(self.dsems))), 0]
        return self.dsems[key]

    def _need(self, eng, deps, same_engine_ok):
        waits = []
        kn = self.known[eng]
        for sid, (sem, val) in deps.items():
            if same_engine_ok and sid == eng:
                continue
            if kn.get(sid, 0) >= val:
                continue
            kn[sid] = val
            waits.append((sem, val))
        return waits

    _defer = None

    def begin_defer(self):
        self._defer = []

    def end_defer(self):
        l = self._defer
        self._defer = None
        return l

    def replay(self, X, Y=()):
        ix = iy = 0
        nx, ny = len(X), len(Y)
        while ix < nx or iy < ny:
            if iy >= ny or (ix < nx and ix * ny <= iy * nx):
                self.op(*X[ix]); ix += 1
            else:
                self.op(*Y[iy]); iy += 1

    def op(self, eng, fn, reads=(), writes=(), dsem=None, nosame=None):
        if self._defer is not None:
            self._defer.append((eng, fn, tuple(reads), tuple(writes), dsem, nosame))
            return None
        if nosame is None:
            nosame = (eng == "pe")
        deps = {}
        for r in reads:
            for sid, sv in self.regW.get(r, {}).items():
                if deps.get(sid, (None, 0))[1] < sv[1]:
                    deps[sid] = sv
        for w in writes:
            for d in (self.regW.get(w, {}), self.regR.get(w, {})):
                for sid, sv in d.items():
                    if deps.get(sid, (None, 0))[1] < sv[1]:
                        deps[sid] = sv
        waits = self._need(eng, deps, nosame)
        if dsem is None:
            self.cnt[eng] += 1
            sid, sem, val, inc = eng, self.sem[eng], self.cnt[eng], 1
        else:
            ds = self._dsem(dsem)
            ds[1] += 16
            sid, sem, val, inc = ("d", dsem), ds[0], ds[1], 16
        for r in reads:
            self.regR.setdefault(r, {})[sid] = (sem, val)
        for w in writes:
            self.regW.setdefault(w, {})[sid] = (sem, val)
        self.ops[eng].append((waits, fn, sem, inc))
        self.nops += 1
        return (sid, sem, val)

    def barrier(self, roll=30000):
        deps = {}
        for e in self.COMPUTE:
            if self.cnt[e] > 0:
                deps[e] = (self.sem[e], self.cnt[e])
        for k, (sem, c) in self.dsems.items():
            if c > 0:
                deps[("d", k)] = (sem, c)
        for e in self.engs:
            waits = self._need(e, dict(deps), False)
            self.ops[e].append((waits, None, None, 0))
        self.regW.clear(); self.regR.clear()
        for e in self.COMPUTE:
            if self.cnt[e] > roll:
                self.nroll = getattr(self, "nroll", 0) + 1
                self.sem[e] = self.es.enter_context(self.nc.semaphore("s_%s_%d" % (e, self.nroll)))
                self.cnt[e] = 0
                for e2 in self.engs:
                    self.known[e2].pop(e, None)
        for k in list(self.dsems.keys()):
            if self.dsems[k][1] > roll:
                self.nroll = getattr(self, "nroll", 0) + 1
                self.dsems[k] = [self.es.enter_context(self.nc.semaphore("d_r%d" % self.nroll)), 0]
                for e2 in self.engs:
                    self.known[e2].pop(("d", k), None)

    def finish_wait(self, eng, regions):
        deps = {}
        for r in regions:
            for sid, sv in self.regW.get(r, {}).items():
                if deps.get(sid, (None, 0))[1] < sv[1]:
                    deps[sid] = sv
        waits = self._need(eng, deps, False)
        self.ops[eng].append((waits, None, None, 0))

    def emit(self):
        nc = self.nc
        with nc.Block() as block:
            def mk(ename):
                lst = self.ops[ename]

                def body(e):
                    for waits, fn, sem, inc in lst:
                        for (ws, wv) in waits:
                            e.wait_ge(ws, wv)
                        if fn is not None:
                            ins = fn(e)
                            ins.then_inc(sem, inc)
                return body
            block.tensor(mk("pe"))
            block.scalar(mk("act"))
            block.vector(mk("dve"))
            block.gpsimd(mk("pool"))
            block.sync(mk("sp"))


D = 1024
DFF = 2816
NFC = DFF // 128
DIN = 3656
NPB = 3648
EPS = 1e-6
NIT = 16
BIGA = 240000.0
TOPK = 256
NFB = 21


def _barrier(S):
    S.barrier()


def build_fused(NCH, DEPTH=2, R=1):
    NT = NCH * 512
    SK = NCH * R * 512
    NKT = SK // 128
    NB = SK // 256
    nc = bass.Bass("TRN2", target_bir_lowering=False)
    din = lambda name, shape, dt=F32: nc.dram_tensor(name, list(shape), dt, kind="ExternalInput").ap()
    dout = lambda name, shape, dt=F32: nc.dram_tensor(name, list(shape), dt, kind="ExternalOutput").ap()
    dscr = lambda name, shape, dt=F32: nc.dram_tensor(name, list(shape), dt, kind="Internal").ap()
    uid = [0]

    def nm(name):
        uid[0] += 1
        return "s%d_%s" % (uid[0], name)

    x_in = din("x", [NT, D]); cT = din("cT", [128, 8]); identd = din("ident", [128, 128]); uincd = din("uinc", [128, 128])
    cs = din("cs", [NT, 16])
    L = []
    for l in range(DEPTH):
        p = {}
        p["wada"] = din("wada%d" % l, [D, 9 * D]); p["badar"] = din("badar%d" % l, [1, 9 * D]); p["badac"] = din("badac%d" % l, [128, 72])
        p["ng"] = din("ng%d" % l, [128, 24]); p["w_in"] = din("w_in%d" % l, [D, DIN]); p["qkg"] = din("qkg%d" % l, [1, 1024])
        p["w_out"] = din("w_out%d" % l, [D, D]); p["outg"] = din("outg%d" % l, [1, D])
        for i in range(2):
            p["w1_%d" % i] = din("w1_%d_%d" % (l, i), [D, DFF]); p["w3_%d" % i] = din("w3_%d_%d" % (l, i), [D, DFF]); p["w2_%d" % i] = din("w2_%d_%d" % (l, i), [DFF, D])
        L.append(p)
    mLE = din("mLE", [128, 4 * R, 512], BF16); mLT = din("mLT", [128, 4 * R, 512], BF16)
    cbd = din("cb", [128, 4, R * 512]); ohd = din("oh", [32, SK], BF16)
    pbd = din("pastbias", [128, NCH, 4, 32]); pvd = din("pastvalid", [128, NCH, 4, 32]); fud = din("future", [128, NCH, 4, 32])
    pow2d = din("pow2", [128, NIT + 1])
    y_out = dout("y", [NT, D])
    xA = dscr("xA", [NT, D]); xB = dscr("xB", [NT, D]); xC = dscr("xC", [NT, D])
    FT = dscr("FT", [NFB * 128, SK], BF16); Vscr = dscr("Vscr", [NKT, 128, 16 * 65], BF16); wiS = dscr("wiS", [NT, 8])
    OT = dscr("OT", [16, 65, NT]); nsT = dscr("nsT", [NCH, NKT, 128, 512], BF16)

    with ExitStack() as es:
        S = Sched(nc, es)
        ps = es.enter_context(nc.psum_tensor("ps", [128, 8, 512], F32))
        sbg = lambda name, shape, dt=F32: es.enter_context(nc.sbuf_tensor(nm(name), list(shape), dt))
        ident32 = sbg("ident32", [128, 128]); ident = sbg("identb", [128, 128], BF16); id32b = sbg("id32b", [128, 128])
        nbigI = sbg("nbigI", [128, 128], BF16); uinc = sbg("uinc", [128, 128], BF16); ones = sbg("ones", [128, 128], BF16)
        sc = sbg("sc", [128, 8]); scb = sbg("scb", [128, 8, 128]); epsb = sbg("epsb", [128, 1])
        S.op("sp", C("dma_start", out=ident32[:], in_=identd[:]), writes=["ident32"], dsem="c_id")
        S.op("sp", C("dma_start", out=id32b[:], in_=uincd[:]), writes=["id32b"], dsem="c_id2")
        S.op("sp", C("dma_start", out=sc[:], in_=cT[:]), writes=["sc"], dsem="c_sc")
        S.op("dve", C("tensor_copy", out=ident[:], in_=ident32[:]), reads=["ident32"], writes=["ident"])
        S.op("dve", C("tensor_scalar", out=nbigI[:], in0=ident32[:], scalar1=-BIGA, scalar2=None, op0=ALU.mult), reads=["ident32"], writes=["nbigI"])
        S.op("dve", C("tensor_copy", out=uinc[:], in_=id32b[:]), reads=["id32b"], writes=["uinc"])
        S.op("dve", C("memset", ones[:], 1.0), writes=["ones"])
        S.op("dve", C("memset", epsb[:], EPS), writes=["epsb"])
        S.op("act", C("activation", out=sc[:], in_=sc[:], func=AF.Silu), reads=["sc"], writes=["sc"])
        S.op("dve", C("tensor_copy", out=scb[:], in_=sc[:].unsqueeze(2).to_broadcast([128, 8, 128])), reads=["sc"], writes=["scb"])

        def mod_phase(p, cols, rows, scope):
            out = {}
            sbs = lambda name, shape, dt=F32: scope.enter_context(nc.sbuf_tensor(nm(name), list(shape), dt))
            for (m, name) in cols:
                out[name] = sbs(name, [128, 8])
            for (m, name, scl) in rows:
                out[name] = sbs(name, [128, 1024])
            ngt = sbs("ng", [128, 24])
            with ExitStack() as ph:
                wa = ph.enter_context(nc.sbuf_tensor(nm("wa"), [128, 8, 1024], F32))
                bc = ph.enter_context(nc.sbuf_tensor(nm("bc"), [128, 72], F32))
                br = ph.enter_context(nc.sbuf_tensor(nm("br"), [128, 1024], F32))
                S.op("sp", C("dma_start", out=bc[:], in_=p["badac"][:]), writes=["bc"], dsem="bc")
                for (m, name) in cols:
                    t = out[name]
                    S.op("sp", C("dma_start", out=wa[:], in_=p["wada"][:, m * 1024:(m + 1) * 1024].rearrange("(k p) n -> p k n", p=128)), writes=["wa"], dsem="wa")
                    for kk in range(8):
                        for k in range(8):
                            S.op("pe", C("matmul", ps[:, 0, kk:kk + 1], lhsT=wa[:, k, kk * 128:(kk + 1) * 128], rhs=sc[:, k:k + 1], start=(k == 0), stop=(k == 7)),
                                 reads=["wa", "sc"], writes=["b0"])
                    S.op("dve", C("tensor_tensor", out=t[:], in0=ps[:, 0, 0:8], in1=bc[:, m * 8:(m + 1) * 8], op=ALU.add), reads=["b0", "bc"], writes=["modv"])
                for (m, name, scl) in rows:
                    t = out[name]
                    S.op("sp", C("dma_start", out=wa[:], in_=p["wada"][:, m * 1024:(m + 1) * 1024].rearrange("(k p) n -> p k n", p=128)), writes=["wa"], dsem="wa")
                    S.op("sp", C("dma_start", out=br[:], in_=p["badar"][:, m * 1024:(m + 1) * 1024].partition_broadcast(128)), writes=["br"], dsem="br")
                    for hf in range(2):
                        for k in range(8):
                            S.op("pe", C("matmul", ps[:, 1 + hf, :], lhsT=scb[:, k, :], rhs=wa[:, k, hf * 512:(hf + 1) * 512], start=(k == 0), stop=(k == 7)),
                                 reads=["wa", "scb"], writes=["b%d" % (1 + hf)])
                    S.op("dve", C("tensor_tensor", out=t[:], in0=ps[:, 1:3, :].rearrange("p a b -> p (a b)"), in1=br[:], op=ALU.add), reads=["b1", "b2", "br"], writes=["modv"])
                    if scl != 1.0:
                        S.op("dve", C("tensor_scalar", out=t[:], in0=t[:], scalar1=scl, scalar2=None, op0=ALU.mult), reads=["modv"], writes=["modv"])
                S.op("sp", C("dma_start", out=ngt[:], in_=p["ng"][:]), writes=["modv"], dsem="c_ng")
                out["ng"] = ngt
                _barrier(S)
            return out

        def make_A(scope, sct, ngt, which):
            A = scope.enter_context(nc.sbuf_tensor(nm("A"), [128, 8], F32))
            S.op("dve", C("scalar_tensor_tensor", out=A[:], in0=sct[:], scalar=1.0, in1=ngt[:, which * 8:(which + 1) * 8], op0=ALU.add, op1=ALU.mult), reads=["modv"], writes=["modv"])
            return A

        def norm_T(tiles, xs, A, B, hT):
            ss, rs, xn = tiles
            for t in range(2):
                S.op("act", C("activation", out=xn[:, t, :], in_=xs[:, t, :], func=AF.Square, accum_out=ss[:, t:t + 1]), reads=["xs"], writes=["xn", "ss"])
            S.op("act", C("activation", out=rs[:], in_=ss[:], func=AF.Ln, scale=1.0 / D, bias=epsb[:]), reads=["ss", "epsb"], writes=["rs"])
            S.op("act", C("activation", out=rs[:], in_=rs[:], func=AF.Exp, scale=-0.5), reads=["rs"], writes=["rs"])
            for t in range(2):
                S.op("dve", C("tensor_scalar", out=xn[:, t, :], in0=xs[:, t, :], scalar1=rs[:, t:t + 1], scalar2=None, op0=ALU.mult), reads=["xs", "rs"], writes=["xn"])
            psT = ps[:, 6:8, :].rearrange("p a b -> p (a b)").bitcast(BF16)
            for k in range(8):
                for t in range(2):
                    S.op("pe", C("transpose", out=psT[:, k * 256 + t * 128:k * 256 + (t + 1) * 128], in_=xn[:, t, k * 128:(k + 1) * 128], identity=ident[:]),
                         reads=["xn", "ident"], writes=["b6", "b7"])
            for k in range(8):
                if k % 2 == 0:
                    S.op("dve", C("tensor_scalar", out=hT[:, k, :], in0=psT[:, k * 256:(k + 1) * 256], scalar1=A[:, k:k + 1], scalar2=B[:, k:k + 1], op0=ALU.mult, op1=ALU.add),
                         reads=["b6", "b7", "modv"], writes=["hT"])
                else:
                    S.op("act", C("activation", out=hT[:, k, :], in_=psT[:, k * 256:(k + 1) * 256], func=AF.Identity, scale=A[:, k:k + 1], bias=B[:, k:k + 1]),
                         reads=["b6", "b7", "modv"], writes=["hT"])

        def ffn_phase(src, dst, A, B, G, w1, w3, w2):
            with ExitStack() as ph:
                sbp = lambda name, shape, dt=F32: ph.enter_context(nc.sbuf_tensor(nm(name), list(shape), dt))
                w1s = sbp("w1s", [128, 8, DFF], BF16); w3s = sbp("w3s", [128, 8, DFF], BF16); w2s = sbp("w2s", [128, NFC, D], BF16)
                xs = sbp("xs", [128, 2, D]); xn = sbp("xn", [128, 2, D], BF16)
                ss = sbp("ss", [128, 2]); rs = sbp("rs", [128, 2])
                hT = sbp("hT", [128, 8, 256], BF16)
                s1 = [sbp("s1_%d" % i, [128, 256]) for i in range(2)]
                g = [sbp("g_%d" % i, [128, 256], BF16) for i in range(2)]
                tmp = sbp("tmp", [128, D])
                hw = DFF // 2
                for k in range(8):
                    for hf in range(2):
                        S.op("pool", C("dma_start", out=w1s[:, k, hf * hw:(hf + 1) * hw], in_=w1[k * 128:(k + 1) * 128, hf * hw:(hf + 1) * hw]), writes=["w1s"], dsem="w1s")
                        S.op("pool", C("dma_start", out=w3s[:, k, hf * hw:(hf + 1) * hw], in_=w3[k * 128:(k + 1) * 128, hf * hw:(hf + 1) * hw]), writes=["w3s"], dsem="w3s")
                for fc in range(NFC):
                    S.op("pool", C("dma_start", out=w2s[:, fc, :], in_=w2[fc * 128:(fc + 1) * 128, :]), writes=["w2s"], dsem="w2s")
                for hs in range(NT // 256):
                    r0 = hs * 256
                    S.op("sp", C("dma_start", out=xs[:], in_=src[r0:r0 + 256, :].rearrange("(t p) d -> p t d", p=128)), writes=["xs"], dsem="xs")
                    norm_T((ss, rs, xn), xs, A, B, hT)

                    def ymm(fc):
                        gb = g[fc % 2]
                        for t in range(2):
                            for hf in range(2):
                                S.op("pe", C("matmul", ps[:, t * 2 + hf, :], lhsT=gb[:, t * 128:(t + 1) * 128], rhs=w2s[:, fc, hf * 512:(hf + 1) * 512],
                                             start=(fc == 0), stop=(fc == NFC - 1)), reads=["g%d" % (fc % 2), "w2s"], writes=["b%d" % (t * 2 + hf)])
                    for fc in range(NFC):
                        ub = 4 + 2 * (fc % 2)
                        for (wsb, wn, bank) in ((w1s, "w1s", ub), (w3s, "w3s", ub + 1)):
                            for k in range(8):
                                S.op("pe", C("matmul", ps[:, bank, 0:256], lhsT=wsb[:, k, fc * 128:(fc + 1) * 128], rhs=hT[:, k, :], start=(k == 0), stop=(k == 7)),
                                     reads=[wn, "hT"], writes=["b%d" % bank])
                        if fc > 0:
                            ymm(fc - 1)
                        S.op("act", C("activation", out=s1[fc % 2][:], in_=ps[:, ub, 0:256], func=AF.Silu), reads=["b%d" % ub], writes=["s1_%d" % (fc % 2)])
                        S.op("dve", C("tensor_tensor", out=g[fc % 2][:], in0=ps[:, ub + 1, 0:256], in1=s1[fc % 2][:], op=ALU.mult),
                             reads=["b%d" % (ub + 1), "s1_%d" % (fc % 2)], writes=["g%d" % (fc % 2)])
                    ymm(NFC - 1)
                    for t in range(2):
                        S.op("dve", C("tensor_tensor", out=tmp[:], in0=ps[:, 2 * t:2 * t + 2, :].rearrange("p a b -> p (a b)"), in1=G[:], op=ALU.mult),
                             reads=["b%d" % (2 * t), "b%d" % (2 * t + 1), "modv"], writes=["tmp"])
                        S.op("pool", C("tensor_tensor", out=xs[:, t, :], in0=tmp[:], in1=xs[:, t, :], op=ALU.add), reads=["tmp", "xs"], writes=["xs"])
                    S.op("sp", C("dma_start", out=dst[r0:r0 + 256, :].rearrange("(t p) d -> p t d", p=128), in_=xs[:]), reads=["xs"], writes=["xdst"], dsem="xo")
                _barrier(S)

        def lt_mix(l, src, dst):
            p = L[l]
            with ExitStack() as mixscope:
                mm = mod_phase(p, [(6, "sh3"), (7, "sc3")], [(5, "g2", 1.0), (8, "g3", 0.5)], mixscope)
                A3 = make_A(mixscope, mm["sc3"], mm["ng"], 2)
                with ExitStack() as ph:
                    sbp = lambda name, shape, dt=F32: ph.enter_context(nc.sbuf_tensor(nm(name), list(shape), dt))
                    wos = sbp("wos", [128, 8, D], BF16)
                    og = sbp("og", [128, D])
                    ot65 = sbp("ot65", [65, 16, 128])
                    oa = sbp("oa", [128, 16, 65]); sq = sbp("osq", [128, 16, 64]); oss = sbp("oss", [128, 16]); z2 = sbp("z2", [128, 16])
                    on = sbp("on", [128, D], BF16); on32 = sbp("on32", [128, D]); oT = sbp("oT", [128, 8, 128], BF16)
                    xs = sbp("mxs", [128, D]); tmp = sbp("mtmp", [128, D]); xo = sbp("mxo", [128, D])
                    for k in range(8):
                        S.op("pool", C("dma_start", out=wos[:, k, :], in_=p["w_out"][k * 128:(k + 1) * 128, :]), writes=["wos"], dsem="wos")
                    S.op("sp", C("dma_start", out=og[:], in_=p["outg"][:, :].partition_broadcast(128)), writes=["og"], dsem="og")
                    for tt in range(NT // 128):
                        r0 = tt * 128
                        S.op("sp", C("dma_start", out=ot65[:], in_=OT[:, :, r0:r0 + 128].rearrange("h r n -> r h n")), writes=["ot65"], dsem="ot65")
                        S.op("sp", C("dma_start", out=xs[:], in_=src[r0:r0 + 128, :]), writes=["mxs"], dsem="mxs")
                        for h in range(16):
                            bank = 2 + h // 7; off = (h % 7) * 65
                            S.op("pe", C("transpose", out=ps[:, bank, off:off + 65], in_=ot65[:, h, :], identity=ident32[0:65, 0:65]), reads=["ot65", "ident32"], writes=["b%d" % bank])
                        for (bank, h0, nh_) in ((2, 0, 7), (3, 7, 7), (4, 14, 2)):
                            S.op("act", C("activation", out=oa[:, h0:h0 + nh_, :].rearrange("p h d -> p (h d)"), in_=ps[:, bank, 0:nh_ * 65], func=AF.Copy),
                                 reads=["b%d" % bank], writes=["oa"])
                        S.op("pool", C("tensor_tensor", out=sq[:], in0=oa[:, :, 0:64], in1=oa[:, :, 0:64], op=ALU.mult), reads=["oa"], writes=["osq"])
                        S.op("dve", C("tensor_reduce", out=oss[:], in_=sq[:, :, :], axis=AX.X, op=ALU.add, opt_input=False, opt_output=False), reads=["osq"], writes=["oss"])
                        S.op("dve", C("scalar_tensor_tensor", out=z2[:], in0=oa[:, :, 64], scalar=EPS, in1=oa[:, :, 64], op0=ALU.mult, op1=ALU.mult), reads=["oa"], writes=["z2"])
                        S.op("dve", C("scalar_tensor_tensor", out=oss[:], in0=oss[:], scalar=1.0 / 64, in1=z2[:], op0=ALU.mult, op1=ALU.add), reads=["oss", "z2"], writes=["oss"])
                        S.op("act", C("activation", out=oss[:], in_=oss[:], func=AF.Ln), reads=["oss"], writes=["oss"])
                        S.op("act", C("activation", out=oss[:], in_=oss[:], func=AF.Exp, scale=-0.5), reads=["oss"], writes=["oss"])
                        S.op("dve", C("tensor_tensor", out=on32[:].rearrange("p (h d) -> p h d", d=64), in0=oa[:, :, 0:64],
                                      in1=oss[:].unsqueeze(2).to_broadcast([128, 16, 64]), op=ALU.mult), reads=["oa", "oss"], writes=["on32"])
                        S.op("pool", C("tensor_tensor", out=on[:], in0=on32[:], in1=og[:], op=ALU.mult), reads=["on32", "og"], writes=["on"])
                        psT = ps[:, 6, :].bitcast(BF16)
                        for k in range(8):
                            S.op("pe", C("transpose", out=psT[:, k * 128:(k + 1) * 128], in_=on[:, k * 128:(k + 1) * 128], identity=ident[:]), reads=["on", "ident"], writes=["b6"])
                        S.op("act", C("activation", out=oT[:].rearrange("p k t -> p (k t)"), in_=psT[:, 0:1024], func=AF.Copy), reads=["b6"], writes=["oT"])
                        for hf in range(2):
                            for k in range(8):
                                S.op("pe", C("matmul", ps[:, hf, :], lhsT=oT[:, k, :], rhs=wos[:, k, hf * 512:(hf + 1) * 512], start=(k == 0), stop=(k == 7)),
                                     reads=["oT", "wos"], writes=["b%d" % hf])
                        S.op("dve", C("tensor_tensor", out=tmp[:], in0=ps[:, 0:2, :].rearrange("p a b -> p (a b)"), in1=mm["g2"][:], op=ALU.mult),
                             reads=["b0", "b1", "modv"], writes=["mtmp"])
                        S.op("pool", C("tensor_tensor", out=xo[:], in0=tmp[:], in1=xs[:], op=ALU.add), reads=["mtmp", "mxs"], writes=["mxo"])
                        S.op("sp", C("dma_start", out=xC[r0:r0 + 128, :], in_=xo[:]), reads=["mxo"], writes=["xdst"], dsem="mxo")
                    _barrier(S)
                ffn_phase(xC, dst, A3, mm["sh3"], mm["g3"], p["w1_1"], p["w3_1"], p["w2_1"])

        def lt_pre(l, src, dst):
            p = L[l]
            with ExitStack() as prescope:
                pm = mod_phase(p, [(0, "sh1"), (1, "sc1"), (3, "sh2"), (4, "sc2")], [(2, "g1", 0.5)], prescope)
                A1 = make_A(prescope, pm["sc1"], pm["ng"], 0)
                A2 = make_A(prescope, pm["sc2"], pm["ng"], 1)
                ffn_phase(src, dst, A1, pm["sh1"], pm["g1"], p["w1_0"], p["w3_0"], p["w2_0"])
                with ExitStack() as ph:
                    sbp = lambda name, shape, dt=F32: ph.enter_context(nc.sbuf_tensor(nm(name), list(shape), dt))
                    wis = sbp("wis", [128, 8, DIN], BF16)
                    qg = sbp("qg", [128, 1024])
                    xs = sbp("pxs", [128, 2, D]); xn = sbp("pxn", [128, 2, D], BF16)
                    ss = sbp("pss", [128, 2]); rs = sbp("prs", [128, 2])
                    hT = sbp("phT", [128, 8, 256], BF16)
                    pr = [sbp("pr%d" % i, [128, DIN]) for i in range(2)]
                    prb = [sbp("prb%d" % i, [128, NPB], BF16) for i in range(2)]
                    vaug = [sbp("vaug%d" % i, [128, 16, 65], BF16) for i in range(2)]
                    fT = sbp("fT", [128, NFB, 256], BF16)
                    sq = sbp("psq", [128, 1024]); hs_ = sbp("phs", [128, 16])
                    cst = sbp("cst", [128, 2, 16])
                    r1 = sbp("r1", [128, 25, 8]); r2 = sbp("r2", [128, 25, 8]); r3 = sbp("r3", [128, 25, 8]); r4 = sbp("r4", [128, 25, 8])
                    cw = 1828
                    for k in range(8):
                        for hf in range(2):
                            S.op("pool", C("dma_start", out=wis[:, k, hf * cw:(hf + 1) * cw], in_=p["w_in"][k * 128:(k + 1) * 128, hf * cw:(hf + 1) * cw]), writes=["wis"], dsem="wis")
                    S.op("sp", C("dma_start", out=qg[:], in_=p["qkg"][:, :].partition_broadcast(128)), writes=["qg"], dsem="qg")
                    for t in range(2):
                        S.op("pool", C("memset", vaug[t][:], 1.0), writes=["vaug%d" % t])
                    S.op("pool", C("memset", fT[:], 0.0), writes=["fT"])
                    for hs in range(NT // 256):
                        r0 = hs * 256
                        S.op("sp", C("dma_start", out=xs[:], in_=dst[r0:r0 + 256, :].rearrange("(t p) d -> p t d", p=128)), writes=["xs"], dsem="pxs")
                        S.op("sp", C("dma_start", out=cst[:], in_=cs[r0:r0 + 256, :].rearrange("(t p) d -> p t d", p=128)), writes=["cst"], dsem="cst")
                        norm_T((ss, rs, xn), xs, A2, pm["sh2"], hT)
                        for t in range(2):
                            prt = pr[t]; prbt = prb[t]; pn = "pr%d" % t; pbn = "prb%d" % t
                            ngrp = (DIN + 511) // 512
                            for gi in range(ngrp):
                                c0 = gi * 512; c1 = min(DIN, c0 + 512)
                                bank = gi % 6 if gi < 6 else gi - 6
                                for k in range(8):
                                    S.op("pe", C("matmul", ps[:, bank, 0:c1 - c0], lhsT=hT[:, k, t * 128:(t + 1) * 128], rhs=wis[:, k, c0:c1], start=(k == 0), stop=(k == 7)),
                                         reads=["hT", "wis"], writes=["b%d" % bank])
                                S.op("act", C("activation", out=prt[:, c0:c1], in_=ps[:, bank, 0:c1 - c0], func=AF.Copy), reads=["b%d" % bank], writes=[pn])
                            S.op("pool", C("tensor_tensor", out=sq[:], in0=prt[:, 0:1024], in1=prt[:, 0:1024], op=ALU.mult), reads=[pn], writes=["psq"])
                            S.op("dve", C("tensor_reduce", out=hs_[:], in_=sq[:].rearrange("p (h d) -> p h d", d=64), axis=AX.X, op=ALU.add), reads=["psq"], writes=["phs"])
                            S.op("act", C("activation", out=hs_[:], in_=hs_[:], func=AF.Ln, scale=1.0 / 64, bias=epsb[:]), reads=["phs", "epsb"], writes=["phs"])
                            S.op("act", C("activation", out=hs_[:], in_=hs_[:], func=AF.Exp, scale=-0.5), reads=["phs"], writes=["phs"])
                            S.op("dve", C("tensor_tensor", out=prt[:, 0:1024].rearrange("p (h d) -> p h d", d=64), in0=prt[:, 0:1024].rearrange("p (h d) -> p h d", d=64),
                                          in1=hs_[:].unsqueeze(2).to_broadcast([128, 16, 64]), op=ALU.mult), reads=[pn, "phs"], writes=[pn])
                            S.op("pool", C("tensor_tensor", out=prt[:, 0:1024], in0=prt[:, 0:1024], in1=qg[:], op=ALU.mult), reads=[pn, "qg"], writes=[pn])
                            hv = prt[:, 0:1600].rearrange("p (h d) -> p h d", d=64)
                            cosb = cst[:, t, 0:8].unsqueeze(1).to_broadcast([128, 25, 8]); sinb = cst[:, t, 8:16].unsqueeze(1).to_broadcast([128, 25, 8])
                            S.op("dve", C("tensor_tensor", out=r1[:], in0=hv[:, :, 0:8], in1=cosb, op=ALU.mult), reads=[pn, "cst"], writes=["r1"])
                            S.op("pool", C("tensor_tensor", out=r2[:], in0=hv[:, :, 8:16], in1=sinb, op=ALU.mult), reads=[pn, "cst"], writes=["r2"])
                            S.op("dve", C("tensor_tensor", out=r3[:], in0=hv[:, :, 8:16], in1=cosb, op=ALU.mult), reads=[pn, "cst"], writes=["r3"])
                            S.op("pool", C("tensor_tensor", out=r4[:], in0=hv[:, :, 0:8], in1=sinb, op=ALU.mult), reads=[pn, "cst"], writes=["r4"])
                            S.op("dve", C("tensor_tensor", out=hv[:, :, 0:8], in0=r1[:], in1=r2[:], op=ALU.subtract), reads=["r1", "r2"], writes=[pn])
                            S.op("pool", C("tensor_tensor", out=hv[:, :, 8:16], in0=r3[:], in1=r4[:], op=ALU.add), reads=["r3", "r4"], writes=[pn])
                            S.op("act", C("activation", out=prbt[:], in_=prt[:, 0:NPB], func=AF.Copy), reads=[pn], writes=[pbn])
                            S.op("pool", C("tensor_copy", out=vaug[t][:, :, 0:64], in_=prbt[:, 2624:3648].rearrange("p (h d) -> p h d", d=64)), reads=[pbn], writes=["vaug%d" % t])
                            S.op("sp", C("dma_start", out=Vscr[hs * 2 + t], in_=vaug[t][:].rearrange("p h d -> p (h d)")), reads=["vaug%d" % t], writes=["Vscr"], dsem="vaug%d" % t)
                            S.op("sp", C("dma_start", out=wiS[r0 + t * 128:r0 + (t + 1) * 128, :], in_=prt[:, NPB:DIN]), reads=[pn], writes=["wiS"], dsem="W%d" % t)
                        psT = ps[:, 6:8, :].rearrange("p a b -> p (a b)").bitcast(BF16)
                        for rnd in range(3):
                            b0 = rnd * 8; nb_ = min(8, NFB - b0)
                            for bi in range(nb_):
                                blk = b0 + bi
                                fw = 128 if blk < 20 else 64
                                for t in range(2):
                                    S.op("pe", C("transpose", out=psT[0:fw, bi * 256 + t * 128:bi * 256 + (t + 1) * 128], in_=prb[t][:, blk * 128:blk * 128 + fw], identity=ident[:]),
                                         reads=["prb%d" % t, "ident"], writes=["b6", "b7"])
                            if b0 + nb_ <= 20:
                                S.op("dve", C("tensor_copy", out=fT[:, b0:b0 + nb_, :].rearrange("p b n -> p (b n)"), in_=psT[:, 0:nb_ * 256]), reads=["b6", "b7"], writes=["fT"])
                            else:
                                S.op("dve", C("tensor_copy", out=fT[:, b0:b0 + nb_ - 1, :].rearrange("p b n -> p (b n)"), in_=psT[:, 0:(nb_ - 1) * 256]), reads=["b6", "b7"], writes=["fT"])
                                S.op("dve", C("tensor_copy", out=fT[0:64, NFB - 1, :], in_=psT[0:64, (nb_ - 1) * 256:nb_ * 256]), reads=["b6", "b7"], writes=["fT"])
                        S.op("sp", C("dma_start", out=FT.rearrange("(b p) n -> p b n", p=128)[:, :, r0:r0 + 256], in_=fT[:]), reads=["fT"], writes=["FT"], dsem="fT")
                    _barrier(S)

        def lb():
            with ExitStack() as lbs:
                sbl = lambda name, shape, dt=F32: lbs.enter_context(nc.sbuf_tensor(nm(name), list(shape), dt))
                mle = sbl("mle", [128, 4 * R, 512], BF16); mlt = sbl("mlt", [128, 4 * R, 512], BF16)
                S.op("sp", C("dma_start", out=mle[:], in_=mLE[:]), writes=["mle"], dsem="c_mle")
                S.op("sp", C("dma_start", out=mlt[:], in_=mLT[:]), writes=["mlt"], dsem="c_mlt")
                with ExitStack() as ph:
                    sbp = lambda name, shape, dt=F32: ph.enter_context(nc.sbuf_tensor(nm(name), list(shape), dt))
                    kis = sbp("kis", [128, SK // 2], BF16)
                    qis = sbp("qis", [128, 8, 512], BF16)
                    wis = sbp("wisb", [128, 4, 8])
                    cb = sbp("cb", [128, 4, R * 512])
                    pow2 = sbp("pow2", [128, NIT + 1])
                    accs = [sbp("acc%d" % i, [128, SK]) for i in range(2)]
                    nsel = sbp("nsel", [128, SK], BF16); junk = sbp("junkb", [128, SK], BF16); junk2 = sbp("junk2", [128, SK], BF16)
                    rr = [sbp("rr%d" % i, [128, 512]) for i in range(2)]
                    stg = [sbp("stg%d" % i, [128, 4, 128], BF16) for i in range(2)]
                    mns = [sbp("mn%d" % i, [128, 16 * R]) for i in range(2)]
                    m8s = [sbp("m8_%d" % i, [128, 8]) for i in range(2)]
                    los = [sbp("lo%d" % i, [128, 1]) for i in range(2)]
                    w0 = sbp("w0", [128, 1]); W = sbp("W", [128, NIT + 1])
                    negmid = sbp("negmid", [128, 1]); ssum = sbp("ssum", [128, 1]); dl = sbp("dl", [128, 1]); thr = sbp("thr", [128, 1])
                    S.op("sp", C("dma_start", out=kis[0:64, :], in_=FT[1536:1600, 0:SK // 2]), reads=["FT"], writes=["kis"], dsem="kis")
                    S.op("sp", C("dma_start", out=kis[64:128, :], in_=FT[1536:1600, SK // 2:SK]), reads=["FT"], writes=["kis"], dsem="kis")
                    S.op("sp", C("dma_start", out=cb[:], in_=cbd[:]), writes=["cb"], dsem="cb")
                    S.op("sp", C("dma_start", out=pow2[:], in_=pow2d[:]), writes=["pow2"], dsem="pow2")
                    tcount = [0]

                    def gen_index(n):
                        j, qt = n // 4, n % 4
                        nkc = R * (j + 1)
                        Kmax = nkc * 512
                        acc = accs[n % 2]; an = "acc%d" % (n % 2); mn = mns[n % 2]; mnn = "mn%d" % (n % 2); m8 = m8s[n % 2]; lo = los[n % 2]; bn = "br%d" % (n % 2)
                        if qt == 0:
                            qsrc = FT[1024:1536, j * 512:(j + 1) * 512].rearrange("(h d) n -> d h n", d=64)
                            S.op("sp", C("dma_start", out=qis[0:64], in_=qsrc), reads=["FT"], writes=["qis"], dsem="qis")
                            S.op("sp", C("dma_start", out=qis[64:128], in_=qsrc), reads=["FT"], writes=["qis"], dsem="qis")
                            S.op("sp", C("dma_start", out=wis[:], in_=wiS[j * 512:(j + 1) * 512, :].rearrange("(t p) h -> p t h", p=128)), reads=["wiS"], writes=["wisb"], dsem="wisb")
                        for kc in range(nkc):
                            k0 = kc * 512
                            half = 0 if k0 < SK // 2 else 1
                            kcol = k0 - half * (SK // 2)
                            pb = half * 64
                            ab = 2 + (kc % 2)
                            for h in range(8):
                                lb_ = h % 2
                                S.op("pe", C("matmul", ps[:, lb_, :], lhsT=qis[pb:pb + 64, h, qt * 128:(qt + 1) * 128], rhs=kis[pb:pb + 64, kcol:kcol + 512], start=True, stop=True),
                                     reads=["qis", "kis"], writes=["b%d" % lb_])
                                S.op("act", C("activation", out=rr[h % 2][:], in_=ps[:, lb_, :], func=AF.Relu), reads=["b%d" % lb_], writes=["rr%d" % (h % 2)])
                                if h == 0:
                                    S.op("dve", C("tensor_scalar", out=ps[:, ab, :], in0=rr[0][:], scalar1=wis[:, qt, 0:1], scalar2=None, op0=ALU.mult),
                                         reads=["rr0", "wisb"], writes=["b%d" % ab])
                                else:
                                    S.op("dve", C("scalar_tensor_tensor", out=ps[:, ab, :], in0=rr[h % 2][:], scalar=wis[:, qt, h:h + 1], in1=ps[:, ab, :], op0=ALU.mult, op1=ALU.add),
                                         reads=["rr%d" % (h % 2), "wisb", "b%d" % ab], writes=["b%d" % ab])
                            S.op("dve", C("tensor_scalar", out=acc[:, k0:k0 + 512], in0=ps[:, ab, :], scalar1=1.0, scalar2=None, op0=ALU.mult, op1=ALU.min, accum_out=mn[:, kc:kc + 1]),
                                 reads=["b%d" % ab], writes=[an, mnn])
                        S.op("pool", C("tensor_tensor", out=acc[:, Kmax - R * 512:Kmax], in0=acc[:, Kmax - R * 512:Kmax], in1=cb[:, qt, :], op=ALU.add), reads=[an, "cb"], writes=[an])
                        S.op("dve", C("max", out=m8[:], in_=acc[:, 0:Kmax]), reads=[an], writes=[bn])
                        if nkc > 1:
                            S.op("dve", C("tensor_reduce", out=lo[:], in_=mn[:, 0:nkc], axis=AX.X, op=ALU.min), reads=[mnn], writes=[bn])
                            S.op("dve", C("tensor_scalar", out=lo[:], in0=lo[:], scalar1=-1.0, scalar2=None, op0=ALU.add), reads=[bn], writes=[bn])
                        else:
                            S.op("dve", C("tensor_scalar", out=lo[:], in0=mn[:, 0:1], scalar1=-1.0, scalar2=None, op0=ALU.add), reads=[mnn], writes=[bn])

                    def gen_select(n):
                        j, qt = n // 4, n % 4
                        nkc = R * (j + 1)
                        Kmax = nkc * 512
                        acc = accs[n % 2]; an = "acc%d" % (n % 2); m8 = m8s[n % 2]; lo = los[n % 2]; bn = "br%d" % (n % 2)
                        S.op("dve", C("tensor_tensor", out=w0[:], in0=m8[:, 0:1], in1=lo[:], op=ALU.subtract), reads=[bn], writes=["w0"])
                        S.op("dve", C("tensor_scalar", out=W[:], in0=pow2[:], scalar1=w0[:, 0:1], scalar2=None, op0=ALU.mult), reads=["pow2", "w0"], writes=["W"])
                        S.op("dve", C("scalar_tensor_tensor", out=negmid[:], in0=lo[:], scalar=-1.0, in1=W[:, 0:1], op0=ALU.mult, op1=ALU.subtract), reads=[bn, "W"], writes=["negmid"])
                        for it in range(NIT):
                            if it % 4 == 3:
                                S.op("dve", C("tensor_scalar", out=thr[:], in0=negmid[:], scalar1=-1.0, scalar2=None, op0=ALU.mult), reads=["negmid"], writes=["thr"])
                                S.op("dve", C("tensor_scalar", out=junk2[:, 0:Kmax], in0=acc[:, 0:Kmax], scalar1=thr[:, 0:1], scalar2=None, op0=ALU.is_gt, op1=ALU.add, accum_out=ssum[:, 0:1]),
                                     reads=[an, "thr"], writes=["junk2", "ssum"])
                                S.op("dve", C("scalar_tensor_tensor", out=dl[:], in0=ssum[:], scalar=float(TOPK), in1=W[:, it:it + 1], op0=ALU.is_ge, op1=ALU.mult), reads=["ssum", "W"], writes=["dl"])
                            else:
                                S.op("act", C("activation", out=junk[:, 0:Kmax], in_=acc[:, 0:Kmax], func=AF.Sign, bias=negmid[:, 0:1], scale=1.0, accum_out=ssum[:, 0:1]),
                                     reads=[an, "negmid"], writes=["junk", "ssum"])
                                S.op("dve", C("scalar_tensor_tensor", out=dl[:], in0=ssum[:], scalar=float(2 * TOPK - Kmax), in1=W[:, it:it + 1], op0=ALU.is_ge, op1=ALU.mult), reads=["ssum", "W"], writes=["dl"])
                            S.op("dve", C("scalar_tensor_tensor", out=negmid[:], in0=negmid[:], scalar=W[:, it + 1:it + 2], in1=dl[:], op0=ALU.add, op1=ALU.subtract),
                                 reads=["negmid", "W", "dl"], writes=["negmid"])
                        S.op("dve", C("scalar_tensor_tensor", out=thr[:], in0=negmid[:], scalar=-1.0, in1=W[:, NIT:NIT + 1], op0=ALU.mult, op1=ALU.subtract), reads=["negmid", "W"], writes=["thr"])
                        S.op("dve", C("tensor_scalar", out=nsel[:, 0:Kmax], in0=acc[:, 0:Kmax], scalar1=thr[:, 0:1], scalar2=None, op0=ALU.is_gt), reads=[an, "thr"], writes=["nsel"])
                        for g4 in range(nkc):
                            tb = 4 + (tcount[0] % 2)
                            sg = stg[tcount[0] % 2]; sgn = "stg%d" % (tcount[0] % 2)
                            tcount[0] += 1
                            pT = ps[:, tb, :].bitcast(BF16)
                            for u in range(4):
                                kt = g4 * 4 + u
                                S.op("pe", C("transpose", out=pT[:, u * 128:(u + 1) * 128], in_=nsel[:, kt * 128:(kt + 1) * 128], identity=ident[:]), reads=["nsel", "ident"], writes=["b%d" % tb])
                            S.op("dve", C("tensor_copy", out=sg[:].rearrange("p a b -> p (a b)"), in_=pT[:, 0:512]), reads=["b%d" % tb], writes=[sgn])
                            S.op("sp", C("dma_start", out=nsT[j, g4 * 4:(g4 + 1) * 4, :, qt * 128:(qt + 1) * 128].rearrange("a p q -> p a q"), in_=sg[:]), reads=[sgn], writes=["nsT"], dsem=sgn)

                    NQ = 4 * NCH
                    gen_index(0)
                    for n in range(NQ):
                        S.begin_defer(); gen_select(n); Y = S.end_defer()
                        X = []
                        if n + 1 < NQ:
                            S.begin_defer(); gen_index(n + 1); X = S.end_defer()
                        S.replay(X, Y)
                    _barrier(S)

                def sweep_phase(kind):
                    nh = {"A": 4, "B2": 4, "C": 8}[kind]
                    KR = 96 if kind == "A" else 64
                    hbase = {"A": 0, "B2": 4, "C": 8}[kind]
                    qrow = {"A": 0, "B2": 512, "C": 1600}[kind]
                    krow = {"A": 256, "B2": 768, "C": 2112}[kind]
                    with ExitStack() as ph:
                        sbp = lambda name, shape, dt=F32: ph.enter_context(nc.sbuf_tensor(nm(name), list(shape), dt))
                        kT = [sbp("kT%d" % i, [KR, SK], BF16) for i in range(2)]
                        qS = [sbp("qS%d" % i, [KR, 512], BF16) for i in range(3)]
                        VG = sbp("VG", [128, NKT, 4, 65], BF16)
                        NPB_ = 3 if kind == "C" else 4
                        Pt = [sbp("P%d" % i, [128, 512], BF16) for i in range(NPB_)]
                        osb = [sbp("osb%d" % i, [65, 512]) for i in range(2)]
                        if kind == "A":
                            pbs = sbp("pbs", [128, NCH, 4, 32]); pvs = sbp("pvs", [128, NCH, 4, 32]); fus = sbp("fus", [128, NCH, 4, 32])
                            km32 = sbp("km32", [64, 32]); kmT = sbp("kmT", [64, 32]); q32 = sbp("q32", [64, 512])
                            gm = sbp("gm", [128, 4, 32]); m8 = sbp("m8a", [128, 4, 8]); nM = sbp("nM", [128, 4, 32]); nMp = sbp("nMp", [128, 4, 96], BF16)
                            S.op("sp", C("dma_start", out=pbs[:], in_=pbd[:]), writes=["pbs"], dsem="pbs")
                            S.op("sp", C("dma_start", out=pvs[:], in_=pvd[:]), writes=["pvs"], dsem="pvs")
                            S.op("sp", C("dma_start", out=fus[:], in_=fud[:]), writes=["fus"], dsem="fus")
                            S.op("pool", C("memset", nMp[:], 0.0), writes=["nMp"])
                            S.op("pool", C("memset", km32[:], 0.0), writes=["km32"])
                        if kind == "C":
                            e32 = [sbp("e32_%d" % i, [128, 512]) for i in range(3)]
                            sp_ = [sbp("sp%d" % i, [128, 512], BF16) for i in range(3)]
                            spa = [sbp("spacc%d" % i, [128, 512], BF16) for i in range(3)]
                        if kind == "B2":
                            nst = [sbp("nst%d" % i, [128, 512], BF16) for i in range(4)]
                        ocount = 0
                        qcount = 0
                        for h in range(nh):
                            hb = h % 2
                            hh = h % 4
                            kTh = kT[hb]
                            kn = "kT%d" % hb
                            if hh == 0:
                                hg = (hbase + h)
                                nq = 4
                                for qd in range(nq):
                                    t0_ = qd * (NKT // nq); t1_ = (qd + 1) * (NKT // nq)
                                    S.op("sp", C("dma_start", out=VG[:, t0_:t1_].rearrange("p t h d -> p t (h d)"), in_=Vscr[t0_:t1_, :, hg * 65:(hg + 4) * 65].rearrange("t p f -> p t f")),
                                         reads=["Vscr"], writes=["VG"], dsem="VG")
                            S.op("sp", C("dma_start", out=kTh[0:64, :], in_=FT[krow + h * 64:krow + (h + 1) * 64, 0:SK]), reads=["FT"], writes=[kn], dsem=kn)
                            if kind == "A":
                                S.op("sp", C("dma_start", out=kTh[64:96, :], in_=ohd[:]), writes=[kn], dsem=kn)
                                S.op("dve", C("tensor_reduce", out=km32[:, 0:NB], in_=kTh[0:64, :].rearrange("p (b k) -> p b k", k=256), axis=AX.X, op=ALU.add), reads=[kn], writes=["km32"])
                                S.op("dve", C("tensor_scalar", out=kmT[:], in0=km32[:], scalar1=1.0 / 256, scalar2=None, op0=ALU.mult), reads=["km32"], writes=["kmT"])
                            for j in range(NCH):
                                nkt = 4 * R * (j + 1)
                                d0 = 4 * R * j
                                qsb = qS[qcount % 3]; qn = "qS%d" % (qcount % 3)
                                qcount += 1
                                S.op("sp", C("dma_start", out=qsb[0:64, :], in_=FT[qrow + h * 64:qrow + (h + 1) * 64, j * 512:(j + 1) * 512]), reads=["FT"], writes=[qn], dsem=qn)
                                qs = qsb[:, :]
                                ob = 6 + (ocount % 2); obn = "b%d" % ob
                                osbt = osb[ocount % 2]; osn = "osb%d" % (ocount % 2)
                                ocount += 1
                                if kind == "A":
                                    S.op("pool", C("tensor_copy", out=q32[:], in_=qsb[0:64, :]), reads=[qn], writes=["q32"])
                                    for qt in range(4):
                                        S.op("pe", C("matmul", ps[:, 5, qt * 32:(qt + 1) * 32], lhsT=q32[:, qt * 128:(qt + 1) * 128], rhs=kmT[:], start=True, stop=True),
                                             reads=["q32", "kmT"], writes=["b5"])
                                    S.op("dve", C("tensor_tensor", out=gm[:].rearrange("p a b -> p (a b)"), in0=ps[:, 5, 0:128], in1=pbs[:, j].rearrange("p a b -> p (a b)"), op=ALU.add),
                                         reads=["b5", "pbs"], writes=["gm"])
                                    for qt in range(4):
                                        S.op("dve", C("max", out=m8[:, qt, :], in_=gm[:, qt, :]), reads=["gm"], writes=["m8a"])
                                    for qt in range(4):
                                        S.op("dve", C("tensor_scalar", out=nM[:, qt, :], in0=gm[:, qt, :], scalar1=m8[:, qt, 2:3], scalar2=None, op0=ALU.is_lt), reads=["gm", "m8a"], writes=["nM"])
                                    S.op("dve", C("tensor_tensor", out=nM[:], in0=nM[:], in1=pvs[:, j], op=ALU.mult), reads=["nM", "pvs"], writes=["nM"])
                                    S.op("dve", C("tensor_tensor", out=nMp[:, :, 64:96], in0=nM[:], in1=fus[:, j], op=ALU.add), reads=["nM", "fus"], writes=["nMp"])
                                    for qt in range(4):
                                        S.op("pe", C("matmul", ps[0:96, 5, qt * 128:(qt + 1) * 128], lhsT=nMp[:, qt, :], rhs=ident[:], start=True, stop=True), reads=["nMp", "ident"], writes=["b5"])
                                    S.op("act", C("activation", out=qsb[64:96, :], in_=ps[64:96, 5, :], func=AF.Copy), reads=["b5"], writes=[qn])
                                if kind == "C":
                                    S.op("pool", C("memset", spa[0][:], 0.0), writes=["spacc0"])
                                order = list(range(nkt)) if kind != "C" else list(range(nkt - 1, -1, -1))
                                n = len(order)

                                def st0(i):
                                    kt = order[i]
                                    sbk = (i % 2) if kind == "C" else (i % 4); sbn = "b%d" % sbk
                                    diag = kt >= d0
                                    u = kt - d0
                                    if kind == "B2":
                                        nb_ = nst[i % 4]; nbn = "nst%d" % (i % 4)
                                        S.op("sp", C("dma_start", out=nb_[:], in_=nsT[j, kt]), reads=["nsT"], writes=[nbn], dsem=nbn)
                                    S.op("pe", C("matmul", ps[:, sbk, :], lhsT=kTh[:, kt * 128:(kt + 1) * 128], rhs=qs, start=True, stop=True), reads=[kn, qn], writes=[sbn])
                                    if kind in ("A", "B2"):
                                        pt = Pt[i % 4]; ptn = "P%d" % (i % 4)
                                        S.op("act", C("activation", out=pt[:], in_=ps[:, sbk, :], func=AF.Exp, scale=0.125), reads=[sbn], writes=[ptn])
                                        if diag and kind == "A":
                                            S.op("pool", C("tensor_tensor", out=pt[:], in0=pt[:], in1=mle[:, u, :], op=ALU.mult), reads=[ptn, "mle"], writes=[ptn])
                                        if kind == "B2":
                                            S.op("dve", C("tensor_tensor", out=pt[:], in0=pt[:], in1=nb_[:], op=ALU.mult), reads=[ptn, nbn], writes=[ptn])
                                    else:
                                        eb = e32[i % 3]; ebn = "e32_%d" % (i % 3)
                                        spb = sp_[i % 3]; spn = "sp%d" % (i % 3)
                                        S.op("act", C("activation", out=eb[:], in_=ps[:, sbk, :], func=AF.Exp, scale=0.125), reads=[sbn], writes=[ebn])
                                        if diag:
                                            S.op("pool", C("tensor_tensor", out=eb[:], in0=eb[:], in1=mlt[:, u, :], op=ALU.mult), reads=[ebn, "mlt"], writes=[ebn])
                                        S.op("act", C("activation", out=spb[:], in_=eb[:], func=AF.Ln, bias=1.0), reads=[ebn], writes=[spn])
                                        S.op("pool", C("tensor_tensor", out=spa[(i + 1) % 3][:], in0=spa[i % 3][:], in1=spb[:], op=ALU.add),
                                             reads=["spacc%d" % (i % 3), spn], writes=["spacc%d" % ((i + 1) % 3)])

                                def st1(i):
                                    kt = order[i]
                                    diag = kt >= d0
                                    u = kt - d0
                                    wbk = 2 + (i % 2); wbn = "b%d" % wbk
                                    xbk = 4 + (i % 2); xbn = "b%d" % xbk
                                    spb = sp_[i % 3]; spn = "sp%d" % (i % 3)
                                    eb = e32[i % 3]; ebn = "e32_%d" % (i % 3)
                                    pt = Pt[i % 3]; ptn = "P%d" % (i % 3)
                                    S.op("pe", C("matmul", ps[:, wbk, :], lhsT=uinc[:], rhs=spb[:], start=True, stop=(i == 0)), reads=["uinc", spn], writes=[wbn])
                                    if i > 0:
                                        S.op("pe", C("matmul", ps[:, wbk, :], lhsT=ones[:], rhs=spa[i % 3][:], start=False, stop=True), reads=["ones", "spacc%d" % (i % 3)], writes=[wbn])
                                    S.op("act", C("activation", out=ps[:, xbk, :], in_=ps[:, wbk, :], func=AF.Exp, scale=-1.0), reads=[wbn], writes=[xbn])
                                    S.op("dve", C("tensor_tensor", out=pt[:], in0=ps[:, xbk, :], in1=eb[:], op=ALU.mult), reads=[xbn, ebn], writes=[ptn])

                                def st2(i):
                                    kt = order[i]
                                    pt = Pt[i % NPB_]; ptn = "P%d" % (i % NPB_)
                                    S.op("pe", C("matmul", ps[0:65, ob, :], lhsT=VG[:, kt, hh, :], rhs=pt[:], start=(i == 0), stop=(i == n - 1)), reads=["VG", ptn], writes=[obn])

                                if kind == "C":
                                    for t in range(n + 2):
                                        if t < n:
                                            st0(t)
                                        if 1 <= t <= n:
                                            st1(t - 1)
                                        if t >= 2:
                                            st2(t - 2)
                                else:
                                    for t in range(n + 2):
                                        if t < n:
                                            st0(t)
                                        if t >= 2:
                                            st2(t - 2)
                                S.op("dve", C("tensor_copy", out=osbt[:], in_=ps[0:65, ob, :]), reads=[obn], writes=[osn])
                                if kind == "C":
                                    S.op("dve", C("memset", osbt[64:65, :], 1.0), reads=[osn], writes=[osn])
                                S.op("sp", C("dma_start", out=OT[hbase + h, :, j * 512:(j + 1) * 512], in_=osbt[:]), reads=[osn], writes=["OT"], dsem=osn)
                        _barrier(S)

                for kind in ("A", "C", "B2"):
                    sweep_phase(kind)

        cur = x_in
        for l in range(DEPTH):
            if l > 0:
                lt_mix(l - 1, cur, xA)
                cur = xA
            lt_pre(l, cur, xB)
            cur = xB
            lb()
        lt_mix(DEPTH - 1, cur, y_out)
        S.finish_wait("sp", ["xdst"])
        _barrier(S)
        S.emit()
    return nc

BF = ml_dtypes.bfloat16
NIT = 16
BIGA = 240000.0


def own_idx(r, NS):
    return np.concatenate([(2 * j + r) * 512 + np.arange(512) for j in range(NS)])


def lb_consts(r, NS):
    p = np.arange(128)
    d = {}
    u = np.arange(8); f = np.arange(512)
    kk = u[None, :, None] * 128 + p[:, None, None]
    qq = r * 512 + f[None, None, :]
    d["mLE"] = (kk <= qq).astype(BF)
    d["mLT"] = (kk < qq).astype(BF)
    col = np.arange(1024)
    qpos = r * 512 + np.arange(4)[None, :, None] * 128 + p[:, None, None]
    d["cb"] = np.where(col[None, None, :] <= qpos, 0.0, -1e30).astype(np.float32)
    j = np.arange(NS)[None, :, None, None]; qt = np.arange(4)[None, None, :, None]; blk = np.arange(32)[None, None, None, :]
    cur = 4 * j + 2 * r + qt // 2 + 0 * p[:, None, None, None]
    d["pastbias"] = np.where(blk < cur, 0.0, -1e30).astype(np.float32)
    d["pastvalid"] = (blk < cur).astype(np.float32)
    d["future"] = (blk > cur).astype(np.float32)
    d["pow2"] = np.tile((0.5 ** (np.arange(NIT) + 1)).astype(np.float32)[None, :], (128, 1))
    d["ident"] = np.eye(128, dtype=np.float32)
    d["uinc"] = (p[:, None] >= p[None, :]).astype(np.float32)
    return d


def lb_kside(P, NS):
    SK = NS * 1024
    NB = SK // 256
    d = {}
    ka = np.zeros((4, 96, SK), dtype=BF)
    ka[:, 0:64, :] = P[:, 256:512].reshape(SK, 4, 64).transpose(1, 2, 0)
    for b in range(NB):
        ka[:, 64 + b, b * 256:(b + 1) * 256] = BF(-BIGA)
    d["kaT"] = ka
    d["kbT"] = np.ascontiguousarray(P[:, 768:1024].reshape(SK, 4, 64).transpose(1, 2, 0))
    kiT = P[:, 1536:1600].T
    d["ki2"] = np.ascontiguousarray(np.concatenate([kiT[:, :SK // 2], kiT[:, SK // 2:]], 0))
    d["kcT"] = np.ascontiguousarray(P[:, 2112:2624].reshape(SK, 8, 64).transpose(1, 2, 0))
    one = np.ones((SK, 4, 1), dtype=BF)
    def vl(v):
        nh = v.shape[1]
        return np.ascontiguousarray(v.reshape(SK // 128, 128, nh, 65).transpose(2, 1, 0, 3))
    d["va"] = vl(np.concatenate([P[:, 2624:2880].reshape(SK, 4, 64), one], 2))
    d["vb"] = vl(np.concatenate([P[:, 2880:3136].reshape(SK, 4, 64), one], 2))
    d["vc"] = vl(np.concatenate([P[:, 3136:3648].reshape(SK, 8, 64), np.ones((SK, 8, 1), dtype=BF)], 2))
    return d


def lb_qside(Pown, wiown):
    NT = Pown.shape[0]
    d = {}
    d["qaT"] = np.ascontiguousarray(Pown[:, 0:256].reshape(NT, 4, 64).transpose(1, 2, 0))
    d["qbT"] = np.ascontiguousarray(Pown[:, 512:768].reshape(NT, 4, 64).transpose(1, 2, 0))
    qiT = Pown[:, 1024:1536].reshape(NT, 8, 64).transpose(2, 1, 0)
    d["qi2"] = np.ascontiguousarray(np.concatenate([qiT, qiT], 0))
    d["qcT"] = np.ascontiguousarray(Pown[:, 1600:2112].reshape(NT, 8, 64).transpose(1, 2, 0))
    d["wi"] = np.ascontiguousarray(wiown)
    return d


def ot_to_oa(OT):
    Oa = np.ascontiguousarray(OT.transpose(2, 0, 1)).copy()
    Oa[:, 8:, 64] = 1.0
    return Oa.reshape(Oa.shape[0], 16 * 65)


def fused_consts(NCH):
    p = np.arange(128)
    d = {}
    u = np.arange(4); f = np.arange(512)
    kk = u[None, :, None] * 128 + p[:, None, None]
    qq = f[None, None, :]
    d["mLE"] = (kk <= qq).astype(BF)
    d["mLT"] = (kk < qq).astype(BF)
    col = np.arange(512)
    qpos = np.arange(4)[None, :, None] * 128 + p[:, None, None]
    d["cb"] = np.where(col[None, None, :] <= qpos, 0.0, -1e30).astype(np.float32)
    j = np.arange(NCH)[None, :, None, None]; qt = np.arange(4)[None, None, :, None]; blk = np.arange(32)[None, None, None, :]
    cur = 2 * j + qt // 2 + 0 * p[:, None, None, None]
    d["pastbias"] = np.where(blk < cur, 0.0, -1e30).astype(np.float32)
    d["pastvalid"] = (blk < cur).astype(np.float32)
    d["future"] = (blk > cur).astype(np.float32)
    d["pow2"] = np.tile((0.5 ** (np.arange(NIT + 1) + 1)).astype(np.float32)[None, :], (128, 1))
    d["ident"] = np.eye(128, dtype=np.float32)
    d["uinc"] = (p[:, None] >= p[None, :]).astype(np.float32)
    SK = NCH * 512
    oh = np.zeros((32, SK), dtype=BF)
    for b in range(SK // 256):
        oh[b, b * 256:(b + 1) * 256] = BF(-BIGA)
    d["oh"] = oh
    return d

PERM = np.concatenate([np.arange(0, 256), np.arange(256, 512), np.arange(768, 1024), np.arange(1024, 1280),
                       np.arange(1536, 2048), np.arange(2048, 2112), np.arange(2120, 2632), np.arange(2632, 3144),
                       np.arange(512, 768), np.arange(1280, 1536), np.arange(3144, 3656), np.arange(2112, 2120)])
ROPE_THETA = 500000.0
_CACHE = {}


def kernel(**inputs):
    inp = {k: np.asarray(v) for k, v in inputs.items()}
    x = inp["x"]
    B, S_, Dm = x.shape
    NCH = S_ // 512
    depth = inp["w_ada"].shape[0]
    ncores = 2 * B
    key = (NCH, depth)
    if key not in _CACHE:
        _CACHE[key] = build_fused(NCH, depth)
    nc = _CACHE[key]
    shared = fused_consts(NCH)
    inv = (ROPE_THETA ** (-np.arange(0, 16, 2, dtype=np.float32) / np.float32(16))).astype(np.float32)
    ang = np.arange(S_, dtype=np.float32)[:, None] * inv[None, :]
    shared["cs"] = np.ascontiguousarray(np.concatenate([np.cos(ang), np.sin(ang)], 1).astype(np.float32))
    for l in range(depth):
        shared["wada%d" % l] = np.ascontiguousarray(inp["w_ada"][l])
        shared["badar%d" % l] = np.ascontiguousarray(inp["b_ada"][l][None, :])
        shared["badac%d" % l] = np.ascontiguousarray(inp["b_ada"][l].reshape(72, 128).T)
        shared["ng%d" % l] = np.ascontiguousarray(inp["norm_g"][l].reshape(24, 128).T)
        shared["w_in%d" % l] = np.ascontiguousarray(inp["w_in"][l][:, PERM])
        shared["qkg%d" % l] = np.ascontiguousarray(np.repeat(inp["qk_g"][l], 4, axis=0).reshape(1, 1024))
        shared["w_out%d" % l] = np.ascontiguousarray(inp["w_out"][l])
        shared["outg%d" % l] = np.ascontiguousarray(inp["out_g"][l][None, :])
        for i in range(2):
            shared["w1_%d_%d" % (l, i)] = np.ascontiguousarray(inp["ffn_w1"][l, i])
            shared["w3_%d_%d" % (l, i)] = np.ascontiguousarray(inp["ffn_w3"][l, i])
            shared["w2_%d_%d" % (l, i)] = np.ascontiguousarray(inp["ffn_w2"][l, i])
    maps = []
    for ci in range(ncores):
        b = ci // 2
        d = dict(shared)
        d["x"] = np.ascontiguousarray(x[b])
        d["cT"] = np.ascontiguousarray(inp["c"][b].reshape(8, 128).T)
        maps.append(d)
    res = run_bass_kernel_spmd(nc, maps, core_ids=list(range(ncores)))
    out = np.empty((B, S_, Dm), dtype=np.float32)
    for b in range(B):
        out[b] = np.asarray(res.results[2 * b]["y"])
    return out
```
